# Optimizing a Trainium2 kernel written in Bass

```python
import numpy as np
import jax
import jax.numpy as jnp
from jax import lax

D_MODEL = 1024
BATCH = 16
SEQ = 2048
DEPTH = 2

GRID_W = 64
CTX_LEN = 256

NA_HEADS = 8
NA_HEAD_DIM = 64
NA_WIDTH = NA_HEADS * NA_HEAD_DIM
NA_KH = 8
NA_KW = 16
NA_QB = 16
NA_BAND = 32
NA_NCB = GRID_W // NA_QB
LRU_WIDTH = 512
LRU_BLOCKS = 8
LRU_BLOCK = LRU_WIDTH // LRU_BLOCKS
LRU_CONV = 4
LRU_CONV_LEFT = 2
LRU_C = 8.0
SC_WIDTH = D_MODEL
SC_K = 3
D_FF = 2816
FFN_RES = 0.5
N_SUB = 3
N_MOD = 3 * N_SUB
ALPHA = (2.0 * DEPTH) ** 0.25
BETA = (8.0 * DEPTH) ** -0.25
LN_EPS = 1e-5
NEG_INF = -1e30

N_EVEN = (DEPTH + 1) // 2
N_ODD = DEPTH // 2
MIX0_IN = 3 * NA_WIDTH + 2 * LRU_WIDTH
MIX0_OUT = NA_WIDTH + LRU_WIDTH

kernel_name = 'hybrid_na_rglru_shortconv_dit'


def layer_norm(h, g, b):
    hf = h.astype(jnp.float32)
    mu = jnp.mean(hf, axis=-1, keepdims=True)
    var = jnp.mean(jnp.square(hf - mu), axis=-1, keepdims=True)
    return ((hf - mu) * lax.rsqrt(var + LN_EPS)).astype(h.dtype) * g + b


def modulate(h, m, k):
    return h * (1 + m[..., 3 * k + 1, :]) + m[..., 3 * k, :]


def post_norm_residual(h, y, m, k, g, b, res_w):
    return layer_norm(ALPHA * h + res_w * m[..., 3 * k + 2, :] * y, g, b)


def swiglu(h, w1, w3, w2):
    return (jax.nn.silu(h @ w1) * (h @ w3)) @ w2


def depthwise_conv(h, w, left):
    k, ch = w.shape
    return lax.conv_general_dilated(
        h, w[:, None, :].astype(h.dtype), (1,), [(left, k - 1 - left)],
        dimension_numbers=('NWC', 'WIO', 'NWC'), feature_group_count=ch)


def _na_column_tables():
    j = np.arange(NA_NCB)[:, None, None]
    u = np.arange(NA_QB)[None, :, None]
    v = np.arange(NA_BAND)[None, None, :]
    band = np.clip(NA_QB * np.arange(NA_NCB) - NA_KW // 2, 0, GRID_W - NA_BAND)
    q_col = NA_QB * j + u
    k_col = band[:, None, None] + v
    start = np.clip(q_col - NA_KW // 2, 0, GRID_W - NA_KW)
    mask = (k_col >= start) & (k_col < start + NA_KW)
    col_idx = np.clip(k_col - q_col + NA_KW - 1, 0, 2 * NA_KW - 2)
    return [int(s) for s in band], mask, col_idx


def neighbourhood_attention(q, k, v, k_ctx, v_ctx, rpb):
    bsz, n, nh, hd = q.shape
    rows = n // GRID_W
    kh = min(NA_KH, rows)
    nk = kh * NA_BAND
    bands, col_mask, col_idx = _na_column_tables()
    mask = np.broadcast_to(col_mask[:, :, None, :], (NA_NCB, NA_QB, kh, NA_BAND)).reshape(NA_NCB, NA_QB, nk)
    scale = hd ** -0.5
    qg = q.reshape(bsz, rows, NA_NCB, NA_QB, nh, hd)
    kg = k.reshape(bsz, rows, GRID_W, nh, hd)
    vg = v.reshape(bsz, rows, GRID_W, nh, hd)
    rpb_f = rpb.astype(jnp.float32)

    def gather_bands(t_rows):
        t = jnp.stack([t_rows[:, :, s:s + NA_BAND] for s in bands], axis=1)
        return t.reshape(bsz, NA_NCB, nk, nh, hd)

    def row_block(r):
        r0 = jnp.clip(r - kh // 2, 0, rows - kh)
        kb = gather_bands(lax.dynamic_slice_in_dim(kg, r0, kh, axis=1))
        vb = gather_bands(lax.dynamic_slice_in_dim(vg, r0, kh, axis=1))
        qr = lax.dynamic_index_in_dim(qg, r, axis=1, keepdims=False)
        s_loc = jnp.einsum('bjqhd,bjkhd->bhjqk', qr, kb, preferred_element_type=jnp.float32) * scale
        s_ctx = jnp.einsum('bjqhd,bchd->bhjqc', qr, k_ctx, preferred_element_type=jnp.float32) * scale
        row_off = r0 + jnp.arange(kh) - r + NA_KH - 1
        bias = rpb_f[:, row_off][:, :, col_idx]
        bias = bias.transpose(0, 2, 3, 1, 4).reshape(nh, NA_NCB, NA_QB, nk)
        s_loc = jnp.where(mask, s_loc + bias, NEG_INF)
        p = jax.nn.softmax(jnp.concatenate([s_loc, s_ctx], axis=-1), axis=-1).astype(v.dtype)
        return (jnp.einsum('bhjqk,bjkhd->bjqhd', p[..., :nk], vb)
                + jnp.einsum('bhjqc,bchd->bjqhd', p[..., nk:], v_ctx))

    out = lax.map(row_block, jnp.arange(rows))
    return jnp.moveaxis(out, 0, 1).reshape(bsz, n, nh * hd)


def context_attention(q, k, v):
    bsz, n, nh, hd = q.shape
    s = jnp.einsum('bqhd,bkhd->bhqk', q, k, preferred_element_type=jnp.float32) * hd ** -0.5
    p = jax.nn.softmax(s, axis=-1).astype(v.dtype)
    return jnp.einsum('bhqk,bkhd->bqhd', p, v).reshape(bsz, n, nh * hd)


def rglru_coeffs(xc, w_a, b_a, w_x, b_x, lam):
    bsz, n, rw = xc.shape
    xb = xc.reshape(bsz, n, LRU_BLOCKS, LRU_BLOCK)
    gate_r = jax.nn.sigmoid((jnp.einsum('bsnk,nkj->bsnj', xb, w_a).reshape(bsz, n, rw) + b_a).astype(jnp.float32))
    gate_i = jax.nn.sigmoid((jnp.einsum('bsnk,nkj->bsnj', xb, w_x).reshape(bsz, n, rw) + b_x).astype(jnp.float32))
    log_a = LRU_C * gate_r * jax.nn.log_sigmoid(lam.astype(jnp.float32))
    a = jnp.exp(log_a)
    b = jnp.sqrt(-jnp.expm1(2.0 * log_a)) * (gate_i * xc.astype(jnp.float32))
    return a, b


def linear_scan(a, b, h0):
    def combine(left, right):
        return left[0] * right[0], right[0] * left[1] + right[1]
    a_cum, h = lax.associative_scan(combine, (a, b), axis=1)
    return h + a_cum * h0[:, None, :]


def rglru_bidirectional(xc, w_a, b_a, w_x, b_x, lam, h0_fwd, h0_bwd):
    a_f, b_f = rglru_coeffs(xc, w_a[0], b_a[0], w_x[0], b_x[0], lam[0])
    a_b, b_b = rglru_coeffs(xc, w_a[1], b_a[1], w_x[1], b_x[1], lam[1])
    h_f = linear_scan(a_f, b_f, h0_fwd)
    h_b = linear_scan(a_b[:, ::-1], b_b[:, ::-1], h0_bwd)[:, ::-1]
    return (h_f + h_b).astype(xc.dtype), h_f[:, -1], h_b[:, 0]


def even_mixer(z, z_ctx, w_in, rpb, conv_w, conv_b, w_a, b_a, w_x, b_x, lam, w_out, with_ctx_out):
    cuts = [NA_WIDTH, 2 * NA_WIDTH, 3 * NA_WIDTH, 3 * NA_WIDTH + LRU_WIDTH]

    def project(t):
        bsz, n, _ = t.shape
        q, k, v, xr, gr = jnp.split(t @ w_in, cuts, axis=-1)
        hs = (bsz, n, NA_HEADS, NA_HEAD_DIM)
        xr = depthwise_conv(xr, conv_w, LRU_CONV_LEFT) + conv_b
        return q.reshape(hs), k.reshape(hs), v.reshape(hs), xr, gr

    qc, kc, vc, xc, gc = project(z_ctx)
    q, k, v, xl, gl = project(z)
    zeros = jnp.zeros((z.shape[0], LRU_WIDTH), jnp.float32)
    y_c, hf_ctx, hb_ctx = rglru_bidirectional(xc, w_a, b_a, w_x, b_x, lam, zeros, zeros)
    y_l, _, _ = rglru_bidirectional(xl, w_a, b_a, w_x, b_x, lam, hf_ctx, hb_ctx)
    na = neighbourhood_attention(q, k, v, kc, vc, rpb)
    out = jnp.concatenate([na, y_l * jax.nn.gelu(gl)], axis=-1) @ w_out
    out_ctx = None
    if with_ctx_out:
        ca = context_attention(qc, kc, vc)
        out_ctx = jnp.concatenate([ca, y_c * jax.nn.gelu(gc)], axis=-1) @ w_out
    return out, out_ctx


def short_conv_mixer(z, w_in, conv_w, w_out):
    gate_b, gate_c, xv = jnp.split(z @ w_in, 3, axis=-1)
    return (gate_b * depthwise_conv(gate_c * xv, conv_w, 1)) @ w_out


def setup_inputs(seed: int = 0) -> dict:
    key = jax.random.key(seed)
    ks = jax.random.split(key, 24)
    f32 = jnp.float32
    D = D_MODEL

    def nrm(k, shape, s):
        return jax.random.normal(k, shape, f32) * s

    u = jax.random.uniform(ks[19], (N_EVEN, 2, LRU_WIDTH), f32, 0.9, 0.999)
    s = u ** (1.0 / LRU_C)
    return {
        'x': nrm(ks[0], (BATCH, SEQ, D), 1.0),
        'c': nrm(ks[1], (BATCH, D), 1.0),
        'ctx': nrm(ks[2], (BATCH, CTX_LEN, D), 1.0),
        'c_ctx': nrm(ks[3], (D,), 1.0),
        'mod_w': nrm(ks[4], (DEPTH, D, N_MOD * D), 0.5 * D ** -0.5),
        'mod_b': nrm(ks[5], (DEPTH, N_MOD * D), 0.02),
        'ln_g': 1.0 + nrm(ks[6], (DEPTH, N_SUB, D), 0.02),
        'ln_b': nrm(ks[7], (DEPTH, N_SUB, D), 0.02),
        'ffn_w1': nrm(ks[8], (DEPTH, 2, D, D_FF), D ** -0.5),
        'ffn_w3': nrm(ks[9], (DEPTH, 2, D, D_FF), D ** -0.5),
        'ffn_w2': nrm(ks[10], (DEPTH, 2, D_FF, D), BETA * D_FF ** -0.5),
        'mix0_w_in': nrm(ks[11], (N_EVEN, D, MIX0_IN), D ** -0.5),
        'na_rpb': nrm(ks[12], (N_EVEN, NA_HEADS, 2 * NA_KH - 1, 2 * NA_KW - 1), 0.1),
        'lru_conv_w': nrm(ks[13], (N_EVEN, LRU_CONV, LRU_WIDTH), LRU_CONV ** -0.5),
        'lru_conv_b': nrm(ks[14], (N_EVEN, LRU_WIDTH), 0.02),
        'lru_w_a': nrm(ks[15], (N_EVEN, 2, LRU_BLOCKS, LRU_BLOCK, LRU_BLOCK), LRU_BLOCK ** -0.5),
        'lru_b_a': nrm(ks[16], (N_EVEN, 2, LRU_WIDTH), 0.02),
        'lru_w_x': nrm(ks[17], (N_EVEN, 2, LRU_BLOCKS, LRU_BLOCK, LRU_BLOCK), LRU_BLOCK ** -0.5),
        'lru_b_x': nrm(ks[18], (N_EVEN, 2, LRU_WIDTH), 0.02),
        'lru_lambda': jnp.log(s) - jnp.log1p(-s),
        'mix0_w_out': nrm(ks[20], (N_EVEN, MIX0_OUT, D), BETA * MIX0_OUT ** -0.5),
        'mix1_w_in': nrm(ks[21], (N_ODD, D, 3 * SC_WIDTH), D ** -0.5),
        'sconv_w': nrm(ks[22], (N_ODD, SC_K, SC_WIDTH), SC_K ** -0.5),
        'mix1_w_out': nrm(ks[23], (N_ODD, SC_WIDTH, D), BETA * SC_WIDTH ** -0.5),
    }


def reference(x, c, ctx, c_ctx, mod_w, mod_b, ln_g, ln_b, ffn_w1, ffn_w3, ffn_w2,
              mix0_w_in, na_rpb, lru_conv_w, lru_conv_b, lru_w_a, lru_b_a, lru_w_x, lru_b_x,
              lru_lambda, mix0_w_out, mix1_w_in, sconv_w, mix1_w_out):
    h = x
    hc = ctx
    cond = jax.nn.silu(c)
    cond_ctx = jax.nn.silu(c_ctx)
    for layer in range(DEPTH):
        even = layer % 2 == 0
        ctx_next = layer < DEPTH - 1
        ctx_here = ctx_next or even
        m = (cond @ mod_w[layer] + mod_b[layer]).reshape(-1, 1, N_MOD, D_MODEL)
        mc = (cond_ctx @ mod_w[layer] + mod_b[layer]).reshape(N_MOD, D_MODEL)

        w1, w3, w2 = ffn_w1[layer, 0], ffn_w3[layer, 0], ffn_w2[layer, 0]
        g, b = ln_g[layer, 0], ln_b[layer, 0]
        h = post_norm_residual(h, swiglu(modulate(h, m, 0), w1, w3, w2), m, 0, g, b, FFN_RES)
        if ctx_here:
            hc = post_norm_residual(hc, swiglu(modulate(hc, mc, 0), w1, w3, w2), mc, 0, g, b, FFN_RES)

        if even:
            e = layer // 2
            y, yc = even_mixer(modulate(h, m, 1), modulate(hc, mc, 1), mix0_w_in[e], na_rpb[e],
                               lru_conv_w[e], lru_conv_b[e], lru_w_a[e], lru_b_a[e], lru_w_x[e],
                               lru_b_x[e], lru_lambda[e], mix0_w_out[e], ctx_next)
        else:
            o = layer // 2
            y = short_conv_mixer(modulate(h, m, 1), mix1_w_in[o], sconv_w[o], mix1_w_out[o])
            yc = short_conv_mixer(modulate(hc, mc, 1), mix1_w_in[o], sconv_w[o], mix1_w_out[o]) if ctx_next else None
        g, b = ln_g[layer, 1], ln_b[layer, 1]
        h = post_norm_residual(h, y, m, 1, g, b, 1.0)

        w1, w3, w2 = ffn_w1[layer, 1], ffn_w3[layer, 1], ffn_w2[layer, 1]
        g2, b2 = ln_g[layer, 2], ln_b[layer, 2]
        if ctx_next:
            hc = post_norm_residual(hc, yc, mc, 1, g, b, 1.0)
            hc = post_norm_residual(hc, swiglu(modulate(hc, mc, 2), w1, w3, w2), mc, 2, g2, b2, FFN_RES)
        h = post_norm_residual(h, swiglu(modulate(h, m, 2), w1, w3, w2), m, 2, g2, b2, FFN_RES)
    return h
```

```python
import numpy as np
from contextlib import ExitStack
import concourse.bass as bass
import concourse.mybir as mybir
from concourse.bass_utils import run_bass_kernel_spmd

F32 = mybir.dt.float32
BF16 = mybir.dt.bfloat16
AF = mybir.ActivationFunctionType
ALU = mybir.AluOpType

D = 1024
SEQ = 2048
CTX = 256
DFF = 2816
NFC = 22
GRID_W = 64
ALPHA = 4.0 ** 0.25
LN_EPS = 1e-5 / (ALPHA * ALPHA)
NCORES = 8
ENGS = ("pe", "act", "dve", "pool", "sp")


class Stream:
    def __init__(self, sem):
        self.sem = sem
        self.count = 0


class Prog:
    def __init__(self, nc, stack):
        self.nc = nc
        self.stack = stack
        self.q = {e: [] for e in ENGS}
        self.cnt = {e: 0 for e in ENGS}
        self.esem = {e: stack.enter_context(nc.semaphore("es_" + e)) for e in ENGS if e != "sp"}
        self.waited = {}
        self.lastw = {}
        self.readers = {}
        self.streams = []
        self.scount = {}
        self.n_ops = 0

    def stream(self, name=None):
        s = Stream(self.stack.enter_context(self.nc.semaphore(name or ("ds%d" % len(self.streams)))))
        self.streams.append(s)
        self.scount["s_%d" % id(s)] = (lambda s=s: s.count)
        return s

    def _need(self, eng, tok, waits):
        if tok is None:
            return
        sid, sem, val, teng = tok
        if teng == eng and eng == "pe":
            return
        if teng is None:
            val = max(val, self.scount[sid]())
        k = (eng, sid)
        if self.waited.get(k, 0) >= val:
            return
        self.waited[k] = val
        waits.append((sem, val))

    def _deps(self, eng, reads, writes):
        waits = []
        for k in reads:
            self._need(eng, self.lastw.get(k), waits)
        for k in writes:
            self._need(eng, self.lastw.get(k), waits)
            for t in self.readers.get(k, ()):
                self._need(eng, t, waits)
        best = {}
        for sem, val in waits:
            if id(sem) not in best or best[id(sem)][1] < val:
                best[id(sem)] = (sem, val)
        return list(best.values())

    def _commit(self, tok, reads, writes):
        for k in reads:
            self.readers.setdefault(k, []).append(tok)
        for k in writes:
            self.lastw[k] = tok
            self.readers[k] = []

    def op(self, eng, fn, reads=(), writes=(), strict=False):
        self.group(eng, [fn], reads, writes, strict)

    def group(self, eng, fns, reads=(), writes=(), strict=False):
        waits = self._deps(eng + "_strict" if strict else eng, reads, writes)
        self.cnt[eng] += 1
        tok = ("e_" + eng, self.esem[eng], self.cnt[eng], eng)
        n = len(fns)
        for i, fn in enumerate(fns):
            self.q[eng].append((waits if i == 0 else [], fn, (self.esem[eng], 1) if i == n - 1 else None))
        self._commit(tok, reads, writes)
        self.n_ops += n

    def dma(self, eng, stream, fn, reads=(), writes=()):
        waits = self._deps(eng + "_q", reads, writes)
        stream.count += 16
        tok = ("s_%d" % id(stream), stream.sem, stream.count, None)
        self.q[eng].append((waits, fn, (stream.sem, 16)))
        self._commit(tok, reads, writes)
        self.n_ops += 1

    def barrier(self):
        toks = [("e_" + e, self.esem[e], self.cnt[e], e) for e in self.esem if self.cnt[e] > 0]
        toks += [("s_%d" % id(s), s.sem, s.count, None) for s in self.streams if s.count > 0]
        for e, ident in (("pe", "pe"), ("act", "act"), ("dve", "dve"), ("pool", "pool"), ("pool", "pool_q"),
                         ("sp", "sp_q"), ("act", "act_q")):
            waits = []
            for t in toks:
                self._need(ident, t, waits)
            if waits:
                self.q[e].append((waits, None, None))
        self.lastw = {}
        self.readers = {}

    def emit(self):
        nc = self.nc
        with nc.Block() as block:
            def run(engname):
                def body(e):
                    for waits, fn, inc in self.q[engname]:
                        for sem, val in waits:
                            e.wait_ge(sem, val)
                        if fn is not None:
                            ins = fn(e)
                            if inc is not None:
                                ins.then_inc(inc[0], inc[1])
                return body
            block.sync(run("sp"))
            block.tensor(run("pe"))
            block.scalar(run("act"))
            block.vector(run("dve"))
            block.gpsimd(run("pool"))


def to_pm(a):
    T, Dd = a.shape
    nch = Dd // 128
    return np.ascontiguousarray(a.T.reshape(nch, 128, T).transpose(1, 0, 2).reshape(128, nch * T))


def from_pm(o, T):
    nch = o.shape[1] // T
    return np.ascontiguousarray(o.reshape(128, nch, T).transpose(2, 1, 0).reshape(T, nch * 128))


def lhsT_layout(W):
    K, N = W.shape
    kc, ncc = K // 128, N // 128
    return np.ascontiguousarray(W.reshape(kc, 128, ncc, 128).transpose(2, 1, 0, 3).reshape(ncc * 128, K))


def vec_pm(v):
    return np.ascontiguousarray(v.reshape(-1, 128).T)


ALL_PHASES = ("L0F0", "L0MIX", "L0F1", "L1F0", "L1MIX", "L1F1")


class Builder:
    def __init__(self, NS=2, phases=ALL_PHASES, dump_ctx=False):
        self.NS = NS
        self.phases = phases
        self.dump_ctx = dump_ctx
        self.nc = bass.Bass("TRN2", target_bir_lowering=False)
        self.conv_done = set()
        self.scr_keys = {}

    def dram(self, name, shape, dt, kind):
        return self.nc.dram_tensor(name, shape, dt, kind=kind).ap()

    def sb(self, st, name, shape, dt):
        self.ntile = getattr(self, "ntile", 0) + 1
        return st.enter_context(self.nc.sbuf_tensor("%s_%d" % (name, self.ntile), shape, dt))

    def declare(self):
        NS = self.NS
        di = lambda n, s: self.dram(n, s, F32, "ExternalInput")
        self.xT = di("xT", [NS * 128, 8 * SEQ])
        self.cxT = di("cxT", [NS * 128, 8 * CTX])
        self.cond = di("cond", [128, 24])
        self.modw = di("modw", [2 * 72 * 128, 1024])
        self.modb3 = di("modb3", [128, 2 * 72 * 3])
        self.lng = di("lng", [128, 48])
        self.lnb = di("lnb", [128, 48])
        self.w1 = di("w1", [4 * NFC * 128, 1024])
        self.w3 = di("w3", [4 * NFC * 128, 1024])
        self.w2 = di("w2", [4 * 8 * 128, DFF])
        self.win0 = di("win0", [20 * 128, 1024])
        self.wout0 = di("wout0", [8 * 128, 1024])
        self.win1 = di("win1", [24 * 128, 1024])
        self.wout1 = di("wout1", [8 * 128, 1024])
        self.scw = di("scw", [128, 24])
        self.rpbg = di("rpbg", [128, 4 * 15 * 64])
        self.nmask = di("nmask", [128, 3840])
        self.id2 = di("id2", [128, 64])
        self.lcw = di("lcw", [128, 16])
        self.lcb = di("lcb", [128, 4])
        self.lwa = di("lwa", [2 * 4 * 128, 128])
        self.lwx = di("lwx", [2 * 4 * 128, 128])
        self.lba = di("lba", [128, 8])
        self.lbx = di("lbx", [128, 8])
        self.llam = di("llam", [128, 8])
        self.out = self.dram("out", [NS * 128, 8 * SEQ], F32, "ExternalOutput")
        if self.dump_ctx:
            self.outc = self.dram("outc", [NS * 128, 8 * CTX], F32, "ExternalOutput")
            self.dbg = self.dram("dbg", [256, 4 * SEQ], BF16, "ExternalOutput")
            self.dbg2 = self.dram("dbg2", [128, 5 * (CTX + SEQ)], F32, "ExternalOutput")
            self.dbg3 = self.dram("dbg3", [128, 8 * SEQ], F32, "ExternalOutput")
            self.dbg4 = self.dram("dbg4", [128, 8 * CTX], F32, "ExternalOutput")
        ds = lambda n, s: self.dram(n, s, BF16, "Internal")
        self.w1s = ds("w1s", [4 * NFC * 128, 1024])
        self.w3s = ds("w3s", [4 * NFC * 128, 1024])
        self.w2s = ds("w2s", [4 * 8 * 128, DFF])
        self.win0s = ds("win0s", [20 * 128, 1024])
        self.wout0s = ds("wout0s", [8 * 128, 1024])
        self.win1s = ds("win1s", [24 * 128, 1024])
        self.wout1s = ds("wout1s", [8 * 128, 1024])

    def convert(self, name, src, dst, r0, r1, step):
        P = self.P
        for a in range(r0, r1, step):
            b = min(a + step, r1)
            key = ("cv", name, a)
            if key in self.conv_done:
                continue
            self.conv_done.add(key)
            self.scr_keys.setdefault(name, []).append(("scr", name, a))
            P.dma("pool", self.s_conv, lambda e, a=a, b=b: e.dma_start(out=dst[a:b, :], in_=src[a:b, :]),
                  writes=[("scr", name, a)])

    def convert_for(self, phase):
        if phase in ("L0F0", "L0F1", "L1F0", "L1F1"):
            q = {"L0F0": 0, "L0F1": 1, "L1F0": 2, "L1F1": 3}[phase]
            self.convert("w1_%d" % q, self.w1, self.w1s, q * NFC * 128, (q + 1) * NFC * 128, 704)
            self.convert("w3_%d" % q, self.w3, self.w3s, q * NFC * 128, (q + 1) * NFC * 128, 704)
            self.convert("w2_%d" % q, self.w2, self.w2s, q * 1024, (q + 1) * 1024, 256)
        elif phase == "L0MIX":
            self.convert("win0", self.win0, self.win0s, 0, 20 * 128, 640)
            self.convert("wout0", self.wout0, self.wout0s, 0, 1024, 512)
        elif phase == "L1MIX":
            self.convert("win1", self.win1, self.win1s, 0, 24 * 128, 768)
            self.convert("wout1", self.wout1, self.wout1s, 0, 1024, 512)

    def mv(self, l, n, dc, w):
        c = ((l * 72 + n * 8 + dc) * 3 + w)
        return self.MV[:, c:c + 1]

    def lnv(self, t, l, k, dc):
        c = (l * 3 + k) * 8 + dc
        return t[:, c:c + 1]

    def build(self):
        nc = self.nc
        self.declare()
        with ExitStack() as st:
            self.st = st
            P = self.P = Prog(nc, st)
            self.s_conv = P.stream("s_conv")
            self.s_const = P.stream("s_const")
            self.s_io = P.stream("s_io")
            self.s_ioc = P.stream("s_ioc")
            self.H = self.sb(st, "H", [128, 8 * SEQ], F32)
            self.HC = self.sb(st, "HC", [128, 8 * CTX], F32)
            self.MV = self.sb(st, "MV", [128, 2 * 72 * 3], F32)
            self.LNG = self.sb(st, "LNG", [128, 48], F32)
            self.LNB = self.sb(st, "LNB", [128, 48], F32)
            self.ONES = self.sb(st, "ONES", [128, 128], F32)
            self.PS = [st.enter_context(nc.psum_tensor("ps%d" % i, [128, 512], F32)) for i in range(8)]
            self.prologue()
            for s in range(self.NS):
                self.sequence(s)
            P.barrier()
            P.emit()
        return nc

    def prologue(self):
        P, nc = self.P, self.nc
        first = [p for p in ALL_PHASES if p in self.phases][0]
        P.dma("sp", self.s_const, lambda e: e.dma_start(out=self.LNG[:], in_=self.lng), writes=["LNG"])
        P.dma("sp", self.s_const, lambda e: e.dma_start(out=self.LNB[:], in_=self.lnb), writes=["LNB"])
        P.op("pool", lambda e: e.memset(self.ONES[:], 1.0), writes=["ONES"])
        with ExitStack() as sc:
            CF = self.sb(sc, "CF", [128, 24], F32)
            CS = self.sb(sc, "CS", [128, 24], BF16)
            MB = self.sb(sc, "MB", [128, 2 * 72 * 3], F32)
            G = 8
            MW = [self.sb(sc, "MW%d" % i, [128, G * 1024], BF16) for i in range(2)]
            s_mw = [P.stream("s_mw%d" % i) for i in range(2)]
            P.dma("sp", self.s_const, lambda e: e.dma_start(out=CF[:], in_=self.cond), writes=["CF"])
            P.dma("sp", self.s_const, lambda e: e.dma_start(out=MB[:], in_=self.modb3), writes=["MB"])
            P.op("act", lambda e: e.activation(out=CS[:], in_=CF[:], func=AF.Silu), reads=["CF"], writes=["CS"])
            ngrp = 2 * 72 // G
            for g in range(ngrp):
                if g == 3:
                    self.convert_for(first)
                slot = g % 2
                r0 = g * G * 128
                P.dma("pool", s_mw[slot],
                      lambda e, slot=slot, r0=r0: e.dma_start(
                          out=MW[slot][:].rearrange("p (g c) -> p g c", g=G),
                          in_=self.modw[r0:r0 + G * 128, :].rearrange("(g p) c -> p g c", p=128)),
                      writes=[("MW", slot)])
                ps = self.PS[slot]
                fns = []
                for jl in range(G):
                    for k in range(8):
                        fns.append(lambda e, slot=slot, jl=jl, k=k, ps=ps: e.matmul(
                            ps[:, jl * 3:jl * 3 + 3], MW[slot][:, jl * 1024 + k * 128: jl * 1024 + (k + 1) * 128],
                            CS[:, k * 3:k * 3 + 3], start=(k == 0), stop=(k == 7)))
                P.group("pe", fns, reads=[("MW", slot), "CS"], writes=[("ps", slot)])
                c0 = g * G * 3
                P.op("dve", lambda e, ps=ps, c0=c0: e.tensor_tensor(
                    out=self.MV[:, c0:c0 + G * 3], in0=ps[:, 0:G * 3], in1=MB[:, c0:c0 + G * 3], op=ALU.add),
                    reads=[("ps", slot), "MB"], writes=["MV"])
            for l in range(2):
                for k in range(3):
                    c0 = (l * 72 + (3 * k + 1) * 8) * 3
                    P.op("dve", lambda e, c0=c0: e.tensor_scalar_add(out=self.MV[:, c0:c0 + 24], in0=self.MV[:, c0:c0 + 24], scalar1=1.0),
                         reads=["MV"], writes=["MV"])
                    c1 = (l * 72 + (3 * k + 2) * 8) * 3
                    rw = (1.0 if k == 1 else 0.5) / ALPHA
                    P.op("dve", lambda e, c1=c1, rw=rw: e.tensor_scalar_mul(out=self.MV[:, c1:c1 + 24], in0=self.MV[:, c1:c1 + 24], scalar1=rw),
                         reads=["MV"], writes=["MV"])
            P.barrier()

    def sequence(self, s):
        P = self.P
        ph = [p for p in ALL_PHASES if p in self.phases]
        for dc in range(8):
            P.dma("sp", self.s_io, lambda e, dc=dc: e.dma_start(out=self.H[:, dc * SEQ:(dc + 1) * SEQ],
                                                              in_=self.xT[s * 128:(s + 1) * 128, dc * SEQ:(dc + 1) * SEQ]),
                  writes=[("h", dc, t) for t in range(4)])
        P.dma("sp", self.s_ioc, lambda e: e.dma_start(out=self.HC[:], in_=self.cxT[s * 128:(s + 1) * 128, :]),
              writes=[("hc", dc) for dc in range(8)])
        units = []
        for p in ph:
            if p in ("L0F0", "L0F1", "L1F0", "L1F1") and units and units[-1][0] in ("L0F1",) and p == "L1F0":
                units[-1].append(p)
            else:
                units.append([p])
        for i, u in enumerate(units):
            if s == 0 and i + 1 < len(units):
                for p in units[i + 1]:
                    self.convert_for(p)
            if u[0] in ("L0F0", "L0F1", "L1F0", "L1F1"):
                self.ffn_phase(s, [(int(p[1]), int(p[3]), p == "L0F0") for p in u])
            elif u[0] == "L1MIX":
                self.l1mix_phase(s)
            elif u[0] == "L0MIX":
                self.l0mix_phase(s)
        for dc in range(8):
            P.dma("sp", self.s_io, lambda e, dc=dc: e.dma_start(out=self.out[s * 128:(s + 1) * 128, dc * SEQ:(dc + 1) * SEQ],
                                                              in_=self.H[:, dc * SEQ:(dc + 1) * SEQ]),
                  reads=[("h", dc, t) for t in range(4)], writes=[("out", s, dc)])
        if self.dump_ctx:
            P.dma("sp", self.s_ioc, lambda e: e.dma_start(out=self.outc[s * 128:(s + 1) * 128, :], in_=self.HC[:]),
                  reads=[("hc", dc) for dc in range(8)], writes=[("outc", s)])

    def hview(self, stream, dc, t0, n):
        if stream == "h":
            return self.H[:, dc * SEQ + t0: dc * SEQ + t0 + n]
        return self.HC[:, dc * CTX + t0: dc * CTX + t0 + n]

    def hkeys(self, stream, dc, t0, n):
        if stream == "h":
            return [("h", dc, t) for t in range(t0 // 512, (t0 + n + 511) // 512)]
        return [("hc", dc)]

    def stats_accum(self, tl, ti, stream, dc, t0, n):
        P = self.P
        SQ, SS, QQ = tl["SQ"], tl["SS"][ti], tl["QQ"][ti]
        hv = self.hview(stream, dc, t0, n)
        hk = self.hkeys(stream, dc, t0, n)
        b = dc % 2
        if dc == 0:
            P.op("act", lambda e: e.activation(out=QQ[:, 0:n], in_=hv, func=AF.Square), reads=hk, writes=[("QQ", ti)])
            return
        P.op("act", lambda e: e.activation(out=SQ[b][:, 0:n], in_=hv, func=AF.Square), reads=hk, writes=[("SQ", b)])
        P.op("dve", lambda e: e.tensor_tensor(out=QQ[:, 0:n], in0=QQ[:, 0:n], in1=SQ[b][:, 0:n], op=ALU.add),
             reads=[("QQ", ti), ("SQ", b)], writes=[("QQ", ti)])
        if dc == 1:
            hv0 = self.hview(stream, 0, t0, n)
            P.op("pool", lambda e: e.tensor_tensor(out=SS[:, 0:n], in0=hv0, in1=hv, op=ALU.add),
                 reads=hk + self.hkeys(stream, 0, t0, n), writes=[("SS", ti)])
        else:
            P.op("pool", lambda e: e.tensor_tensor(out=SS[:, 0:n], in0=SS[:, 0:n], in1=hv, op=ALU.add),
                 reads=hk + [("SS", ti)], writes=[("SS", ti)])

    def ln_tile(self, tl, stream, l, k, w, t0, n, ti=0, pre=False):
        P = self.P
        T1, MEAN, MSQ, VAR, RSTD, EPS = tl["T1"], tl["MEAN"], tl["MSQ"], tl["VAR"], tl["RSTD"], tl["EPS"]
        SS, QQ = tl["SS"][ti], tl["QQ"][ti]
        psS, psQ = self.PS[6], self.PS[7]
        if not pre:
            for dc in range(8):
                self.stats_accum(tl, ti, stream, dc, t0, n)
        P.group("pe", [lambda e: e.matmul(psS[:, 0:n], self.ONES[:], SS[:, 0:n], start=True, stop=True)],
                reads=[("SS", ti), "ONES"], writes=[("ps", 6)])
        P.group("pe", [lambda e: e.matmul(psQ[:, 0:n], self.ONES[:], QQ[:, 0:n], start=True, stop=True)],
                reads=[("QQ", ti), "ONES"], writes=[("ps", 7)])
        P.op("dve", lambda e: e.tensor_scalar_mul(out=MEAN[:, 0:n], in0=psS[:, 0:n], scalar1=1.0 / D),
             reads=[("ps", 6)], writes=["MEAN"])
        P.op("dve", lambda e: e.tensor_tensor(out=MSQ[:, 0:n], in0=MEAN[:, 0:n], in1=MEAN[:, 0:n], op=ALU.mult),
             reads=["MEAN"], writes=["MSQ"])
        P.op("dve", lambda e: e.scalar_tensor_tensor(out=VAR[:, 0:n], in0=psQ[:, 0:n], scalar=1.0 / D, in1=MSQ[:, 0:n],
                                                     op0=ALU.mult, op1=ALU.subtract),
             reads=[("ps", 7), "MSQ"], writes=["VAR"])
        P.op("act", lambda e: e.activation(out=VAR[:, 0:n], in_=VAR[:, 0:n], func=AF.Sqrt, bias=EPS[:, 0:1]),
             reads=["VAR", "EPS"], writes=["VAR"])
        P.op("dve", lambda e: e.reciprocal(out=RSTD[:, 0:n], in_=VAR[:, 0:n]), reads=["VAR"], writes=["RSTD"])
        for dc in range(8):
            hv = self.hview(stream, dc, t0, n)
            hk = self.hkeys(stream, dc, t0, n)
            b = dc % 4
            e1 = "pool" if dc % 2 == 0 else "dve"
            P.op(e1, lambda e, hv=hv, b=b: e.tensor_tensor(out=T1[b][:, 0:n], in0=hv, in1=MEAN[:, 0:n], op=ALU.subtract),
                 reads=hk + ["MEAN"], writes=[("T1", b)])
            P.op("dve", lambda e, b=b: e.tensor_tensor(out=T1[b][:, 0:n], in0=T1[b][:, 0:n], in1=RSTD[:, 0:n], op=ALU.mult),
                 reads=[("T1", b), "RSTD"], writes=[("T1", b)])
            P.op("act", lambda e, hv=hv, b=b, dc=dc: e.activation(out=hv, in_=T1[b][:, 0:n], func=AF.Identity,
                                                                 scale=self.lnv(self.LNG, l, k, dc), bias=self.lnv(self.LNB, l, k, dc)),
                 reads=[("T1", b), "LNG", "LNB"], writes=hk)

    def ln_tiles_alloc(self, sc, ntiles=1):
        tl = {}
        tl["SQ"] = [self.sb(sc, "SQ%d" % i, [128, 512], F32) for i in range(2)]
        tl["T1"] = [self.sb(sc, "T1%d" % i, [128, 512], F32) for i in range(4)]
        for nm in ("MEAN", "MSQ", "VAR", "RSTD"):
            tl[nm] = self.sb(sc, nm, [128, 512], F32)
        tl["SS"] = [self.sb(sc, "SS%d" % i, [128, 512], F32) for i in range(ntiles)]
        tl["QQ"] = [self.sb(sc, "QQ%d" % i, [128, 512], F32) for i in range(ntiles)]
        EPS = tl["EPS"] = self.sb(sc, "EPS", [128, 1], F32)
        self.P.op("pool", lambda e: e.memset(EPS[:], LN_EPS), writes=["EPS"])
        return tl

    def ffn_phase(self, s, specs):
        P = self.P
        with ExitStack() as sc:
            TB = 1024
            Z = self.sb(sc, "Z", [128, 8 * TB], BF16)
            G = self.sb(sc, "G", [128, NFC * TB], BF16)
            W13 = [self.sb(sc, "W13_%d" % i, [128, 4 * 1024], BF16) for i in range(3)]
            W2 = [self.sb(sc, "W2_%d" % i, [128, DFF], BF16) for i in range(2)]
            SL = [self.sb(sc, "SL%d" % i, [128, 512], F32) for i in range(2)]
            tl = self.ln_tiles_alloc(sc, 2)
            s13 = [P.stream() for _ in range(len(W13))]
            s2 = [P.stream() for _ in range(len(W2))]
            st = {"g13": 0, "g2": 0, "it": 0, "itb": 0}
            blocks = []
            for (l, which, with_ctx) in specs:
                q = l * 2 + which
                k = 0 if which == 0 else 2
                if with_ctx:
                    blocks.append((l, q, k, "hc", 2, 0, CTX))
                blocks += [(l, q, k, "h", s, 0, TB), (l, q, k, "h", s, TB, TB)]
            ctxs = []
            for (l, q, k, stream, w, t0, tb) in blocks:
                ctxs.append(dict(l=l, q=q, k=k, stream=stream, w=(2 if stream == "hc" else s), t0=t0, tb=tb,
                                 Z=Z, G=G, W13=W13, W2=W2, SL=SL, tl=tl, s13=s13, s2=s2, st=st,
                                 tiles=[(a, min(512, tb - a)) for a in range(0, tb, 512)]))
            self.ffn_z(ctxs[0])
            for i, c in enumerate(ctxs):
                self.ffn_ab(c)
                if i + 1 < len(ctxs):
                    self.ffn_z(ctxs[i + 1])
                for ti, (a, n) in enumerate(c["tiles"]):
                    self.ln_tile(tl, c["stream"], c["l"], c["k"], c["w"], c["t0"] + a, n, ti=ti, pre=True)
            P.barrier()

    def ffn_z(self, c):
        P = self.P
        l, k, w, stream, t0, tb, Z = c["l"], c["k"], c["w"], c["stream"], c["t0"], c["tb"], c["Z"]
        for dc in range(8):
            hv = self.hview(stream, dc, t0, tb)
            hk = self.hkeys(stream, dc, t0, tb)
            zv = Z[:, dc * tb: (dc + 1) * tb]
            if dc % 2 == 0:
                P.op("act", lambda e, hv=hv, zv=zv, dc=dc: e.activation(out=zv, in_=hv, func=AF.Identity,
                                                                       scale=self.mv(l, 3 * k + 1, dc, w), bias=self.mv(l, 3 * k, dc, w)),
                     reads=hk + ["MV"], writes=[("Z", dc)])
            else:
                P.op("dve", lambda e, hv=hv, zv=zv, dc=dc: e.tensor_scalar(out=zv, in0=hv, scalar1=self.mv(l, 3 * k + 1, dc, w),
                                                                          scalar2=self.mv(l, 3 * k, dc, w), op0=ALU.mult, op1=ALU.add),
                     reads=hk + ["MV"], writes=[("Z", dc)])

    def ffn_ab(self, c):
        P = self.P
        l, q, k, w, stream, t0, tb = c["l"], c["q"], c["k"], c["w"], c["stream"], c["t0"], c["tb"]
        Z, G, W13, W2, SL, tl, s13, s2, st, tiles = c["Z"], c["G"], c["W13"], c["W2"], c["SL"], c["tl"], c["s13"], c["s2"], c["st"], c["tiles"]
        N13, N2 = len(W13), len(W2)

        def load13(fg):
            slot = st["g13"] % N13
            st["g13"] += 1
            r0 = (q * NFC + fg * 2) * 128
            for wi, src in enumerate((self.w1s, self.w3s)):
                P.dma("sp", s13[slot], lambda e, slot=slot, wi=wi, src=src, r0=r0: e.dma_start(
                    out=W13[slot][:, wi * 2048:(wi + 1) * 2048].rearrange("p (f c) -> p f c", f=2),
                    in_=src[r0:r0 + 256, :].rearrange("(f p) c -> p f c", p=128)),
                    reads=self.scr_keys["w1_%d" % q] + self.scr_keys["w3_%d" % q], writes=[("W13", slot)])
            return slot

        def load2(dc):
            slot = st["g2"] % N2
            st["g2"] += 1
            r0 = (q * 8 + dc) * 128
            P.dma("sp", s2[slot], lambda e, slot=slot, r0=r0: e.dma_start(out=W2[slot][:], in_=self.w2s[r0:r0 + 128, :]),
                  reads=self.scr_keys["w2_%d" % q], writes=[("W2", slot)])
            return slot

        nfg = NFC // 2
        slots13 = {0: load13(0), 1: load13(1)}
        slots2 = {}
        for fg in range(nfg):
            if fg + 2 < nfg:
                slots13[fg + 2] = load13(fg + 2)
            if fg == nfg - 2:
                slots2[0] = load2(0)
            if fg == nfg - 1:
                slots2[1] = load2(1)
            slot = slots13[fg]
            for fl in range(2):
                f = fg * 2 + fl
                for (a, n) in tiles:
                    b = st["it"] % 2
                    st["it"] += 1
                    p1, p3 = self.PS[b], self.PS[2 + b]
                    for wi, pp in ((0, p1), (1, p3)):
                        fns = []
                        for kk in range(8):
                            fns.append(lambda e, slot=slot, wi=wi, fl=fl, kk=kk, pp=pp, a=a, n=n: e.matmul(
                                pp[:, 0:n], W13[slot][:, wi * 2048 + fl * 1024 + kk * 128: wi * 2048 + fl * 1024 + (kk + 1) * 128],
                                Z[:, kk * tb + a: kk * tb + a + n], start=(kk == 0), stop=(kk == 7)))
                        P.group("pe", fns, reads=[("W13", slot)] + [("Z", kk) for kk in range(8)],
                                writes=[("ps", b if wi == 0 else 2 + b)])
                    P.op("act", lambda e, b=b, p1=p1, n=n: e.activation(out=SL[b][:, 0:n], in_=p1[:, 0:n], func=AF.Silu),
                         reads=[("ps", b)], writes=[("SL", b)])
                    P.op("dve", lambda e, b=b, p3=p3, f=f, a=a, n=n: e.tensor_tensor(
                        out=G[:, f * tb + a: f * tb + a + n], in0=SL[b][:, 0:n], in1=p3[:, 0:n], op=ALU.mult),
                        reads=[("SL", b), ("ps", 2 + b)], writes=[("G", f)])
        for dc in range(8):
            if dc + 1 < 8 and dc >= 1:
                slots2[dc + 1] = load2(dc + 1)
            slot = slots2[dc]
            for ti, (a, n) in enumerate(tiles):
                b = st["itb"] % 2
                st["itb"] += 1
                py = self.PS[4 + b]
                fns = []
                for f in range(NFC):
                    fns.append(lambda e, slot=slot, f=f, py=py, a=a, n=n: e.matmul(
                        py[:, 0:n], W2[slot][:, f * 128:(f + 1) * 128], G[:, f * tb + a: f * tb + a + n],
                        start=(f == 0), stop=(f == NFC - 1)))
                P.group("pe", fns, reads=[("W2", slot)] + [("G", f) for f in range(NFC)], writes=[("ps", 4 + b)])
                hv = self.hview(stream, dc, t0 + a, n)
                hk = self.hkeys(stream, dc, t0 + a, n)
                P.op("dve", lambda e, py=py, hv=hv, dc=dc, n=n: e.scalar_tensor_tensor(
                    out=hv, in0=py[:, 0:n], scalar=self.mv(l, 3 * k + 2, dc, w), in1=hv, op0=ALU.mult, op1=ALU.add),
                    reads=[("ps", 4 + b), "MV"] + hk, writes=hk)
                self.stats_accum(tl, ti, stream, dc, t0 + a, n)

    def l1mix_phase(self, s):
        P = self.P
        l, k, w = 1, 1, s
        with ExitStack() as sc:
            Z = self.sb(sc, "Zm", [128, 8 * SEQ], BF16)
            F = self.sb(sc, "Fm", [128, 8 * SEQ], BF16)
            sc2 = ExitStack()
            U = [self.sb(sc2, "U%d" % i, [128, SEQ + 2], F32) for i in range(2)]
            GB = [self.sb(sc2, "GB%d" % i, [128, SEQ], BF16) for i in range(2)]
            ACC = self.sb(sc2, "ACC", [128, SEQ], F32)
            XV = [self.sb(sc2, "XV%d" % i, [128, 512], F32) for i in range(2)]
            WIN = [self.sb(sc2, "WIN%d" % i, [128, 3 * 1024], BF16) for i in range(2)]
            SCW = self.sb(sc2, "SCW", [128, 24], F32)
            swin = [P.stream() for _ in range(2)]
            swout = [P.stream() for _ in range(2)]
            P.dma("sp", self.s_const, lambda e: e.dma_start(out=SCW[:], in_=self.scw), writes=["SCW"])
            for i in range(2):
                P.op("pool", lambda e, i=i: e.memset(U[i][:, 0:1], 0.0), writes=[("U", i)])
                P.op("pool", lambda e, i=i: e.memset(U[i][:, SEQ + 1:SEQ + 2], 0.0), writes=[("U", i)])
            for dc in range(8):
                hv = self.hview("h", dc, 0, SEQ)
                hk = self.hkeys("h", dc, 0, SEQ)
                zv = Z[:, dc * SEQ:(dc + 1) * SEQ]
                if dc % 2 == 0:
                    P.op("act", lambda e, hv=hv, zv=zv, dc=dc: e.activation(out=zv, in_=hv, func=AF.Identity,
                                                                           scale=self.mv(l, 3 * k + 1, dc, w), bias=self.mv(l, 3 * k, dc, w)),
                         reads=hk + ["MV"], writes=[("Z", dc)])
                else:
                    P.op("dve", lambda e, hv=hv, zv=zv, dc=dc: e.tensor_scalar(out=zv, in0=hv, scalar1=self.mv(l, 3 * k + 1, dc, w),
                                                                              scalar2=self.mv(l, 3 * k, dc, w), op0=ALU.mult, op1=ALU.add),
                         reads=hk + ["MV"], writes=[("Z", dc)])

            def loadwin(dc):
                slot = dc % 2
                for j in range(3):
                    r0 = (j * 8 + dc) * 128
                    P.dma("sp", swin[slot], lambda e, slot=slot, j=j, r0=r0: e.dma_start(
                        out=WIN[slot][:, j * 1024:(j + 1) * 1024], in_=self.win1s[r0:r0 + 128, :]),
                        reads=self.scr_keys["win1"], writes=[("WIN", slot)])

            def loadwout(dc):
                slot = dc % 2
                r0 = dc * 128
                P.dma("sp", swout[slot], lambda e, slot=slot, r0=r0: e.dma_start(out=WOUT[slot][:], in_=self.wout1s[r0:r0 + 128, :]),
                      reads=self.scr_keys["wout1"], writes=[("WOUT", slot)])

            loadwin(0)
            it = 0
            for dc in range(8):
                if dc + 1 < 8:
                    loadwin(dc + 1)
                slot = dc % 2
                ub = dc % 2
                for tt in range(4):
                    pb = (it % 2) * 3
                    it += 1
                    xb = tt % 2
                    for j in range(3):
                        pp = self.PS[pb + j]
                        fns = []
                        for kk in range(8):
                            fns.append(lambda e, slot=slot, j=j, kk=kk, pp=pp, tt=tt: e.matmul(
                                pp[:, :], WIN[slot][:, j * 1024 + kk * 128: j * 1024 + (kk + 1) * 128],
                                Z[:, kk * SEQ + tt * 512: kk * SEQ + (tt + 1) * 512], start=(kk == 0), stop=(kk == 7)))
                        P.group("pe", fns, reads=[("WIN", slot)] + [("Z", kk) for kk in range(8)], writes=[("ps", pb + j)])
                    P.op("act", lambda e, pb=pb, ub=ub, tt=tt: e.activation(out=GB[ub][:, tt * 512:(tt + 1) * 512], in_=self.PS[pb][:, :], func=AF.Copy),
                         reads=[("ps", pb)], writes=[("GB", ub)])
                    P.op("act", lambda e, pb=pb, xb=xb: e.activation(out=XV[xb][:, :], in_=self.PS[pb + 2][:, :], func=AF.Copy),
                         reads=[("ps", pb + 2)], writes=[("XV", xb)])
                    P.op("dve", lambda e, pb=pb, xb=xb, ub=ub, tt=tt: e.tensor_tensor(
                        out=U[ub][:, 1 + tt * 512: 1 + (tt + 1) * 512], in0=self.PS[pb + 1][:, :], in1=XV[xb][:, :], op=ALU.mult),
                        reads=[("ps", pb + 1), ("XV", xb)], writes=[("U", ub)])
                sw = lambda tap, dc=dc: SCW[:, dc * 3 + tap: dc * 3 + tap + 1]
                P.op("act", lambda e, ub=ub, sw=sw: e.activation(out=ACC[:, :], in_=U[ub][:, 0:SEQ], func=AF.Copy, scale=sw(0)),
                     reads=[("U", ub), "SCW"], writes=["ACC"])
                for tap in (1, 2):
                    P.op("dve", lambda e, ub=ub, sw=sw, tap=tap: e.scalar_tensor_tensor(
                        out=ACC[:, :], in0=U[ub][:, tap:tap + SEQ], scalar=sw(tap), in1=ACC[:, :], op0=ALU.mult, op1=ALU.add),
                        reads=[("U", ub), "SCW", "ACC"], writes=["ACC"])
                P.op("dve", lambda e, ub=ub, dc=dc: e.tensor_tensor(out=F[:, dc * SEQ:(dc + 1) * SEQ], in0=ACC[:, :], in1=GB[ub][:, :], op=ALU.mult),
                     reads=["ACC", ("GB", ub)], writes=[("F", dc)])
            P.barrier()
            sc2.close()
            WOUT = [self.sb(sc, "WOUT%d" % i, [128, 1024], BF16) for i in range(2)]
            tl = self.ln_tiles_alloc(sc, 4)
            loadwout(0)
            it = 0
            for dc in range(8):
                if dc + 1 < 8:
                    loadwout(dc + 1)
                slot = dc % 2
                for tt in range(4):
                    b = 6 + it % 2
                    it += 1
                    py = self.PS[b]
                    fns = []
                    for fc in range(8):
                        fns.append(lambda e, slot=slot, fc=fc, py=py, tt=tt: e.matmul(
                            py[:, :], WOUT[slot][:, fc * 128:(fc + 1) * 128], F[:, fc * SEQ + tt * 512: fc * SEQ + (tt + 1) * 512],
                            start=(fc == 0), stop=(fc == 7)))
                    P.group("pe", fns, reads=[("WOUT", slot)] + [("F", fc) for fc in range(8)], writes=[("ps", b)])
                    hv = self.hview("h", dc, tt * 512, 512)
                    hk = self.hkeys("h", dc, tt * 512, 512)
                    P.op("dve", lambda e, py=py, hv=hv, dc=dc: e.scalar_tensor_tensor(
                        out=hv, in0=py[:, :], scalar=self.mv(l, 3 * k + 2, dc, w), in1=hv, op0=ALU.mult, op1=ALU.add),
                        reads=[("ps", b), "MV"] + hk, writes=hk)
                    self.stats_accum(tl, tt, "h", dc, tt * 512, 512)
            for tt in range(4):
                self.ln_tile(tl, "h", l, k, w, tt * 512, 512, ti=tt, pre=True)
            P.barrier()

    def l0mix_phase(self, s):
        P = self.P
        l, k, w = 0, 1, s
        ZT = CTX + SEQ
        with ExitStack() as sc:
            Z = self.sb(sc, "Z0", [128, 8 * ZT], BF16)
            F = self.sb(sc, "F0", [128, 4 * SEQ], BF16)
            NWS = 4
            WS = [self.sb(sc, "WS%d" % i, [128, 1024], BF16) for i in range(NWS)]
            sws = [P.stream() for _ in range(NWS)]
            WO = [self.sb(sc, "WO%d" % i, [128, 512], BF16) for i in range(2)]
            swo = [P.stream() for _ in range(2)]
            wst = {"n": 0}

            def loadw(cc):
                slot = wst["n"] % NWS
                wst["n"] += 1
                P.dma("sp", sws[slot], lambda e, slot=slot, cc=cc: e.dma_start(out=WS[slot][:], in_=self.win0s[cc * 128:(cc + 1) * 128, :]),
                      reads=self.scr_keys["win0"], writes=[("WS", slot)])
                return slot

            if self.dump_ctx:
                P.dma("sp", self.s_ioc, lambda e: e.dma_start(out=self.dbg3[:, :], in_=self.H[:]),
                      reads=[("h", dc, t) for dc in range(8) for t in range(4)], writes=["dbg3"])
                P.dma("sp", self.s_ioc, lambda e: e.dma_start(out=self.dbg4[:, :], in_=self.HC[:]),
                      reads=[("hc", dc) for dc in range(8)], writes=["dbg4"])
            for dc in range(8):
                for (stream, ww, off, n) in (("hc", 2, 0, CTX), ("h", s, CTX, SEQ)):
                    hv = self.hview(stream, dc, 0, n)
                    hk = self.hkeys(stream, dc, 0, n)
                    zv = Z[:, dc * ZT + off: dc * ZT + off + n]
                    if dc % 2 == 0:
                        P.op("act", lambda e, hv=hv, zv=zv, dc=dc, ww=ww: e.activation(
                            out=zv, in_=hv, func=AF.Identity, scale=self.mv(l, 3 * k + 1, dc, ww), bias=self.mv(l, 3 * k, dc, ww)),
                            reads=hk + ["MV"], writes=[("Z", dc)])
                    else:
                        P.op("dve", lambda e, hv=hv, zv=zv, dc=dc, ww=ww: e.tensor_scalar(
                            out=zv, in0=hv, scalar1=self.mv(l, 3 * k + 1, dc, ww), scalar2=self.mv(l, 3 * k, dc, ww),
                            op0=ALU.mult, op1=ALU.add), reads=hk + ["MV"], writes=[("Z", dc)])
            zkeys = [("Z", kk) for kk in range(8)]
            pst = {"n": 0}

            def proj(slot, c0, n, evac, bank=None):
                if bank is None:
                    bank = 2 + pst["n"] % 2
                    pst["n"] += 1
                pp = self.PS[bank]
                fns = []
                for kk in range(8):
                    fns.append(lambda e, kk=kk, pp=pp: e.matmul(
                        pp[:, 0:n], WS[slot][:, kk * 128:(kk + 1) * 128], Z[:, kk * ZT + c0: kk * ZT + c0 + n],
                        start=(kk == 0), stop=(kk == 7)))
                P.group("pe", fns, reads=[("WS", slot)] + zkeys, writes=[("ps", bank)])
                evac(pp, bank)

            def halfproj(half, tl=None):
                it = 0
                def loadwo(dc):
                    slot = dc % 2
                    P.dma("sp", swo[slot], lambda e, slot=slot, dc=dc: e.dma_start(
                        out=WO[slot][:], in_=self.wout0s[dc * 128:(dc + 1) * 128, half * 512:(half + 1) * 512]),
                        reads=self.scr_keys["wout0"], writes=[("WO", slot)])
                loadwo(0)
                for dc in range(8):
                    if dc + 1 < 8:
                        loadwo(dc + 1)
                    slot = dc % 2
                    for tt in range(4):
                        b = 2 + it % 2
                        it += 1
                        py = self.PS[b]
                        fns = []
                        for fc in range(4):
                            fns.append(lambda e, slot=slot, fc=fc, py=py, tt=tt: e.matmul(
                                py[:, :], WO[slot][:, fc * 128:(fc + 1) * 128], F[:, fc * SEQ + tt * 512: fc * SEQ + (tt + 1) * 512],
                                start=(fc == 0), stop=(fc == 3)))
                        P.group("pe", fns, reads=[("WO", slot)] + [("F", fc) for fc in range(4)], writes=[("ps", b)])
                        hv = self.hview("h", dc, tt * 512, 512)
                        hk = self.hkeys("h", dc, tt * 512, 512)
                        P.op("dve", lambda e, py=py, hv=hv, dc=dc: e.scalar_tensor_tensor(
                            out=hv, in0=py[:, :], scalar=self.mv(l, 3 * k + 2, dc, w), in1=hv, op0=ALU.mult, op1=ALU.add),
                            reads=[("ps", b), "MV"] + hk, writes=hk)
                        if tl is not None:
                            self.stats_accum(tl, tt, "h", dc, tt * 512, 512)

            with ExitStack() as sn:
                Tt = self.sb(sn, "Tt", [128, 3840], BF16)
                ID2 = self.sb(sn, "ID2", [128, 64], BF16)
                ONB = self.sb(sn, "ONB", [128, 64], BF16)
                with ExitStack() as stmp:
                    RPb = self.sb(stmp, "RPb", [128, 3840], BF16)
                    NMb = self.sb(stmp, "NMb", [128, 3840], BF16)
                    P.dma("pool", self.s_const, lambda e: e.dma_start(out=RPb[:], in_=self.rpbg), writes=["RPb"])
                    P.dma("pool", self.s_const, lambda e: e.dma_start(out=NMb[:], in_=self.nmask), writes=["NMb"])
                    P.dma("pool", self.s_const, lambda e: e.dma_start(out=ID2[:], in_=self.id2), writes=["ID2"])
                    P.op("dve", lambda e: e.tensor_tensor(out=Tt[:], in0=RPb[:], in1=NMb[:], op=ALU.add), reads=["RPb", "NMb"], writes=["Tt"])
                    P.op("pool", lambda e: e.memset(ONB[:], 1.0), writes=["ONB"])
                    P.barrier()
                QT = self.sb(sn, "QT", [128, SEQ], BF16)
                KT = self.sb(sn, "KT", [128, ZT], BF16)
                V = self.sb(sn, "V", [128, 18 * 128], BF16)
                V2 = self.sb(sn, "V2", [128, 15 * 128], BF16)
                PC = [self.sb(sn, "PC%d" % i, [128, 2 * SEQ], BF16) for i in range(2)]
                PL = [self.sb(sn, "PL%d" % i, [128, 256], BF16) for i in range(2)]
                RD = [self.sb(sn, "RD%d" % i, [128, 512], F32) for i in range(2)]
                PLS = [self.sb(sn, "PLS%d" % i, [128, 512], F32) for i in range(2)]
                for hp in range(4):
                    sq_, sk_, sv_ = loadw(hp), loadw(4 + hp), loadw(8 + hp)
                    for tt in range(4):
                        proj(sq_, CTX + tt * 512, 512, lambda pp, bank, tt=tt: P.op(
                            "act", lambda e, pp=pp, tt=tt: e.activation(out=QT[:, tt * 512:(tt + 1) * 512], in_=pp[:, :], func=AF.Copy, scale=0.125),
                            reads=[("ps", bank)], writes=["QT"]))
                    proj(sk_, 0, CTX, lambda pp, bank: P.op(
                        "dve", lambda e, pp=pp: e.tensor_copy(out=KT[:, 0:CTX], in_=pp[:, 0:CTX]), reads=[("ps", bank)], writes=["KT"]))
                    for tt in range(4):
                        proj(sk_, CTX + tt * 512, 512, lambda pp, bank, tt=tt: P.op(
                            "dve", lambda e, pp=pp, tt=tt: e.tensor_copy(out=KT[:, CTX + tt * 512: CTX + (tt + 1) * 512], in_=pp[:, :]),
                            reads=[("ps", bank)], writes=["KT"]))
                    def vproj(dst, dkey, chunks):
                        for g0 in range(0, len(chunks), 4):
                            grp = chunks[g0:g0 + 4]
                            bank = 2 + pst["n"] % 2
                            pst["n"] += 1
                            pp = self.PS[bank]
                            fns = []
                            for gi, (ci, tok0) in enumerate(grp):
                                for kk in range(8):
                                    fns.append(lambda e, gi=gi, tok0=tok0, kk=kk, pp=pp, sv_=sv_: e.matmul(
                                        pp[:, gi * 128:(gi + 1) * 128], Z[:, kk * ZT + tok0: kk * ZT + tok0 + 128],
                                        WS[sv_][:, kk * 128:(kk + 1) * 128], start=(kk == 0), stop=(kk == 7)))
                            P.group("pe", fns, reads=[("WS", sv_)] + zkeys, writes=[("ps", bank)])
                            c0 = grp[0][0]
                            nn = len(grp) * 128
                            P.op("act", lambda e, pp=pp, c0=c0, nn=nn, dst=dst: e.activation(out=dst[:, c0 * 128: c0 * 128 + nn], in_=pp[:, 0:nn], func=AF.Copy),
                                 reads=[("ps", bank)], writes=[dkey])
                    vproj(V, "V", [(ci, ci * 128) for ci in range(18)])
                    vproj(V2, "V2", [(ci, CTX + 64 + ci * 128) for ci in range(15)])
                    for hh in range(2):
                        hs = slice(hh * 64, (hh + 1) * 64)
                        for cc in range(2):
                            for qt in range(4):
                                bank = 2 + pst["n"] % 2
                                pst["n"] += 1
                                pp = self.PS[bank]
                                P.group("pe", [lambda e, pp=pp, hs=hs, cc=cc, qt=qt: e.matmul(
                                    pp[:, :], KT[hs, cc * 128:(cc + 1) * 128], QT[hs, qt * 512:(qt + 1) * 512], start=True, stop=True)],
                                    reads=["KT", "QT"], writes=[("ps", bank)])
                                P.op("act", lambda e, pp=pp, hh=hh, cc=cc, qt=qt: e.activation(
                                    out=PC[hh][:, cc * SEQ + qt * 512: cc * SEQ + (qt + 1) * 512], in_=pp[:, :], func=AF.Exp),
                                    reads=[("ps", bank)], writes=[("PC", hh)])
                    units = [(rg, hh, rr) for rg in range(4) for hh in range(2) for rr in range(8)]

                    def qk(ui):
                        rg, hh, rr = units[ui]
                        r = rg * 8 + rr
                        r0 = min(max(r - 4, 0), 24)
                        hs = slice(hh * 64, (hh + 1) * 64)
                        bank = ui % 2
                        pp = self.PS[bank]
                        fns = []
                        for c in range(4):
                            k0 = CTX + (r0 + 2 * c) * 64
                            ro0 = r0 + 2 * c - r + 7
                            t0 = hp * 960 + ro0 * 64
                            fns.append(lambda e, pp=pp, hs=hs, c=c, k0=k0, r=r: e.matmul(
                                pp[:, c * 64:(c + 1) * 64], KT[hs, k0:k0 + 128], QT[hs, r * 64:(r + 1) * 64], start=True, stop=False))
                            fns.append(lambda e, pp=pp, hs=hs, c=c, t0=t0: e.matmul(
                                pp[:, c * 64:(c + 1) * 64], Tt[hs, t0:t0 + 128], ID2[hs, 0:64], start=False, stop=True))
                        P.group("pe", fns, reads=["KT", "QT", "Tt", "ID2"], writes=[("ps", bank)])
                        P.op("act", lambda e, pp=pp, bank=bank: e.activation(out=PL[bank][:, :], in_=pp[:, 0:256], func=AF.Exp),
                             reads=[("ps", bank)], writes=[("PL", bank)])

                    def pv(ui):
                        rg, hh, rr = units[ui]
                        r = rg * 8 + rr
                        r0 = min(max(r - 4, 0), 24)
                        bank = ui % 2
                        nb, db = 4 + (rg % 2), 6 + (rg % 2)
                        hs = slice(hh * 64, (hh + 1) * 64)
                        sl_ = (rg * 2 + hh) % 2
                        if rr == 0:
                            fns = []
                            for cc in range(2):
                                fns.append(lambda e, cc=cc, nb=nb, hs=hs, hh=hh, rg=rg: e.matmul(
                                    self.PS[nb][hs, :], V[:, cc * 128 + hh * 64: cc * 128 + (hh + 1) * 64],
                                    PC[hh][:, cc * SEQ + rg * 512: cc * SEQ + (rg + 1) * 512], start=(cc == 0), stop=False))
                            for cc in range(2):
                                fns.append(lambda e, cc=cc, db=db, hs=hs, hh=hh, rg=rg: e.matmul(
                                    self.PS[db][hs, :], ONB[:, 0:64],
                                    PC[hh][:, cc * SEQ + rg * 512: cc * SEQ + (rg + 1) * 512], start=(cc == 0), stop=False))
                            P.group("pe", fns, reads=["V", ("PC", hh), "ONB"], writes=[("ps", nb), ("ps", db)])
                        P.op("dve", lambda e, bank=bank, sl_=sl_, rr=rr: e.tensor_reduce(
                            out=PLS[sl_][:, rr * 64:(rr + 1) * 64], in_=PL[bank][:, :].rearrange("p (c q) -> p q c", c=4),
                            axis=mybir.AxisListType.X, op=ALU.add), reads=[("PL", bank)], writes=[("PLS", sl_)])
                        fns = []
                        for c in range(4):
                            if r0 % 2 == 0:
                                vsrc, ci = V, 2 + r0 // 2 + c
                            else:
                                vsrc, ci = V2, (r0 - 1) // 2 + c
                            lv = vsrc[:, ci * 128 + hh * 64: ci * 128 + (hh + 1) * 64]
                            rv = PL[bank][:, c * 64:(c + 1) * 64]
                            fns.append(lambda e, lv=lv, rv=rv, c=c, nb=nb, hs=hs, rr=rr: e.matmul(
                                self.PS[nb][hs, rr * 64:(rr + 1) * 64], lv, rv, start=False, stop=(c == 3)))
                        P.group("pe", fns, reads=["V", "V2", ("PL", bank)], writes=[("ps", nb)])
                        if rr == 7:
                            P.group("pe", [lambda e, db=db, hs=hs, sl_=sl_: e.matmul(
                                self.PS[db][hs, :], self.ONES[:, 0:64], PLS[sl_][:, :], start=False, stop=True)],
                                reads=[("PLS", sl_), "ONES"], writes=[("ps", db)])
                        if hh == 1 and rr == 7:
                            rb = rg % 2
                            P.op("dve", lambda e, rb=rb, db=db: e.reciprocal(out=RD[rb][:, :], in_=self.PS[db][:, :]),
                                 reads=[("ps", db)], writes=[("RD", rb)])
                            P.op("dve", lambda e, rb=rb, nb=nb, rg=rg, hp=hp: e.tensor_tensor(
                                out=F[:, hp * SEQ + rg * 512: hp * SEQ + (rg + 1) * 512], in0=self.PS[nb][:, :], in1=RD[rb][:, :], op=ALU.mult),
                                reads=[("ps", nb), ("RD", rb)], writes=[("F", hp)])

                    qk(0)
                    for ui in range(len(units)):
                        if ui + 1 < len(units):
                            qk(ui + 1)
                        pv(ui)
                P.barrier()
            if self.dump_ctx:
                P.dma("sp", self.s_ioc, lambda e: e.dma_start(out=self.dbg[0:128, :], in_=F[:]), reads=[("F", i) for i in range(4)], writes=["dbg0"])
                P.barrier()
            halfproj(0)
            P.barrier()

            with ExitStack() as sl:
                XRT = self.sb(sl, "XRT", [128, ZT + 8], F32)
                XC = self.sb(sl, "XC", [128, ZT], F32)
                XCB = self.sb(sl, "XCB", [128, ZT], BF16)
                GG = self.sb(sl, "GG", [128, SEQ], BF16)
                A = self.sb(sl, "A", [128, ZT], F32)
                B = self.sb(sl, "B", [128, ZT], F32)
                TR = [self.sb(sl, "TR%d" % i, [128, 512], F32) for i in range(2)]
                T2 = self.sb(sl, "T2", [128, ZT], F32)
                BD = self.sb(sl, "BD", [128, 16 * 128], BF16)
                LCW = self.sb(sl, "LCW", [128, 16], F32)
                LCB = self.sb(sl, "LCB", [128, 4], F32)
                LBA = self.sb(sl, "LBA", [128, 8], F32)
                LBX = self.sb(sl, "LBX", [128, 8], F32)
                CL = self.sb(sl, "CL", [128, 8], F32)
                CLH = self.sb(sl, "CLH", [128, 8], F32)
                QRT = self.sb(sl, "QRT", [128, 1], F32)
                P.dma("pool", self.s_const, lambda e: e.dma_start(out=BD[:, 0:1024].rearrange("p (g c) -> p g c", g=8),
                                                                  in_=self.lwa.rearrange("(g p) c -> p g c", p=128)), writes=["BD"])
                P.dma("pool", self.s_const, lambda e: e.dma_start(out=BD[:, 1024:2048].rearrange("p (g c) -> p g c", g=8),
                                                                  in_=self.lwx.rearrange("(g p) c -> p g c", p=128)), writes=["BD"])
                for (t, src, nm) in ((LCW, self.lcw, "LCW"), (LCB, self.lcb, "LCB"), (LBA, self.lba, "LBA"), (LBX, self.lbx, "LBX"), (CL, self.llam, "CL")):
                    P.dma("sp", self.s_const, lambda e, t=t, src=src: e.dma_start(out=t[:], in_=src), writes=[nm])
                P.op("pool", lambda e: e.memset(QRT[:], 0.25), writes=["QRT"])
                P.op("act", lambda e: e.activation(out=CL[:], in_=CL[:], func=AF.Exp, scale=-1.0), reads=["CL"], writes=["CL"])
                P.op("dve", lambda e: e.tensor_scalar_add(out=CL[:], in0=CL[:], scalar1=1.0), reads=["CL"], writes=["CL"])
                P.op("act", lambda e: e.activation(out=CL[:], in_=CL[:], func=AF.Ln), reads=["CL"], writes=["CL"])
                P.op("dve", lambda e: e.tensor_scalar_mul(out=CLH[:], in0=CL[:], scalar1=-4.0), reads=["CL"], writes=["CLH"])
                P.op("dve", lambda e: e.tensor_scalar_mul(out=CL[:], in0=CL[:], scalar1=-8.0), reads=["CL", "CLH"], writes=["CL"])
                P.op("dve", lambda e: e.tensor_scalar_mul(out=LBA[:], in0=LBA[:], scalar1=0.5), reads=["LBA"], writes=["LBA"])
                P.op("dve", lambda e: e.tensor_scalar_mul(out=LBX[:], in0=LBX[:], scalar1=0.5), reads=["LBX"], writes=["LBX"])
                segs = ((0, 0, CTX), (CTX + 4, CTX, SEQ))
                tiles5 = [(0, CTX)] + [(CTX + tt * 512, 512) for tt in range(4)]
                gst = {"n": 0}
                GK = 0.7978845608028654
                for j in range(4):
                    sx, sg_ = loadw(12 + j), loadw(16 + j)
                    for (xb, cb_, n) in segs:
                        P.op("pool", lambda e, xb=xb: e.memset(XRT[:, xb:xb + 2], 0.0), writes=["XRT"])
                        P.op("pool", lambda e, xb=xb, n=n: e.memset(XRT[:, xb + 2 + n: xb + 4 + n], 0.0), writes=["XRT"])
                    proj(sx, 0, CTX, lambda pp, bank: P.op(
                        "dve", lambda e, pp=pp: e.tensor_copy(out=XRT[:, 2:2 + CTX], in_=pp[:, 0:CTX]), reads=[("ps", bank)], writes=["XRT"]))
                    for tt in range(4):
                        proj(sx, CTX + tt * 512, 512, lambda pp, bank, tt=tt: P.op(
                            "dve", lambda e, pp=pp, tt=tt: e.tensor_copy(out=XRT[:, CTX + 6 + tt * 512: CTX + 6 + (tt + 1) * 512], in_=pp[:, :]),
                            reads=[("ps", bank)], writes=["XRT"]))
                    for tt in range(4):
                        def gevac(pp, bank, tt=tt):
                            b = gst["n"] % 2
                            gst["n"] += 1
                            P.op("act", lambda e, pp=pp, b=b: e.activation(out=TR[b][:, :], in_=pp[:, :], func=AF.Square), reads=[("ps", bank)], writes=[("TR", b)])
                            P.op("dve", lambda e, b=b: e.tensor_scalar(out=TR[b][:, :], in0=TR[b][:, :], scalar1=0.044715, scalar2=1.0, op0=ALU.mult, op1=ALU.add),
                                 reads=[("TR", b)], writes=[("TR", b)])
                            P.op("dve", lambda e, pp=pp, b=b: e.tensor_tensor(out=TR[b][:, :], in0=TR[b][:, :], in1=pp[:, :], op=ALU.mult),
                                 reads=[("TR", b), ("ps", bank)], writes=[("TR", b)])
                            P.op("act", lambda e, b=b: e.activation(out=TR[b][:, :], in_=TR[b][:, :], func=AF.Tanh, scale=GK), reads=[("TR", b)], writes=[("TR", b)])
                            P.op("dve", lambda e, pp=pp, b=b, tt=tt: e.scalar_tensor_tensor(out=GG[:, tt * 512:(tt + 1) * 512], in0=TR[b][:, :], scalar=1.0, in1=pp[:, :],
                                                                                           op0=ALU.add, op1=ALU.mult),
                                 reads=[("TR", b), ("ps", bank)], writes=["GG"])
                        proj(sg_, CTX + tt * 512, 512, gevac)
                    for (xb, cb_, n) in segs:
                        P.op("act", lambda e, xb=xb, cb_=cb_, n=n, j=j: e.activation(
                            out=XC[:, cb_:cb_ + n], in_=XRT[:, xb:xb + n], func=AF.Identity, scale=LCW[:, j * 4:j * 4 + 1], bias=LCB[:, j:j + 1]),
                            reads=["XRT", "LCW", "LCB"], writes=["XC"])
                        for tap in (1, 2, 3):
                            P.op("dve", lambda e, xb=xb, cb_=cb_, n=n, j=j, tap=tap: e.scalar_tensor_tensor(
                                out=XC[:, cb_:cb_ + n], in0=XRT[:, xb + tap: xb + tap + n], scalar=LCW[:, j * 4 + tap: j * 4 + tap + 1],
                                in1=XC[:, cb_:cb_ + n], op0=ALU.mult, op1=ALU.add), reads=["XRT", "LCW", "XC"], writes=["XC"])
                    P.op("act", lambda e: e.activation(out=XCB[:, :], in_=XC[:, :], func=AF.Copy), reads=["XC"], writes=["XCB"])
                    for d in range(2):
                        col = d * 4 + j
                        for (c0, n) in tiles5:
                            b = gst["n"] % 2
                            gst["n"] += 1
                            ba, bx = 2 + b, 4 + b
                            for (bank, kind) in ((ba, 0), (bx, 1)):
                                P.group("pe", [lambda e, bank=bank, kind=kind, col=col, c0=c0, n=n: e.matmul(
                                    self.PS[bank][:, 0:n], BD[:, (kind * 8 + col) * 128:(kind * 8 + col + 1) * 128], XCB[:, c0:c0 + n], start=True, stop=True)],
                                    reads=["BD", "XCB"], writes=[("ps", bank)])
                            o0 = c0 if d == 0 else (c0 - CTX if c0 >= CTX else SEQ)
                            P.op("act", lambda e, ba=ba, b=b, n=n, col=col: e.activation(out=TR[b][:, 0:n], in_=self.PS[ba][:, 0:n], func=AF.Tanh, scale=0.5, bias=LBA[:, col:col + 1]),
                                 reads=[("ps", ba), "LBA"], writes=[("TR", b)])
                            P.op("act", lambda e, bx=bx, n=n, col=col, o0=o0: e.activation(out=T2[:, o0:o0 + n], in_=self.PS[bx][:, 0:n], func=AF.Tanh, scale=0.5, bias=LBX[:, col:col + 1]),
                                 reads=[("ps", bx), "LBX"], writes=["T2"])
                            P.op("act", lambda e, b=b, n=n, col=col, o0=o0: e.activation(out=A[:, o0:o0 + n], in_=TR[b][:, 0:n], func=AF.Exp, scale=CLH[:, col:col + 1], bias=CLH[:, col:col + 1]),
                                 reads=[("TR", b), "CLH"], writes=["A"])
                            P.op("act", lambda e, b=b, n=n, col=col, o0=o0: e.activation(out=B[:, o0:o0 + n], in_=TR[b][:, 0:n], func=AF.Exp, scale=CL[:, col:col + 1], bias=CL[:, col:col + 1]),
                                 reads=[("TR", b), "CL"], writes=["B"])
                        P.op("act", lambda e: e.activation(out=B[:, :], in_=B[:, :], func=AF.Sqrt, scale=-0.25, bias=QRT[:, 0:1]), reads=["B", "QRT"], writes=["B"])
                        P.op("dve", lambda e: e.scalar_tensor_tensor(out=B[:, :], in0=T2[:, :], scalar=1.0, in1=B[:, :], op0=ALU.add, op1=ALU.mult),
                             reads=["T2", "B"], writes=["B"])
                        if d == 0:
                            P.op("dve", lambda e: e.tensor_tensor(out=B[:, :], in0=B[:, :], in1=XC[:, :], op=ALU.mult), reads=["B", "XC"], writes=["B"])
                            P.op("dve", lambda e: e.tensor_tensor_scan(out=XRT[:, 0:ZT], data0=A[:, 0:ZT], data1=B[:, 0:ZT], initial=0.0,
                                                                       op0=ALU.mult, op1=ALU.add), reads=["A", "B"], writes=["XRT"])
                        else:
                            P.op("dve", lambda e: e.tensor_tensor(out=B[:, 0:SEQ], in0=B[:, 0:SEQ], in1=XC[:, CTX:ZT], op=ALU.mult), reads=["B", "XC"], writes=["B"])
                            P.op("dve", lambda e: e.tensor_tensor(out=B[:, SEQ:ZT], in0=B[:, SEQ:ZT], in1=XC[:, 0:CTX], op=ALU.mult), reads=["B", "XC"], writes=["B"])
                            P.op("dve", lambda e: e.tensor_tensor_scan(out=XC[:, 0:ZT][:, ::-1], data0=A[:, 0:ZT][:, ::-1], data1=B[:, 0:ZT][:, ::-1],
                                                                       initial=0.0, op0=ALU.mult, op1=ALU.add), reads=["A", "B"], writes=["XC"])
                    P.op("dve", lambda e: e.tensor_tensor(out=A[:, 0:SEQ], in0=XRT[:, CTX:ZT], in1=XC[:, 0:SEQ], op=ALU.add),
                         reads=["XRT", "XC", "A"], writes=["A"], strict=True)
                    P.op("dve", lambda e, j=j: e.scalar_tensor_tensor(out=F[:, j * SEQ:(j + 1) * SEQ], in0=A[:, 0:SEQ], scalar=0.5, in1=GG[:, :],
                                                                      op0=ALU.mult, op1=ALU.mult),
                         reads=["A", "GG"], writes=[("F", j)])
                P.barrier()
            if self.dump_ctx:
                P.dma("sp", self.s_ioc, lambda e: e.dma_start(out=self.dbg[128:256, :], in_=F[:]), reads=[("F", i) for i in range(4)], writes=["dbg1"])
                P.barrier()
            with ExitStack() as sln:
                tl = self.ln_tiles_alloc(sln, 4)
                halfproj(1, tl)
                for tt in range(4):
                    self.ln_tile(tl, "h", l, k, w, tt * 512, 512, ti=tt, pre=True)
                P.barrier()


def prep_shared(inp):
    f = lambda a: np.ascontiguousarray(np.asarray(a, dtype=np.float32))
    sh = {}
    sh["modw"] = np.concatenate([lhsT_layout(f(inp["mod_w"][l])) for l in range(2)], axis=0)
    mb = np.concatenate([vec_pm(f(inp["mod_b"][l])) for l in range(2)], axis=1)
    sh["modb3"] = np.ascontiguousarray(np.repeat(mb, 3, axis=1))
    sh["lng"] = np.concatenate([vec_pm(f(inp["ln_g"][l, k])) for l in range(2) for k in range(3)], axis=1)
    sh["lnb"] = np.concatenate([vec_pm(f(inp["ln_b"][l, k])) for l in range(2) for k in range(3)], axis=1)
    sh["w1"] = np.concatenate([lhsT_layout(f(inp["ffn_w1"][l, j])) for l in range(2) for j in range(2)], axis=0)
    sh["w3"] = np.concatenate([lhsT_layout(f(inp["ffn_w3"][l, j])) for l in range(2) for j in range(2)], axis=0)
    sh["w2"] = np.concatenate([lhsT_layout(f(inp["ffn_w2"][l, j])) for l in range(2) for j in range(2)], axis=0)
    sh["win0"] = lhsT_layout(f(inp["mix0_w_in"][0]))
    sh["wout0"] = lhsT_layout(f(inp["mix0_w_out"][0]))
    sh["win1"] = lhsT_layout(f(inp["mix1_w_in"][0]))
    sh["wout1"] = lhsT_layout(f(inp["mix1_w_out"][0]))
    scw = f(inp["sconv_w"][0])
    sh["scw"] = np.ascontiguousarray(np.stack([vec_pm(scw[t]) for t in range(3)], axis=2).reshape(128, 24))
    rpb = f(inp["na_rpb"][0])
    j = np.arange(64)[:, None]
    kc = np.arange(64)[None, :]
    ci = np.clip(kc - j + 15, 0, 30)
    g = rpb[:, :, ci]
    g = g.transpose(0, 2, 1, 3)
    rp = np.zeros((128, 4 * 15 * 64), np.float32)
    for h in range(8):
        half, idx = h % 2, h // 2
        rp[half * 64:(half + 1) * 64, idx * 960:(idx + 1) * 960] = g[h].reshape(64, 960)
    sh["rpbg"] = rp
    start = np.clip(j - 8, 0, 48)
    valid = (kc >= start) & (kc < start + 16)
    nm = np.where(valid, 0.0, -30000.0).astype(np.float32)
    sh["nmask"] = np.ascontiguousarray(np.tile(np.concatenate([nm, nm], axis=0), (1, 60)))
    eye = np.eye(64, dtype=np.float32)
    sh["id2"] = np.ascontiguousarray(np.concatenate([eye, eye], axis=0))
    cw = f(inp["lru_conv_w"][0])
    sh["lcw"] = np.ascontiguousarray(np.stack([vec_pm(cw[t]) for t in range(4)], axis=2).reshape(128, 16))
    sh["lcb"] = vec_pm(f(inp["lru_conv_b"][0]))
    def bd(wm):
        o = np.zeros((2, 4, 128, 128), np.float32)
        for d in range(2):
            for n in range(8):
                c, hh = n // 2, n % 2
                o[d, c, hh * 64:(hh + 1) * 64, hh * 64:(hh + 1) * 64] = wm[d, n]
        return o.reshape(2 * 4 * 128, 128)
    sh["lwa"] = bd(f(inp["lru_w_a"][0]))
    sh["lwx"] = bd(f(inp["lru_w_x"][0]))
    sh["lba"] = np.concatenate([vec_pm(f(inp["lru_b_a"][0, d])) for d in range(2)], axis=1)
    sh["lbx"] = np.concatenate([vec_pm(f(inp["lru_b_x"][0, d])) for d in range(2)], axis=1)
    sh["llam"] = np.concatenate([vec_pm(f(inp["lru_lambda"][0, d])) for d in range(2)], axis=1)
    return sh


def prep_core(inp, bidx):
    f = lambda a: np.asarray(a, dtype=np.float32)
    m = {}
    m["xT"] = np.concatenate([to_pm(f(inp["x"][b])) for b in bidx], axis=0)
    m["cxT"] = np.concatenate([to_pm(f(inp["ctx"][b])) for b in bidx], axis=0)
    cols = [f(inp["c"][b]) for b in bidx]
    while len(cols) < 2:
        cols.append(cols[0])
    cols.append(f(inp["c_ctx"]))
    cm = np.stack([vec_pm(c) for c in cols], axis=2)
    m["cond"] = np.ascontiguousarray(cm.reshape(128, 24))
    return m


_CACHE = {}


def get_program(NS, phases=ALL_PHASES, dump_ctx=False):
    key = (NS, tuple(phases), dump_ctx)
    if key not in _CACHE:
        _CACHE[key] = Builder(NS, phases, dump_ctx).build()
    return _CACHE[key]


def kernel(**inputs):
    B = inputs["x"].shape[0]
    NS = B // NCORES
    nc = get_program(NS)
    sh = prep_shared(inputs)
    in_maps = []
    for c in range(NCORES):
        m = dict(sh)
        m.update(prep_core(inputs, list(range(c * NS, (c + 1) * NS))))
        in_maps.append(m)
    res = run_bass_kernel_spmd(nc, in_maps, core_ids=list(range(NCORES)))
    out = np.empty((B, SEQ, D), np.float32)
    for c in range(NCORES):
        o = res.results[c]["out"]
        for i in range(NS):
            out[c * NS + i] = from_pm(o[i * 128:(i + 1) * 128], SEQ)
    return out
```

```python
import numpy as np
from contextlib import ExitStack
import concourse.bass as bass
import concourse.mybir as mybir
from concourse.bass_utils import run_bass_kernel_spmd

F32 = mybir.dt.float32
BF16 = mybir.dt.bfloat16
AF = mybir.ActivationFunctionType
ALU = mybir.AluOpType

D = 1024
SEQ = 2048
CTX = 256
DFF = 2816
NFC = 22
GRID_W = 64
ALPHA = 4.0 ** 0.25
LN_EPS = 1e-5 / (ALPHA * ALPHA)
NCORES = 8
ENGS = ("pe", "act", "dve", "pool", "sp")


class Stream:
    def __init__(self, sem):
        self.sem = sem
        self.count = 0


class Prog:
    def __init__(self, nc, stack):
        self.nc = nc
        self.stack = stack
        self.q = {e: [] for e in ENGS}
        self.cnt = {e: 0 for e in ENGS}
        self.esem = {e: stack.enter_context(nc.semaphore("es_" + e)) for e in ENGS if e != "sp"}
        self.waited = {}
        self.lastw = {}
        self.readers = {}
        self.streams = []
        self.scount = {}
        self.n_ops = 0

    def stream(self, name=None):
        s = Stream(self.stack.enter_context(self.nc.semaphore(name or ("ds%d" % len(self.streams)))))
        self.streams.append(s)
        self.scount["s_%d" % id(s)] = (lambda s=s: s.count)
        return s

    def _need(self, eng, tok, waits):
        if tok is None:
            return
        sid, sem, val, teng = tok
        if teng == eng and eng == "pe":
            return
        if teng is None:
            val = max(val, self.scount[sid]())
        k = (eng, sid)
        if self.waited.get(k, 0) >= val:
            return
        self.waited[k] = val
        waits.append((sem, val))

    def _deps(self, eng, reads, writes):
        waits = []
        for k in reads:
            self._need(eng, self.lastw.get(k), waits)
        for k in writes:
            self._need(eng, self.lastw.get(k), waits)
            for t in self.readers.get(k, ()):
                self._need(eng, t, waits)
        best = {}
        for sem, val in waits:
            if id(sem) not in best or best[id(sem)][1] < val:
                best[id(sem)] = (sem, val)
        return list(best.values())

    def _commit(self, tok, reads, writes):
        for k in reads:
            self.readers.setdefault(k, []).append(tok)
        for k in writes:
            self.lastw[k] = tok
            self.readers[k] = []

    def op(self, eng, fn, reads=(), writes=(), strict=False):
        self.group(eng, [fn], reads, writes, strict)

    def group(self, eng, fns, reads=(), writes=(), strict=False):
        waits = self._deps(eng + "_strict" if strict else eng, reads, writes)
        self.cnt[eng] += 1
        tok = ("e_" + eng, self.esem[eng], self.cnt[eng], eng)
        n = len(fns)
        for i, fn in enumerate(fns):
            self.q[eng].append((waits if i == 0 else [], fn, (self.esem[eng], 1) if i == n - 1 else None))
        self._commit(tok, reads, writes)
        self.n_ops += n

    def dma(self, eng, stream, fn, reads=(), writes=()):
        waits = self._deps(eng + "_q", reads, writes)
        stream.count += 16
        tok = ("s_%d" % id(stream), stream.sem, stream.count, None)
        self.q[eng].append((waits, fn, (stream.sem, 16)))
        self._commit(tok, reads, writes)
        self.n_ops += 1

    def barrier(self):
        toks = [("e_" + e, self.esem[e], self.cnt[e], e) for e in self.esem if self.cnt[e] > 0]
        toks += [("s_%d" % id(s), s.sem, s.count, None) for s in self.streams if s.count > 0]
        for e, ident in (("pe", "pe"), ("act", "act"), ("dve", "dve"), ("pool", "pool"), ("pool", "pool_q"),
                         ("sp", "sp_q"), ("act", "act_q")):
            waits = []
            for t in toks:
                self._need(ident, t, waits)
            if waits:
                self.q[e].append((waits, None, None))
        self.lastw = {}
        self.readers = {}

    def emit(self):
        nc = self.nc
        with nc.Block() as block:
            def run(engname):
                def body(e):
                    for waits, fn, inc in self.q[engname]:
                        for sem, val in waits:
                            e.wait_ge(sem, val)
                        if fn is not None:
                            ins = fn(e)
                            if inc is not None:
                                ins.then_inc(inc[0], inc[1])
                return body
            block.sync(run("sp"))
            block.tensor(run("pe"))
            block.scalar(run("act"))
            block.vector(run("dve"))
            block.gpsimd(run("pool"))


def to_pm(a):
    T, Dd = a.shape
    nch = Dd // 128
    return np.ascontiguousarray(a.T.reshape(nch, 128, T).transpose(1, 0, 2).reshape(128, nch * T))


def from_pm(o, T):
    nch = o.shape[1] // T
    return np.ascontiguousarray(o.reshape(128, nch, T).transpose(2, 1, 0).reshape(T, nch * 128))


def lhsT_layout(W):
    K, N = W.shape
    kc, ncc = K // 128, N // 128
    return np.ascontiguousarray(W.reshape(kc, 128, ncc, 128).transpose(2, 1, 0, 3).reshape(ncc * 128, K))


def vec_pm(v):
    return np.ascontiguousarray(v.reshape(-1, 128).T)


ALL_PHASES = ("L0F0", "L0MIX", "L0F1", "L1F0", "L1MIX", "L1F1")


class Builder:
    def __init__(self, NS=2, phases=ALL_PHASES, dump_ctx=False):
        self.NS = NS
        self.phases = phases
        self.dump_ctx = dump_ctx
        self.nc = bass.Bass("TRN2", target_bir_lowering=False)
        self.conv_done = set()
        self.scr_keys = {}
        self.conv_streams = {}
        self._gs = {}

    def dram(self, name, shape, dt, kind):
        return self.nc.dram_tensor(name, shape, dt, kind=kind).ap()

    def gs(self, name):
        if name not in self._gs:
            self._gs[name] = self.P.stream("gs_" + name)
        return self._gs[name]

    def sb(self, st, name, shape, dt):
        self.ntile = getattr(self, "ntile", 0) + 1
        return st.enter_context(self.nc.sbuf_tensor("%s_%d" % (name, self.ntile), shape, dt))

    def declare(self):
        NS = self.NS
        di = lambda n, s: self.dram(n, s, F32, "ExternalInput")
        self.xT = di("xT", [NS * 128, 8 * SEQ])
        self.cxT = di("cxT", [NS * 128, 8 * CTX])
        self.cond = di("cond", [128, 24])
        self.modw = di("modw", [2 * 72 * 128, 1024])
        self.modb3 = di("modb3", [128, 2 * 72 * 3])
        self.lng = di("lng", [128, 48])
        self.lnb = di("lnb", [128, 48])
        self.w1 = di("w1", [4 * NFC * 128, 1024])
        self.w3 = di("w3", [4 * NFC * 128, 1024])
        self.w2 = di("w2", [4 * 8 * 128, DFF])
        self.win0 = di("win0", [20 * 128, 1024])
        self.wout0 = di("wout0", [8 * 128, 1024])
        self.win1 = di("win1", [24 * 128, 1024])
        self.wout1 = di("wout1", [8 * 128, 1024])
        self.scw = di("scw", [128, 24])
        self.rpbg = di("rpbg", [128, 4 * 15 * 64])
        self.nmask = di("nmask", [128, 3840])
        self.id2 = di("id2", [128, 64])
        self.lcw = di("lcw", [128, 16])
        self.lcb = di("lcb", [128, 4])
        self.lwa = di("lwa", [2 * 4 * 128, 128])
        self.lwx = di("lwx", [2 * 4 * 128, 128])
        self.lba = di("lba", [128, 8])
        self.lbx = di("lbx", [128, 8])
        self.llam = di("llam", [128, 8])
        self.out = self.dram("out", [NS * 128, 8 * SEQ], F32, "ExternalOutput")
        if self.dump_ctx:
            self.outc = self.dram("outc", [NS * 128, 8 * CTX], F32, "ExternalOutput")
            self.dbg = self.dram("dbg", [256, 4 * SEQ], BF16, "ExternalOutput")
            self.dbg2 = self.dram("dbg2", [128, 5 * (CTX + SEQ)], F32, "ExternalOutput")
            self.dbg3 = self.dram("dbg3", [128, 8 * SEQ], F32, "ExternalOutput")
            self.dbg4 = self.dram("dbg4", [128, 8 * CTX], F32, "ExternalOutput")
        ds = lambda n, s: self.dram(n, s, BF16, "Internal")
        self.w1s = ds("w1s", [4 * NFC * 128, 1024])
        self.w3s = ds("w3s", [4 * NFC * 128, 1024])
        self.w2s = ds("w2s", [4 * 8 * 128, DFF])
        self.win0s = ds("win0s", [20 * 128, 1024])
        self.wout0s = ds("wout0s", [8 * 128, 1024])
        self.win1s = ds("win1s", [24 * 128, 1024])
        self.wout1s = ds("wout1s", [8 * 128, 1024])

    def convert(self, name, src, dst, r0, r1, step):
        P = self.P
        for a in range(r0, r1, step):
            b = min(a + step, r1)
            key = ("cv", name, a)
            if key in self.conv_done:
                continue
            self.conv_done.add(key)
            self.scr_keys.setdefault(name, []).append(("scr", name, a))
            if name not in self.conv_streams:
                self.conv_streams[name] = self.gs("cv_" + name)
            P.dma("pool", self.conv_streams[name], lambda e, a=a, b=b: e.dma_start(out=dst[a:b, :], in_=src[a:b, :]),
                  writes=[("scr", name, a)])

    def convert_for(self, phase):
        if phase in ("L0F0", "L0F1", "L1F0", "L1F1"):
            q = {"L0F0": 0, "L0F1": 1, "L1F0": 2, "L1F1": 3}[phase]
            self.convert("w1_%d" % q, self.w1, self.w1s, q * NFC * 128, (q + 1) * NFC * 128, 704)
            self.convert("w3_%d" % q, self.w3, self.w3s, q * NFC * 128, (q + 1) * NFC * 128, 704)
            self.convert("w2_%d" % q, self.w2, self.w2s, q * 1024, (q + 1) * 1024, 256)
        elif phase == "L0MIX":
            self.convert("win0", self.win0, self.win0s, 0, 20 * 128, 640)
            self.convert("wout0", self.wout0, self.wout0s, 0, 1024, 512)
        elif phase == "L1MIX":
            self.convert("win1", self.win1, self.win1s, 0, 24 * 128, 768)
            self.convert("wout1", self.wout1, self.wout1s, 0, 1024, 512)

    def mv(self, l, n, dc, w):
        c = ((l * 72 + n * 8 + dc) * 3 + w)
        return self.MV[:, c:c + 1]

    def lnv(self, t, l, k, dc):
        c = (l * 3 + k) * 8 + dc
        return t[:, c:c + 1]

    def build(self):
        nc = self.nc
        self.declare()
        with ExitStack() as st:
            self.st = st
            P = self.P = Prog(nc, st)
            self.s_conv = P.stream("s_conv")
            self.s_const = P.stream("s_const")
            self.s_io = P.stream("s_io")
            self.s_ioc = P.stream("s_ioc")
            self.H = self.sb(st, "H", [128, 8 * SEQ], F32)
            self.HC = self.sb(st, "HC", [128, 8 * CTX], F32)
            self.MV = self.sb(st, "MV", [128, 2 * 72 * 3], F32)
            self.LNG = self.sb(st, "LNG", [128, 48], F32)
            self.LNB = self.sb(st, "LNB", [128, 48], F32)
            self.ONES = self.sb(st, "ONES", [128, 128], F32)
            self.PS = [st.enter_context(nc.psum_tensor("ps%d" % i, [128, 512], F32)) for i in range(8)]
            self.prologue()
            for s in range(self.NS):
                self.sequence(s)
            P.barrier()
            P.emit()
        return nc

    def prologue(self):
        P, nc = self.P, self.nc
        first = [p for p in ALL_PHASES if p in self.phases][0]
        P.dma("sp", self.s_const, lambda e: e.dma_start(out=self.LNG[:], in_=self.lng), writes=["LNG"])
        P.dma("sp", self.s_const, lambda e: e.dma_start(out=self.LNB[:], in_=self.lnb), writes=["LNB"])
        P.op("pool", lambda e: e.memset(self.ONES[:], 1.0), writes=["ONES"])
        with ExitStack() as sc:
            CF = self.sb(sc, "CF", [128, 24], F32)
            CS = self.sb(sc, "CS", [128, 24], BF16)
            MB = self.sb(sc, "MB", [128, 2 * 72 * 3], F32)
            G = 8
            MW = [self.sb(sc, "MW%d" % i, [128, G * 1024], BF16) for i in range(2)]
            s_mw = [P.stream("s_mw%d" % i) for i in range(2)]
            P.dma("sp", self.s_const, lambda e: e.dma_start(out=CF[:], in_=self.cond), writes=["CF"])
            P.dma("sp", self.s_const, lambda e: e.dma_start(out=MB[:], in_=self.modb3), writes=["MB"])
            P.op("act", lambda e: e.activation(out=CS[:], in_=CF[:], func=AF.Silu), reads=["CF"], writes=["CS"])
            ngrp = 2 * 72 // G
            for g in range(ngrp):
                if g == 3:
                    self.convert_for(first)
                slot = g % 2
                r0 = g * G * 128
                P.dma("pool", s_mw[slot],
                      lambda e, slot=slot, r0=r0: e.dma_start(
                          out=MW[slot][:].rearrange("p (g c) -> p g c", g=G),
                          in_=self.modw[r0:r0 + G * 128, :].rearrange("(g p) c -> p g c", p=128)),
                      writes=[("MW", slot)])
                ps = self.PS[slot]
                fns = []
                for jl in range(G):
                    for k in range(8):
                        fns.append(lambda e, slot=slot, jl=jl, k=k, ps=ps: e.matmul(
                            ps[:, jl * 3:jl * 3 + 3], MW[slot][:, jl * 1024 + k * 128: jl * 1024 + (k + 1) * 128],
                            CS[:, k * 3:k * 3 + 3], start=(k == 0), stop=(k == 7)))
                P.group("pe", fns, reads=[("MW", slot), "CS"], writes=[("ps", slot)])
                c0 = g * G * 3
                P.op("dve", lambda e, ps=ps, c0=c0: e.tensor_tensor(
                    out=self.MV[:, c0:c0 + G * 3], in0=ps[:, 0:G * 3], in1=MB[:, c0:c0 + G * 3], op=ALU.add),
                    reads=[("ps", slot), "MB"], writes=["MV"])
            for l in range(2):
                for k in range(3):
                    c0 = (l * 72 + (3 * k + 1) * 8) * 3
                    P.op("dve", lambda e, c0=c0: e.tensor_scalar_add(out=self.MV[:, c0:c0 + 24], in0=self.MV[:, c0:c0 + 24], scalar1=1.0),
                         reads=["MV"], writes=["MV"])
                    c1 = (l * 72 + (3 * k + 2) * 8) * 3
                    rw = (1.0 if k == 1 else 0.5) / ALPHA
                    P.op("dve", lambda e, c1=c1, rw=rw: e.tensor_scalar_mul(out=self.MV[:, c1:c1 + 24], in0=self.MV[:, c1:c1 + 24], scalar1=rw),
                         reads=["MV"], writes=["MV"])
            P.barrier()

    def sequence(self, s):
        P = self.P
        ph = [p for p in ALL_PHASES if p in self.phases]
        for dc in range(8):
            P.dma("sp", self.s_io, lambda e, dc=dc: e.dma_start(out=self.H[:, dc * SEQ:(dc + 1) * SEQ],
                                                              in_=self.xT[s * 128:(s + 1) * 128, dc * SEQ:(dc + 1) * SEQ]),
                  writes=[("h", dc, t) for t in range(4)])
        P.dma("sp", self.s_ioc, lambda e: e.dma_start(out=self.HC[:], in_=self.cxT[s * 128:(s + 1) * 128, :]),
              writes=[("hc", dc) for dc in range(8)])
        units = []
        for p in ph:
            if p in ("L0F0", "L0F1", "L1F0", "L1F1") and units and units[-1][0] in ("L0F1",) and p == "L1F0":
                units[-1].append(p)
            else:
                units.append([p])
        for i, u in enumerate(units):
            if s == 0 and i + 1 < len(units):
                for p in units[i + 1]:
                    self.convert_for(p)
            if u[0] in ("L0F0", "L0F1", "L1F0", "L1F1"):
                self.ffn_phase(s, [(int(p[1]), int(p[3]), p == "L0F0") for p in u])
            elif u[0] == "L1MIX":
                self.l1mix_phase(s)
            elif u[0] == "L0MIX":
                self.l0mix_phase(s)
        for dc in range(8):
            P.dma("sp", self.s_io, lambda e, dc=dc: e.dma_start(out=self.out[s * 128:(s + 1) * 128, dc * SEQ:(dc + 1) * SEQ],
                                                              in_=self.H[:, dc * SEQ:(dc + 1) * SEQ]),
                  reads=[("h", dc, t) for t in range(4)], writes=[("out", s, dc)])
        if self.dump_ctx:
            P.dma("sp", self.s_ioc, lambda e: e.dma_start(out=self.outc[s * 128:(s + 1) * 128, :], in_=self.HC[:]),
                  reads=[("hc", dc) for dc in range(8)], writes=[("outc", s)])

    def hview(self, stream, dc, t0, n):
        if stream == "h":
            return self.H[:, dc * SEQ + t0: dc * SEQ + t0 + n]
        return self.HC[:, dc * CTX + t0: dc * CTX + t0 + n]

    def hkeys(self, stream, dc, t0, n):
        if stream == "h":
            return [("h", dc, t) for t in range(t0 // 512, (t0 + n + 511) // 512)]
        return [("hc", dc)]

    def stats_accum(self, tl, ti, stream, dc, t0, n):
        P = self.P
        SQ, SS, QQ = tl["SQ"], tl["SS"][ti], tl["QQ"][ti]
        hv = self.hview(stream, dc, t0, n)
        hk = self.hkeys(stream, dc, t0, n)
        b = dc % 2
        if dc == 0:
            P.op("act", lambda e: e.activation(out=QQ[:, 0:n], in_=hv, func=AF.Square), reads=hk, writes=[("QQ", ti)])
            return
        P.op("act", lambda e: e.activation(out=SQ[b][:, 0:n], in_=hv, func=AF.Square), reads=hk, writes=[("SQ", b)])
        P.op("dve", lambda e: e.tensor_tensor(out=QQ[:, 0:n], in0=QQ[:, 0:n], in1=SQ[b][:, 0:n], op=ALU.add),
             reads=[("QQ", ti), ("SQ", b)], writes=[("QQ", ti)])
        if dc == 1:
            hv0 = self.hview(stream, 0, t0, n)
            P.op("pool", lambda e: e.tensor_tensor(out=SS[:, 0:n], in0=hv0, in1=hv, op=ALU.add),
                 reads=hk + self.hkeys(stream, 0, t0, n), writes=[("SS", ti)])
        else:
            P.op("pool", lambda e: e.tensor_tensor(out=SS[:, 0:n], in0=SS[:, 0:n], in1=hv, op=ALU.add),
                 reads=hk + [("SS", ti)], writes=[("SS", ti)])

    def ln_tile(self, tl, stream, l, k, w, t0, n, ti=0, pre=False):
        P = self.P
        T1, MEAN, MSQ, VAR, RSTD, EPS = tl["T1"], tl["MEAN"], tl["MSQ"], tl["VAR"], tl["RSTD"], tl["EPS"]
        SS, QQ = tl["SS"][ti], tl["QQ"][ti]
        psS, psQ = self.PS[6], self.PS[7]
        if not pre:
            for dc in range(8):
                self.stats_accum(tl, ti, stream, dc, t0, n)
        P.group("pe", [lambda e: e.matmul(psS[:, 0:n], self.ONES[:], SS[:, 0:n], start=True, stop=True)],
                reads=[("SS", ti), "ONES"], writes=[("ps", 6)])
        P.group("pe", [lambda e: e.matmul(psQ[:, 0:n], self.ONES[:], QQ[:, 0:n], start=True, stop=True)],
                reads=[("QQ", ti), "ONES"], writes=[("ps", 7)])
        P.op("dve", lambda e: e.tensor_scalar_mul(out=MEAN[:, 0:n], in0=psS[:, 0:n], scalar1=1.0 / D),
             reads=[("ps", 6)], writes=["MEAN"])
        P.op("dve", lambda e: e.tensor_tensor(out=MSQ[:, 0:n], in0=MEAN[:, 0:n], in1=MEAN[:, 0:n], op=ALU.mult),
             reads=["MEAN"], writes=["MSQ"])
        P.op("dve", lambda e: e.scalar_tensor_tensor(out=VAR[:, 0:n], in0=psQ[:, 0:n], scalar=1.0 / D, in1=MSQ[:, 0:n],
                                                     op0=ALU.mult, op1=ALU.subtract),
             reads=[("ps", 7), "MSQ"], writes=["VAR"])
        P.op("act", lambda e: e.activation(out=VAR[:, 0:n], in_=VAR[:, 0:n], func=AF.Sqrt, bias=EPS[:, 0:1]),
             reads=["VAR", "EPS"], writes=["VAR"])
        P.op("dve", lambda e: e.reciprocal(out=RSTD[:, 0:n], in_=VAR[:, 0:n]), reads=["VAR"], writes=["RSTD"])
        for dc in range(8):
            hv = self.hview(stream, dc, t0, n)
            hk = self.hkeys(stream, dc, t0, n)
            b = dc % 4
            e1 = "pool" if dc % 2 == 0 else "dve"
            P.op(e1, lambda e, hv=hv, b=b: e.tensor_tensor(out=T1[b][:, 0:n], in0=hv, in1=MEAN[:, 0:n], op=ALU.subtract),
                 reads=hk + ["MEAN"], writes=[("T1", b)])
            P.op("dve", lambda e, b=b: e.tensor_tensor(out=T1[b][:, 0:n], in0=T1[b][:, 0:n], in1=RSTD[:, 0:n], op=ALU.mult),
                 reads=[("T1", b), "RSTD"], writes=[("T1", b)])
            P.op("act", lambda e, hv=hv, b=b, dc=dc: e.activation(out=hv, in_=T1[b][:, 0:n], func=AF.Identity,
                                                                 scale=self.lnv(self.LNG, l, k, dc), bias=self.lnv(self.LNB, l, k, dc)),
                 reads=[("T1", b), "LNG", "LNB"], writes=hk)

    def ln_tiles_alloc(self, sc, ntiles=1):
        tl = {}
        tl["SQ"] = [self.sb(sc, "SQ%d" % i, [128, 512], F32) for i in range(2)]
        tl["T1"] = [self.sb(sc, "T1%d" % i, [128, 512], F32) for i in range(4)]
        for nm in ("MEAN", "MSQ", "VAR", "RSTD"):
            tl[nm] = self.sb(sc, nm, [128, 512], F32)
        tl["SS"] = [self.sb(sc, "SS%d" % i, [128, 512], F32) for i in range(ntiles)]
        tl["QQ"] = [self.sb(sc, "QQ%d" % i, [128, 512], F32) for i in range(ntiles)]
        EPS = tl["EPS"] = self.sb(sc, "EPS", [128, 1], F32)
        self.P.op("pool", lambda e: e.memset(EPS[:], LN_EPS), writes=["EPS"])
        return tl

    def ffn_phase(self, s, specs):
        P = self.P
        with ExitStack() as sc:
            TB = 1024
            Z = self.sb(sc, "Z", [128, 8 * TB], BF16)
            G = self.sb(sc, "G", [128, NFC * TB], BF16)
            W13 = [self.sb(sc, "W13_%d" % i, [128, 4 * 1024], BF16) for i in range(3)]
            W2 = [self.sb(sc, "W2_%d" % i, [128, DFF], BF16) for i in range(2)]
            SL = [self.sb(sc, "SL%d" % i, [128, 512], F32) for i in range(2)]
            tl = self.ln_tiles_alloc(sc, 2)
            s13 = [self.gs("w13_%d" % i) for i in range(len(W13))]
            s2 = [self.gs("w2_%d" % i) for i in range(len(W2))]
            st = {"g13": 0, "g2": 0, "it": 0, "itb": 0}
            blocks = []
            for (l, which, with_ctx) in specs:
                q = l * 2 + which
                k = 0 if which == 0 else 2
                if with_ctx:
                    blocks.append((l, q, k, "hc", 2, 0, CTX))
                blocks += [(l, q, k, "h", s, 0, TB), (l, q, k, "h", s, TB, TB)]
            ctxs = []
            for (l, q, k, stream, w, t0, tb) in blocks:
                ctxs.append(dict(l=l, q=q, k=k, stream=stream, w=(2 if stream == "hc" else s), t0=t0, tb=tb,
                                 Z=Z, G=G, W13=W13, W2=W2, SL=SL, tl=tl, s13=s13, s2=s2, st=st,
                                 tiles=[(a, min(512, tb - a)) for a in range(0, tb, 512)]))
            self.ffn_z(ctxs[0])
            for i, c in enumerate(ctxs):
                self.ffn_ab(c)
                if i + 1 < len(ctxs):
                    self.ffn_z(ctxs[i + 1])
                for ti, (a, n) in enumerate(c["tiles"]):
                    self.ln_tile(tl, c["stream"], c["l"], c["k"], c["w"], c["t0"] + a, n, ti=ti, pre=True)
            P.barrier()

    def ffn_z(self, c):
        P = self.P
        l, k, w, stream, t0, tb, Z = c["l"], c["k"], c["w"], c["stream"], c["t0"], c["tb"], c["Z"]
        for dc in range(8):
            hv = self.hview(stream, dc, t0, tb)
            hk = self.hkeys(stream, dc, t0, tb)
            zv = Z[:, dc * tb: (dc + 1) * tb]
            if dc % 2 == 0:
                P.op("act", lambda e, hv=hv, zv=zv, dc=dc: e.activation(out=zv, in_=hv, func=AF.Identity,
                                                                       scale=self.mv(l, 3 * k + 1, dc, w), bias=self.mv(l, 3 * k, dc, w)),
                     reads=hk + ["MV"], writes=[("Z", dc)])
            else:
                P.op("dve", lambda e, hv=hv, zv=zv, dc=dc: e.tensor_scalar(out=zv, in0=hv, scalar1=self.mv(l, 3 * k + 1, dc, w),
                                                                          scalar2=self.mv(l, 3 * k, dc, w), op0=ALU.mult, op1=ALU.add),
                     reads=hk + ["MV"], writes=[("Z", dc)])

    def ffn_ab(self, c):
        P = self.P
        l, q, k, w, stream, t0, tb = c["l"], c["q"], c["k"], c["w"], c["stream"], c["t0"], c["tb"]
        Z, G, W13, W2, SL, tl, s13, s2, st, tiles = c["Z"], c["G"], c["W13"], c["W2"], c["SL"], c["tl"], c["s13"], c["s2"], c["st"], c["tiles"]
        N13, N2 = len(W13), len(W2)

        def load13(fg):
            slot = st["g13"] % N13
            st["g13"] += 1
            r0 = (q * NFC + fg * 2) * 128
            for wi, src in enumerate((self.w1s, self.w3s)):
                P.dma("sp", s13[slot], lambda e, slot=slot, wi=wi, src=src, r0=r0: e.dma_start(
                    out=W13[slot][:, wi * 2048:(wi + 1) * 2048].rearrange("p (f c) -> p f c", f=2),
                    in_=src[r0:r0 + 256, :].rearrange("(f p) c -> p f c", p=128)),
                    reads=self.scr_keys["w1_%d" % q] + self.scr_keys["w3_%d" % q], writes=[("W13", slot)])
            return slot

        def load2(dc):
            slot = st["g2"] % N2
            st["g2"] += 1
            r0 = (q * 8 + dc) * 128
            P.dma("sp", s2[slot], lambda e, slot=slot, r0=r0: e.dma_start(out=W2[slot][:], in_=self.w2s[r0:r0 + 128, :]),
                  reads=self.scr_keys["w2_%d" % q], writes=[("W2", slot)])
            return slot

        nfg = NFC // 2
        slots13 = {0: load13(0), 1: load13(1)}
        slots2 = {}
        for fg in range(nfg):
            if fg + 2 < nfg:
                slots13[fg + 2] = load13(fg + 2)
            if fg == nfg - 2:
                slots2[0] = load2(0)
            if fg == nfg - 1:
                slots2[1] = load2(1)
            slot = slots13[fg]
            for fl in range(2):
                f = fg * 2 + fl
                for (a, n) in tiles:
                    b = st["it"] % 2
                    st["it"] += 1
                    p1, p3 = self.PS[b], self.PS[2 + b]
                    for wi, pp in ((0, p1), (1, p3)):
                        fns = []
                        for kk in range(8):
                            fns.append(lambda e, slot=slot, wi=wi, fl=fl, kk=kk, pp=pp, a=a, n=n: e.matmul(
                                pp[:, 0:n], W13[slot][:, wi * 2048 + fl * 1024 + kk * 128: wi * 2048 + fl * 1024 + (kk + 1) * 128],
                                Z[:, kk * tb + a: kk * tb + a + n], start=(kk == 0), stop=(kk == 7)))
                        P.group("pe", fns, reads=[("W13", slot)] + [("Z", kk) for kk in range(8)],
                                writes=[("ps", b if wi == 0 else 2 + b)])
                    P.op("act", lambda e, b=b, p1=p1, n=n: e.activation(out=SL[b][:, 0:n], in_=p1[:, 0:n], func=AF.Silu),
                         reads=[("ps", b)], writes=[("SL", b)])
                    P.op("dve", lambda e, b=b, p3=p3, f=f, a=a, n=n: e.tensor_tensor(
                        out=G[:, f * tb + a: f * tb + a + n], in0=SL[b][:, 0:n], in1=p3[:, 0:n], op=ALU.mult),
                        reads=[("SL", b), ("ps", 2 + b)], writes=[("G", f)])
        for dc in range(8):
            if dc + 1 < 8 and dc >= 1:
                slots2[dc + 1] = load2(dc + 1)
            slot = slots2[dc]
            for ti, (a, n) in enumerate(tiles):
                b = st["itb"] % 2
                st["itb"] += 1
                py = self.PS[4 + b]
                fns = []
                for f in range(NFC):
                    fns.append(lambda e, slot=slot, f=f, py=py, a=a, n=n: e.matmul(
                        py[:, 0:n], W2[slot][:, f * 128:(f + 1) * 128], G[:, f * tb + a: f * tb + a + n],
                        start=(f == 0), stop=(f == NFC - 1)))
                P.group("pe", fns, reads=[("W2", slot)] + [("G", f) for f in range(NFC)], writes=[("ps", 4 + b)])
                hv = self.hview(stream, dc, t0 + a, n)
                hk = self.hkeys(stream, dc, t0 + a, n)
                P.op("dve", lambda e, py=py, hv=hv, dc=dc, n=n: e.scalar_tensor_tensor(
                    out=hv, in0=py[:, 0:n], scalar=self.mv(l, 3 * k + 2, dc, w), in1=hv, op0=ALU.mult, op1=ALU.add),
                    reads=[("ps", 4 + b), "MV"] + hk, writes=hk)
                self.stats_accum(tl, ti, stream, dc, t0 + a, n)

    def l1mix_phase(self, s):
        P = self.P
        l, k, w = 1, 1, s
        with ExitStack() as sc:
            Z = self.sb(sc, "Zm", [128, 8 * SEQ], BF16)
            F = self.sb(sc, "Fm", [128, 8 * SEQ], BF16)
            sc2 = ExitStack()
            U = [self.sb(sc2, "U%d" % i, [128, SEQ + 2], F32) for i in range(2)]
            GB = [self.sb(sc2, "GB%d" % i, [128, SEQ], BF16) for i in range(2)]
            ACC = self.sb(sc2, "ACC", [128, SEQ], F32)
            XV = [self.sb(sc2, "XV%d" % i, [128, 512], F32) for i in range(2)]
            WIN = [self.sb(sc2, "WIN%d" % i, [128, 3 * 1024], BF16) for i in range(2)]
            SCW = self.sb(sc2, "SCW", [128, 24], F32)
            swin = [self.gs("win_%d" % i) for i in range(2)]
            swout = [self.gs("wout_%d" % i) for i in range(2)]
            P.dma("sp", self.s_const, lambda e: e.dma_start(out=SCW[:], in_=self.scw), writes=["SCW"])
            for i in range(2):
                P.op("pool", lambda e, i=i: e.memset(U[i][:, 0:1], 0.0), writes=[("U", i)])
                P.op("pool", lambda e, i=i: e.memset(U[i][:, SEQ + 1:SEQ + 2], 0.0), writes=[("U", i)])
            for dc in range(8):
                hv = self.hview("h", dc, 0, SEQ)
                hk = self.hkeys("h", dc, 0, SEQ)
                zv = Z[:, dc * SEQ:(dc + 1) * SEQ]
                if dc % 2 == 0:
                    P.op("act", lambda e, hv=hv, zv=zv, dc=dc: e.activation(out=zv, in_=hv, func=AF.Identity,
                                                                           scale=self.mv(l, 3 * k + 1, dc, w), bias=self.mv(l, 3 * k, dc, w)),
                         reads=hk + ["MV"], writes=[("Z", dc)])
                else:
                    P.op("dve", lambda e, hv=hv, zv=zv, dc=dc: e.tensor_scalar(out=zv, in0=hv, scalar1=self.mv(l, 3 * k + 1, dc, w),
                                                                              scalar2=self.mv(l, 3 * k, dc, w), op0=ALU.mult, op1=ALU.add),
                         reads=hk + ["MV"], writes=[("Z", dc)])

            def loadwin(dc):
                slot = dc % 2
                for j in range(3):
                    r0 = (j * 8 + dc) * 128
                    P.dma("sp", swin[slot], lambda e, slot=slot, j=j, r0=r0: e.dma_start(
                        out=WIN[slot][:, j * 1024:(j + 1) * 1024], in_=self.win1s[r0:r0 + 128, :]),
                        reads=self.scr_keys["win1"], writes=[("WIN", slot)])

            def loadwout(dc):
                slot = dc % 2
                r0 = dc * 128
                P.dma("sp", swout[slot], lambda e, slot=slot, r0=r0: e.dma_start(out=WOUT[slot][:], in_=self.wout1s[r0:r0 + 128, :]),
                      reads=self.scr_keys["wout1"], writes=[("WOUT", slot)])

            loadwin(0)
            it = 0
            for dc in range(8):
                if dc + 1 < 8:
                    loadwin(dc + 1)
                slot = dc % 2
                ub = dc % 2
                for tt in range(4):
                    pb = (it % 2) * 3
                    it += 1
                    xb = tt % 2
                    for j in range(3):
                        pp = self.PS[pb + j]
                        fns = []
                        for kk in range(8):
                            fns.append(lambda e, slot=slot, j=j, kk=kk, pp=pp, tt=tt: e.matmul(
                                pp[:, :], WIN[slot][:, j * 1024 + kk * 128: j * 1024 + (kk + 1) * 128],
                                Z[:, kk * SEQ + tt * 512: kk * SEQ + (tt + 1) * 512], start=(kk == 0), stop=(kk == 7)))
                        P.group("pe", fns, reads=[("WIN", slot)] + [("Z", kk) for kk in range(8)], writes=[("ps", pb + j)])
                    P.op("act", lambda e, pb=pb, ub=ub, tt=tt: e.activation(out=GB[ub][:, tt * 512:(tt + 1) * 512], in_=self.PS[pb][:, :], func=AF.Copy),
                         reads=[("ps", pb)], writes=[("GB", ub)])
                    P.op("act", lambda e, pb=pb, xb=xb: e.activation(out=XV[xb][:, :], in_=self.PS[pb + 2][:, :], func=AF.Copy),
                         reads=[("ps", pb + 2)], writes=[("XV", xb)])
                    P.op("dve", lambda e, pb=pb, xb=xb, ub=ub, tt=tt: e.tensor_tensor(
                        out=U[ub][:, 1 + tt * 512: 1 + (tt + 1) * 512], in0=self.PS[pb + 1][:, :], in1=XV[xb][:, :], op=ALU.mult),
                        reads=[("ps", pb + 1), ("XV", xb)], writes=[("U", ub)])
                sw = lambda tap, dc=dc: SCW[:, dc * 3 + tap: dc * 3 + tap + 1]
                P.op("act", lambda e, ub=ub, sw=sw: e.activation(out=ACC[:, :], in_=U[ub][:, 0:SEQ], func=AF.Copy, scale=sw(0)),
                     reads=[("U", ub), "SCW"], writes=["ACC"])
                for tap in (1, 2):
                    P.op("dve", lambda e, ub=ub, sw=sw, tap=tap: e.scalar_tensor_tensor(
                        out=ACC[:, :], in0=U[ub][:, tap:tap + SEQ], scalar=sw(tap), in1=ACC[:, :], op0=ALU.mult, op1=ALU.add),
                        reads=[("U", ub), "SCW", "ACC"], writes=["ACC"])
                P.op("dve", lambda e, ub=ub, dc=dc: e.tensor_tensor(out=F[:, dc * SEQ:(dc + 1) * SEQ], in0=ACC[:, :], in1=GB[ub][:, :], op=ALU.mult),
                     reads=["ACC", ("GB", ub)], writes=[("F", dc)])
            P.barrier()
            sc2.close()
            WOUT = [self.sb(sc, "WOUT%d" % i, [128, 1024], BF16) for i in range(2)]
            tl = self.ln_tiles_alloc(sc, 4)
            loadwout(0)
            it = 0
            for dc in range(8):
                if dc + 1 < 8:
                    loadwout(dc + 1)
                slot = dc % 2
                for tt in range(4):
                    b = 6 + it % 2
                    it += 1
                    py = self.PS[b]
                    fns = []
                    for fc in range(8):
                        fns.append(lambda e, slot=slot, fc=fc, py=py, tt=tt: e.matmul(
                            py[:, :], WOUT[slot][:, fc * 128:(fc + 1) * 128], F[:, fc * SEQ + tt * 512: fc * SEQ + (tt + 1) * 512],
                            start=(fc == 0), stop=(fc == 7)))
                    P.group("pe", fns, reads=[("WOUT", slot)] + [("F", fc) for fc in range(8)], writes=[("ps", b)])
                    hv = self.hview("h", dc, tt * 512, 512)
                    hk = self.hkeys("h", dc, tt * 512, 512)
                    P.op("dve", lambda e, py=py, hv=hv, dc=dc: e.scalar_tensor_tensor(
                        out=hv, in0=py[:, :], scalar=self.mv(l, 3 * k + 2, dc, w), in1=hv, op0=ALU.mult, op1=ALU.add),
                        reads=[("ps", b), "MV"] + hk, writes=hk)
                    self.stats_accum(tl, tt, "h", dc, tt * 512, 512)
            for tt in range(4):
                self.ln_tile(tl, "h", l, k, w, tt * 512, 512, ti=tt, pre=True)
            P.barrier()

    def l0mix_phase(self, s):
        P = self.P
        l, k, w = 0, 1, s
        ZT = CTX + SEQ
        with ExitStack() as sc:
            Z = self.sb(sc, "Z0", [128, 8 * ZT], BF16)
            F = self.sb(sc, "F0", [128, 4 * SEQ], BF16)
            NWS = 4
            WS = [self.sb(sc, "WS%d" % i, [128, 1024], BF16) for i in range(NWS)]
            sws = [self.gs("ws_%d" % i) for i in range(NWS)]
            WO = [self.sb(sc, "WO%d" % i, [128, 512], BF16) for i in range(2)]
            swo = [self.gs("wo_%d" % i) for i in range(2)]
            wst = {"n": 0}

            def loadw(cc):
                slot = wst["n"] % NWS
                wst["n"] += 1
                P.dma("sp", sws[slot], lambda e, slot=slot, cc=cc: e.dma_start(out=WS[slot][:], in_=self.win0s[cc * 128:(cc + 1) * 128, :]),
                      reads=self.scr_keys["win0"], writes=[("WS", slot)])
                return slot

            if self.dump_ctx:
                P.dma("sp", self.s_ioc, lambda e: e.dma_start(out=self.dbg3[:, :], in_=self.H[:]),
                      reads=[("h", dc, t) for dc in range(8) for t in range(4)], writes=["dbg3"])
                P.dma("sp", self.s_ioc, lambda e: e.dma_start(out=self.dbg4[:, :], in_=self.HC[:]),
                      reads=[("hc", dc) for dc in range(8)], writes=["dbg4"])
            for dc in range(8):
                for (stream, ww, off, n) in (("hc", 2, 0, CTX), ("h", s, CTX, SEQ)):
                    hv = self.hview(stream, dc, 0, n)
                    hk = self.hkeys(stream, dc, 0, n)
                    zv = Z[:, dc * ZT + off: dc * ZT + off + n]
                    if dc % 2 == 0:
                        P.op("act", lambda e, hv=hv, zv=zv, dc=dc, ww=ww: e.activation(
                            out=zv, in_=hv, func=AF.Identity, scale=self.mv(l, 3 * k + 1, dc, ww), bias=self.mv(l, 3 * k, dc, ww)),
                            reads=hk + ["MV"], writes=[("Z", dc)])
                    else:
                        P.op("dve", lambda e, hv=hv, zv=zv, dc=dc, ww=ww: e.tensor_scalar(
                            out=zv, in0=hv, scalar1=self.mv(l, 3 * k + 1, dc, ww), scalar2=self.mv(l, 3 * k, dc, ww),
                            op0=ALU.mult, op1=ALU.add), reads=hk + ["MV"], writes=[("Z", dc)])
            zkeys = [("Z", kk) for kk in range(8)]
            pst = {"n": 0}

            def proj(slot, c0, n, evac, bank=None):
                if bank is None:
                    bank = 2 + pst["n"] % 2
                    pst["n"] += 1
                pp = self.PS[bank]
                fns = []
                for kk in range(8):
                    fns.append(lambda e, kk=kk, pp=pp: e.matmul(
                        pp[:, 0:n], WS[slot][:, kk * 128:(kk + 1) * 128], Z[:, kk * ZT + c0: kk * ZT + c0 + n],
                        start=(kk == 0), stop=(kk == 7)))
                P.group("pe", fns, reads=[("WS", slot)] + zkeys, writes=[("ps", bank)])
                evac(pp, bank)

            def halfproj(half, tl=None):
                it = 0
                def loadwo(dc):
                    slot = dc % 2
                    P.dma("sp", swo[slot], lambda e, slot=slot, dc=dc: e.dma_start(
                        out=WO[slot][:], in_=self.wout0s[dc * 128:(dc + 1) * 128, half * 512:(half + 1) * 512]),
                        reads=self.scr_keys["wout0"], writes=[("WO", slot)])
                loadwo(0)
                for dc in range(8):
                    if dc + 1 < 8:
                        loadwo(dc + 1)
                    slot = dc % 2
                    for tt in range(4):
                        b = 2 + it % 2
                        it += 1
                        py = self.PS[b]
                        fns = []
                        for fc in range(4):
                            fns.append(lambda e, slot=slot, fc=fc, py=py, tt=tt: e.matmul(
                                py[:, :], WO[slot][:, fc * 128:(fc + 1) * 128], F[:, fc * SEQ + tt * 512: fc * SEQ + (tt + 1) * 512],
                                start=(fc == 0), stop=(fc == 3)))
                        P.group("pe", fns, reads=[("WO", slot)] + [("F", fc) for fc in range(4)], writes=[("ps", b)])
                        hv = self.hview("h", dc, tt * 512, 512)
                        hk = self.hkeys("h", dc, tt * 512, 512)
                        P.op("dve", lambda e, py=py, hv=hv, dc=dc: e.scalar_tensor_tensor(
                            out=hv, in0=py[:, :], scalar=self.mv(l, 3 * k + 2, dc, w), in1=hv, op0=ALU.mult, op1=ALU.add),
                            reads=[("ps", b), "MV"] + hk, writes=hk)
                        if tl is not None:
                            self.stats_accum(tl, tt, "h", dc, tt * 512, 512)

            with ExitStack() as sn:
                Tt = self.sb(sn, "Tt", [128, 3840], BF16)
                ID2 = self.sb(sn, "ID2", [128, 64], BF16)
                ONB = self.sb(sn, "ONB", [128, 64], BF16)
                with ExitStack() as stmp:
                    RPb = self.sb(stmp, "RPb", [128, 3840], BF16)
                    NMb = self.sb(stmp, "NMb", [128, 3840], BF16)
                    P.dma("pool", self.s_const, lambda e: e.dma_start(out=RPb[:], in_=self.rpbg), writes=["RPb"])
                    P.dma("pool", self.s_const, lambda e: e.dma_start(out=NMb[:], in_=self.nmask), writes=["NMb"])
                    P.dma("pool", self.s_const, lambda e: e.dma_start(out=ID2[:], in_=self.id2), writes=["ID2"])
                    P.op("dve", lambda e: e.tensor_tensor(out=Tt[:], in0=RPb[:], in1=NMb[:], op=ALU.add), reads=["RPb", "NMb"], writes=["Tt"])
                    P.op("pool", lambda e: e.memset(ONB[:], 1.0), writes=["ONB"])
                    P.barrier()
                QT = self.sb(sn, "QT", [128, SEQ], BF16)
                KT = self.sb(sn, "KT", [128, ZT], BF16)
                V = self.sb(sn, "V", [128, 18 * 128], BF16)
                V2 = self.sb(sn, "V2", [128, 15 * 128], BF16)
                PC = [self.sb(sn, "PC%d" % i, [128, 2 * SEQ], BF16) for i in range(2)]
                PL = [self.sb(sn, "PL%d" % i, [128, 256], BF16) for i in range(4)]
                RD = [self.sb(sn, "RD%d" % i, [128, 512], F32) for i in range(2)]
                PLS = [self.sb(sn, "PLS%d" % i, [128, 512], F32) for i in range(2)]
                for hp in range(4):
                    sq_, sk_, sv_ = loadw(hp), loadw(4 + hp), loadw(8 + hp)
                    for tt in range(4):
                        proj(sq_, CTX + tt * 512, 512, lambda pp, bank, tt=tt: P.op(
                            "act", lambda e, pp=pp, tt=tt: e.activation(out=QT[:, tt * 512:(tt + 1) * 512], in_=pp[:, :], func=AF.Copy, scale=0.125),
                            reads=[("ps", bank)], writes=["QT"]))
                    proj(sk_, 0, CTX, lambda pp, bank: P.op(
                        "dve", lambda e, pp=pp: e.tensor_copy(out=KT[:, 0:CTX], in_=pp[:, 0:CTX]), reads=[("ps", bank)], writes=["KT"]))
                    for tt in range(4):
                        proj(sk_, CTX + tt * 512, 512, lambda pp, bank, tt=tt: P.op(
                            "dve", lambda e, pp=pp, tt=tt: e.tensor_copy(out=KT[:, CTX + tt * 512: CTX + (tt + 1) * 512], in_=pp[:, :]),
                            reads=[("ps", bank)], writes=["KT"]))
                    def vproj(dst, dkey, chunks):
                        for g0 in range(0, len(chunks), 4):
                            grp = chunks[g0:g0 + 4]
                            bank = 2 + pst["n"] % 2
                            pst["n"] += 1
                            pp = self.PS[bank]
                            fns = []
                            for gi, (ci, tok0) in enumerate(grp):
                                for kk in range(8):
                                    fns.append(lambda e, gi=gi, tok0=tok0, kk=kk, pp=pp, sv_=sv_: e.matmul(
                                        pp[:, gi * 128:(gi + 1) * 128], Z[:, kk * ZT + tok0: kk * ZT + tok0 + 128],
                                        WS[sv_][:, kk * 128:(kk + 1) * 128], start=(kk == 0), stop=(kk == 7)))
                            P.group("pe", fns, reads=[("WS", sv_)] + zkeys, writes=[("ps", bank)])
                            c0 = grp[0][0]
                            nn = len(grp) * 128
                            P.op("act", lambda e, pp=pp, c0=c0, nn=nn, dst=dst: e.activation(out=dst[:, c0 * 128: c0 * 128 + nn], in_=pp[:, 0:nn], func=AF.Copy),
                                 reads=[("ps", bank)], writes=[dkey])
                    vproj(V, "V", [(ci, ci * 128) for ci in range(18)])
                    vproj(V2, "V2", [(ci, CTX + 64 + ci * 128) for ci in range(15)])
                    for hh in range(2):
                        hs = slice(hh * 64, (hh + 1) * 64)
                        for cc in range(2):
                            for qt in range(4):
                                bank = 2 + pst["n"] % 2
                                pst["n"] += 1
                                pp = self.PS[bank]
                                P.group("pe", [lambda e, pp=pp, hs=hs, cc=cc, qt=qt: e.matmul(
                                    pp[:, :], KT[hs, cc * 128:(cc + 1) * 128], QT[hs, qt * 512:(qt + 1) * 512], start=True, stop=True)],
                                    reads=["KT", "QT"], writes=[("ps", bank)])
                                P.op("act", lambda e, pp=pp, hh=hh, cc=cc, qt=qt: e.activation(
                                    out=PC[hh][:, cc * SEQ + qt * 512: cc * SEQ + (qt + 1) * 512], in_=pp[:, :], func=AF.Exp),
                                    reads=[("ps", bank)], writes=[("PC", hh)])
                    units = [(rg, hh, rr) for rg in range(4) for hh in range(2) for rr in range(8)]

                    def qk(ui):
                        rg, hh, rr = units[ui]
                        r = rg * 8 + rr
                        r0 = min(max(r - 4, 0), 24)
                        hs = slice(hh * 64, (hh + 1) * 64)
                        bank = ui % 4
                        pp = self.PS[bank]
                        o = 0
                        fns = []
                        for c in range(4):
                            k0 = CTX + (r0 + 2 * c) * 64
                            ro0 = r0 + 2 * c - r + 7
                            t0 = hp * 960 + ro0 * 64
                            fns.append(lambda e, pp=pp, hs=hs, c=c, k0=k0, r=r, o=o: e.matmul(
                                pp[:, o + c * 64: o + (c + 1) * 64], KT[hs, k0:k0 + 128], QT[hs, r * 64:(r + 1) * 64], start=True, stop=False))
                            fns.append(lambda e, pp=pp, hs=hs, c=c, t0=t0, o=o: e.matmul(
                                pp[:, o + c * 64: o + (c + 1) * 64], Tt[hs, t0:t0 + 128], ID2[hs, 0:64], start=False, stop=True))
                        P.group("pe", fns, reads=["KT", "QT", "Tt", "ID2"], writes=[("ps", bank)])
                        P.op("act", lambda e, pp=pp, bank=bank, o=o: e.activation(out=PL[bank][:, :], in_=pp[:, o:o + 256], func=AF.Exp),
                             reads=[("ps", bank)], writes=[("PL", bank)])

                    def pv(ui):
                        rg, hh, rr = units[ui]
                        r = rg * 8 + rr
                        r0 = min(max(r - 4, 0), 24)
                        bank = ui % 4
                        nb, db = 4 + (rg % 2), 6 + (rg % 2)
                        hs = slice(hh * 64, (hh + 1) * 64)
                        sl_ = (rg * 2 + hh) % 2
                        if rr == 0:
                            fns = []
                            for cc in range(2):
                                fns.append(lambda e, cc=cc, nb=nb, hs=hs, hh=hh, rg=rg: e.matmul(
                                    self.PS[nb][hs, :], V[:, cc * 128 + hh * 64: cc * 128 + (hh + 1) * 64],
                                    PC[hh][:, cc * SEQ + rg * 512: cc * SEQ + (rg + 1) * 512], start=(cc == 0), stop=False))
                            for cc in range(2):
                                fns.append(lambda e, cc=cc, db=db, hs=hs, hh=hh, rg=rg: e.matmul(
                                    self.PS[db][hs, :], ONB[:, 0:64],
                                    PC[hh][:, cc * SEQ + rg * 512: cc * SEQ + (rg + 1) * 512], start=(cc == 0), stop=False))
                            P.group("pe", fns, reads=["V", ("PC", hh), "ONB"], writes=[("ps", nb), ("ps", db)])
                        P.op("dve", lambda e, bank=bank, sl_=sl_, rr=rr: e.tensor_reduce(
                            out=PLS[sl_][:, rr * 64:(rr + 1) * 64], in_=PL[bank][:, :].rearrange("p (c q) -> p q c", c=4),
                            axis=mybir.AxisListType.X, op=ALU.add), reads=[("PL", bank)], writes=[("PLS", sl_)])
                        fns = []
                        for c in range(4):
                            if r0 % 2 == 0:
                                vsrc, ci = V, 2 + r0 // 2 + c
                            else:
                                vsrc, ci = V2, (r0 - 1) // 2 + c
                            lv = vsrc[:, ci * 128 + hh * 64: ci * 128 + (hh + 1) * 64]
                            rv = PL[bank][:, c * 64:(c + 1) * 64]
                            fns.append(lambda e, lv=lv, rv=rv, c=c, nb=nb, hs=hs, rr=rr: e.matmul(
                                self.PS[nb][hs, rr * 64:(rr + 1) * 64], lv, rv, start=False, stop=(c == 3)))
                        P.group("pe", fns, reads=["V", "V2", ("PL", bank)], writes=[("ps", nb)])
                        if rr == 7:
                            P.group("pe", [lambda e, db=db, hs=hs, sl_=sl_: e.matmul(
                                self.PS[db][hs, :], self.ONES[:, 0:64], PLS[sl_][:, :], start=False, stop=True)],
                                reads=[("PLS", sl_), "ONES"], writes=[("ps", db)])
                        if hh == 1 and rr == 7:
                            rb = rg % 2
                            P.op("dve", lambda e, rb=rb, db=db: e.reciprocal(out=RD[rb][:, :], in_=self.PS[db][:, :]),
                                 reads=[("ps", db)], writes=[("RD", rb)])
                            P.op("dve", lambda e, rb=rb, nb=nb, rg=rg, hp=hp: e.tensor_tensor(
                                out=F[:, hp * SEQ + rg * 512: hp * SEQ + (rg + 1) * 512], in0=self.PS[nb][:, :], in1=RD[rb][:, :], op=ALU.mult),
                                reads=[("ps", nb), ("RD", rb)], writes=[("F", hp)])

                    for ui in range(3):
                        qk(ui)
                    for ui in range(len(units)):
                        if ui + 3 < len(units):
                            qk(ui + 3)
                        pv(ui)
                P.barrier()
            if self.dump_ctx:
                P.dma("sp", self.s_ioc, lambda e: e.dma_start(out=self.dbg[0:128, :], in_=F[:]), reads=[("F", i) for i in range(4)], writes=["dbg0"])
                P.barrier()
            halfproj(0)
            P.barrier()

            with ExitStack() as sl:
                XRT = self.sb(sl, "XRT", [128, ZT + 8], F32)
                XC = self.sb(sl, "XC", [128, ZT], F32)
                XCB = self.sb(sl, "XCB", [128, ZT], BF16)
                GG = self.sb(sl, "GG", [128, SEQ], BF16)
                A = self.sb(sl, "A", [128, ZT], F32)
                B = self.sb(sl, "B", [128, ZT], F32)
                TR = [self.sb(sl, "TR%d" % i, [128, 512], F32) for i in range(2)]
                T2 = self.sb(sl, "T2", [128, ZT], F32)
                BD = self.sb(sl, "BD", [128, 16 * 128], BF16)
                LCW = self.sb(sl, "LCW", [128, 16], F32)
                LCB = self.sb(sl, "LCB", [128, 4], F32)
                LBA = self.sb(sl, "LBA", [128, 8], F32)
                LBX = self.sb(sl, "LBX", [128, 8], F32)
                CL = self.sb(sl, "CL", [128, 8], F32)
                CLH = self.sb(sl, "CLH", [128, 8], F32)
                QRT = self.sb(sl, "QRT", [128, 1], F32)
                P.dma("pool", self.s_const, lambda e: e.dma_start(out=BD[:, 0:1024].rearrange("p (g c) -> p g c", g=8),
                                                                  in_=self.lwa.rearrange("(g p) c -> p g c", p=128)), writes=["BD"])
                P.dma("pool", self.s_const, lambda e: e.dma_start(out=BD[:, 1024:2048].rearrange("p (g c) -> p g c", g=8),
                                                                  in_=self.lwx.rearrange("(g p) c -> p g c", p=128)), writes=["BD"])
                for (t, src, nm) in ((LCW, self.lcw, "LCW"), (LCB, self.lcb, "LCB"), (LBA, self.lba, "LBA"), (LBX, self.lbx, "LBX"), (CL, self.llam, "CL")):
                    P.dma("sp", self.s_const, lambda e, t=t, src=src: e.dma_start(out=t[:], in_=src), writes=[nm])
                P.op("pool", lambda e: e.memset(QRT[:], 0.25), writes=["QRT"])
                P.op("act", lambda e: e.activation(out=CL[:], in_=CL[:], func=AF.Exp, scale=-1.0), reads=["CL"], writes=["CL"])
                P.op("dve", lambda e: e.tensor_scalar_add(out=CL[:], in0=CL[:], scalar1=1.0), reads=["CL"], writes=["CL"])
                P.op("act", lambda e: e.activation(out=CL[:], in_=CL[:], func=AF.Ln), reads=["CL"], writes=["CL"])
                P.op("dve", lambda e: e.tensor_scalar_mul(out=CLH[:], in0=CL[:], scalar1=-4.0), reads=["CL"], writes=["CLH"])
                P.op("dve", lambda e: e.tensor_scalar_mul(out=CL[:], in0=CL[:], scalar1=-8.0), reads=["CL", "CLH"], writes=["CL"])
                P.op("dve", lambda e: e.tensor_scalar_mul(out=LBA[:], in0=LBA[:], scalar1=0.5), reads=["LBA"], writes=["LBA"])
                P.op("dve", lambda e: e.tensor_scalar_mul(out=LBX[:], in0=LBX[:], scalar1=0.5), reads=["LBX"], writes=["LBX"])
                segs = ((0, 0, CTX), (CTX + 4, CTX, SEQ))
                tiles5 = [(0, CTX)] + [(CTX + tt * 512, 512) for tt in range(4)]
                gst = {"n": 0}
                GK = 0.7978845608028654
                for j in range(4):
                    sx, sg_ = loadw(12 + j), loadw(16 + j)
                    for (xb, cb_, n) in segs:
                        P.op("pool", lambda e, xb=xb: e.memset(XRT[:, xb:xb + 2], 0.0), writes=["XRT"])
                        P.op("pool", lambda e, xb=xb, n=n: e.memset(XRT[:, xb + 2 + n: xb + 4 + n], 0.0), writes=["XRT"])
                    proj(sx, 0, CTX, lambda pp, bank: P.op(
                        "dve", lambda e, pp=pp: e.tensor_copy(out=XRT[:, 2:2 + CTX], in_=pp[:, 0:CTX]), reads=[("ps", bank)], writes=["XRT"]))
                    for tt in range(4):
                        proj(sx, CTX + tt * 512, 512, lambda pp, bank, tt=tt: P.op(
                            "dve", lambda e, pp=pp, tt=tt: e.tensor_copy(out=XRT[:, CTX + 6 + tt * 512: CTX + 6 + (tt + 1) * 512], in_=pp[:, :]),
                            reads=[("ps", bank)], writes=["XRT"]))
                    for tt in range(4):
                        def gevac(pp, bank, tt=tt):
                            b = gst["n"] % 2
                            gst["n"] += 1
                            P.op("act", lambda e, pp=pp, b=b: e.activation(out=TR[b][:, :], in_=pp[:, :], func=AF.Square), reads=[("ps", bank)], writes=[("TR", b)])
                            P.op("dve", lambda e, b=b: e.tensor_scalar(out=TR[b][:, :], in0=TR[b][:, :], scalar1=0.044715, scalar2=1.0, op0=ALU.mult, op1=ALU.add),
                                 reads=[("TR", b)], writes=[("TR", b)])
                            P.op("dve", lambda e, pp=pp, b=b: e.tensor_tensor(out=TR[b][:, :], in0=TR[b][:, :], in1=pp[:, :], op=ALU.mult),
                                 reads=[("TR", b), ("ps", bank)], writes=[("TR", b)])
                            P.op("act", lambda e, b=b: e.activation(out=TR[b][:, :], in_=TR[b][:, :], func=AF.Tanh, scale=GK), reads=[("TR", b)], writes=[("TR", b)])
                            P.op("dve", lambda e, pp=pp, b=b, tt=tt: e.scalar_tensor_tensor(out=GG[:, tt * 512:(tt + 1) * 512], in0=TR[b][:, :], scalar=1.0, in1=pp[:, :],
                                                                                           op0=ALU.add, op1=ALU.mult),
                                 reads=[("TR", b), ("ps", bank)], writes=["GG"])
                        proj(sg_, CTX + tt * 512, 512, gevac)
                    for (xb, cb_, n) in segs:
                        P.op("act", lambda e, xb=xb, cb_=cb_, n=n, j=j: e.activation(
                            out=XC[:, cb_:cb_ + n], in_=XRT[:, xb:xb + n], func=AF.Identity, scale=LCW[:, j * 4:j * 4 + 1], bias=LCB[:, j:j + 1]),
                            reads=["XRT", "LCW", "LCB"], writes=["XC"])
                        for tap in (1, 2, 3):
                            P.op("dve", lambda e, xb=xb, cb_=cb_, n=n, j=j, tap=tap: e.scalar_tensor_tensor(
                                out=XC[:, cb_:cb_ + n], in0=XRT[:, xb + tap: xb + tap + n], scalar=LCW[:, j * 4 + tap: j * 4 + tap + 1],
                                in1=XC[:, cb_:cb_ + n], op0=ALU.mult, op1=ALU.add), reads=["XRT", "LCW", "XC"], writes=["XC"])
                    P.op("act", lambda e: e.activation(out=XCB[:, :], in_=XC[:, :], func=AF.Copy), reads=["XC"], writes=["XCB"])
                    for d in range(2):
                        col = d * 4 + j
                        for (c0, n) in tiles5:
                            b = gst["n"] % 2
                            gst["n"] += 1
                            ba, bx = 2 + b, 4 + b
                            for (bank, kind) in ((ba, 0), (bx, 1)):
                                P.group("pe", [lambda e, bank=bank, kind=kind, col=col, c0=c0, n=n: e.matmul(
                                    self.PS[bank][:, 0:n], BD[:, (kind * 8 + col) * 128:(kind * 8 + col + 1) * 128], XCB[:, c0:c0 + n], start=True, stop=True)],
                                    reads=["BD", "XCB"], writes=[("ps", bank)])
                            o0 = c0 if d == 0 else (c0 - CTX if c0 >= CTX else SEQ)
                            P.op("act", lambda e, ba=ba, b=b, n=n, col=col: e.activation(out=TR[b][:, 0:n], in_=self.PS[ba][:, 0:n], func=AF.Tanh, scale=0.5, bias=LBA[:, col:col + 1]),
                                 reads=[("ps", ba), "LBA"], writes=[("TR", b)])
                            P.op("act", lambda e, bx=bx, n=n, col=col, o0=o0: e.activation(out=T2[:, o0:o0 + n], in_=self.PS[bx][:, 0:n], func=AF.Tanh, scale=0.5, bias=LBX[:, col:col + 1]),
                                 reads=[("ps", bx), "LBX"], writes=["T2"])
                            P.op("act", lambda e, b=b, n=n, col=col, o0=o0: e.activation(out=A[:, o0:o0 + n], in_=TR[b][:, 0:n], func=AF.Exp, scale=CLH[:, col:col + 1], bias=CLH[:, col:col + 1]),
                                 reads=[("TR", b), "CLH"], writes=["A"])
                            P.op("act", lambda e, b=b, n=n, col=col, o0=o0: e.activation(out=B[:, o0:o0 + n], in_=TR[b][:, 0:n], func=AF.Exp, scale=CL[:, col:col + 1], bias=CL[:, col:col + 1]),
                                 reads=[("TR", b), "CL"], writes=["B"])
                        P.op("act", lambda e: e.activation(out=B[:, :], in_=B[:, :], func=AF.Sqrt, scale=-0.25, bias=QRT[:, 0:1]), reads=["B", "QRT"], writes=["B"])
                        P.op("dve", lambda e: e.scalar_tensor_tensor(out=B[:, :], in0=T2[:, :], scalar=1.0, in1=B[:, :], op0=ALU.add, op1=ALU.mult),
                             reads=["T2", "B"], writes=["B"])
                        if d == 0:
                            P.op("dve", lambda e: e.tensor_tensor(out=B[:, :], in0=B[:, :], in1=XC[:, :], op=ALU.mult), reads=["B", "XC"], writes=["B"])
                            P.op("dve", lambda e: e.tensor_tensor_scan(out=XRT[:, 0:ZT], data0=A[:, 0:ZT], data1=B[:, 0:ZT], initial=0.0,
                                                                       op0=ALU.mult, op1=ALU.add), reads=["A", "B"], writes=["XRT"])
                        else:
                            P.op("dve", lambda e: e.tensor_tensor(out=B[:, 0:SEQ], in0=B[:, 0:SEQ], in1=XC[:, CTX:ZT], op=ALU.mult), reads=["B", "XC"], writes=["B"])
                            P.op("dve", lambda e: e.tensor_tensor(out=B[:, SEQ:ZT], in0=B[:, SEQ:ZT], in1=XC[:, 0:CTX], op=ALU.mult), reads=["B", "XC"], writes=["B"])
                            P.op("dve", lambda e: e.tensor_tensor_scan(out=XC[:, 0:ZT][:, ::-1], data0=A[:, 0:ZT][:, ::-1], data1=B[:, 0:ZT][:, ::-1],
                                                                       initial=0.0, op0=ALU.mult, op1=ALU.add), reads=["A", "B"], writes=["XC"])
                    P.op("dve", lambda e: e.tensor_tensor(out=A[:, 0:SEQ], in0=XRT[:, CTX:ZT], in1=XC[:, 0:SEQ], op=ALU.add),
                         reads=["XRT", "XC", "A"], writes=["A"], strict=True)
                    P.op("dve", lambda e, j=j: e.scalar_tensor_tensor(out=F[:, j * SEQ:(j + 1) * SEQ], in0=A[:, 0:SEQ], scalar=0.5, in1=GG[:, :],
                                                                      op0=ALU.mult, op1=ALU.mult),
                         reads=["A", "GG"], writes=[("F", j)])
                P.barrier()
            if self.dump_ctx:
                P.dma("sp", self.s_ioc, lambda e: e.dma_start(out=self.dbg[128:256, :], in_=F[:]), reads=[("F", i) for i in range(4)], writes=["dbg1"])
                P.barrier()
            with ExitStack() as sln:
                tl = self.ln_tiles_alloc(sln, 4)
                halfproj(1, tl)
                for tt in range(4):
                    self.ln_tile(tl, "h", l, k, w, tt * 512, 512, ti=tt, pre=True)
                P.barrier()


def prep_shared(inp):
    f = lambda a: np.ascontiguousarray(np.asarray(a, dtype=np.float32))
    sh = {}
    sh["modw"] = np.concatenate([lhsT_layout(f(inp["mod_w"][l])) for l in range(2)], axis=0)
    mb = np.concatenate([vec_pm(f(inp["mod_b"][l])) for l in range(2)], axis=1)
    sh["modb3"] = np.ascontiguousarray(np.repeat(mb, 3, axis=1))
    sh["lng"] = np.concatenate([vec_pm(f(inp["ln_g"][l, k])) for l in range(2) for k in range(3)], axis=1)
    sh["lnb"] = np.concatenate([vec_pm(f(inp["ln_b"][l, k])) for l in range(2) for k in range(3)], axis=1)
    sh["w1"] = np.concatenate([lhsT_layout(f(inp["ffn_w1"][l, j])) for l in range(2) for j in range(2)], axis=0)
    sh["w3"] = np.concatenate([lhsT_layout(f(inp["ffn_w3"][l, j])) for l in range(2) for j in range(2)], axis=0)
    sh["w2"] = np.concatenate([lhsT_layout(f(inp["ffn_w2"][l, j])) for l in range(2) for j in range(2)], axis=0)
    sh["win0"] = lhsT_layout(f(inp["mix0_w_in"][0]))
    sh["wout0"] = lhsT_layout(f(inp["mix0_w_out"][0]))
    sh["win1"] = lhsT_layout(f(inp["mix1_w_in"][0]))
    sh["wout1"] = lhsT_layout(f(inp["mix1_w_out"][0]))
    scw = f(inp["sconv_w"][0])
    sh["scw"] = np.ascontiguousarray(np.stack([vec_pm(scw[t]) for t in range(3)], axis=2).reshape(128, 24))
    rpb = f(inp["na_rpb"][0])
    j = np.arange(64)[:, None]
    kc = np.arange(64)[None, :]
    ci = np.clip(kc - j + 15, 0, 30)
    g = rpb[:, :, ci]
    g = g.transpose(0, 2, 1, 3)
    rp = np.zeros((128, 4 * 15 * 64), np.float32)
    for h in range(8):
        half, idx = h % 2, h // 2
        rp[half * 64:(half + 1) * 64, idx * 960:(idx + 1) * 960] = g[h].reshape(64, 960)
    sh["rpbg"] = rp
    start = np.clip(j - 8, 0, 48)
    valid = (kc >= start) & (kc < start + 16)
    nm = np.where(valid, 0.0, -30000.0).astype(np.float32)
    sh["nmask"] = np.ascontiguousarray(np.tile(np.concatenate([nm, nm], axis=0), (1, 60)))
    eye = np.eye(64, dtype=np.float32)
    sh["id2"] = np.ascontiguousarray(np.concatenate([eye, eye], axis=0))
    cw = f(inp["lru_conv_w"][0])
    sh["lcw"] = np.ascontiguousarray(np.stack([vec_pm(cw[t]) for t in range(4)], axis=2).reshape(128, 16))
    sh["lcb"] = vec_pm(f(inp["lru_conv_b"][0]))
    def bd(wm):
        o = np.zeros((2, 4, 128, 128), np.float32)
        for d in range(2):
            for n in range(8):
                c, hh = n // 2, n % 2
                o[d, c, hh * 64:(hh + 1) * 64, hh * 64:(hh + 1) * 64] = wm[d, n]
        return o.reshape(2 * 4 * 128, 128)
    sh["lwa"] = bd(f(inp["lru_w_a"][0]))
    sh["lwx"] = bd(f(inp["lru_w_x"][0]))
    sh["lba"] = np.concatenate([vec_pm(f(inp["lru_b_a"][0, d])) for d in range(2)], axis=1)
    sh["lbx"] = np.concatenate([vec_pm(f(inp["lru_b_x"][0, d])) for d in range(2)], axis=1)
    sh["llam"] = np.concatenate([vec_pm(f(inp["lru_lambda"][0, d])) for d in range(2)], axis=1)
    return sh


def prep_core(inp, bidx):
    f = lambda a: np.asarray(a, dtype=np.float32)
    m = {}
    m["xT"] = np.concatenate([to_pm(f(inp["x"][b])) for b in bidx], axis=0)
    m["cxT"] = np.concatenate([to_pm(f(inp["ctx"][b])) for b in bidx], axis=0)
    cols = [f(inp["c"][b]) for b in bidx]
    while len(cols) < 2:
        cols.append(cols[0])
    cols.append(f(inp["c_ctx"]))
    cm = np.stack([vec_pm(c) for c in cols], axis=2)
    m["cond"] = np.ascontiguousarray(cm.reshape(128, 24))
    return m


_CACHE = {}


def get_program(NS, phases=ALL_PHASES, dump_ctx=False):
    key = (NS, tuple(phases), dump_ctx)
    if key not in _CACHE:
        _CACHE[key] = Builder(NS, phases, dump_ctx).build()
    return _CACHE[key]


def kernel(**inputs):
    B = inputs["x"].shape[0]
    NS = B // NCORES
    nc = get_program(NS)
    sh = prep_shared(inputs)
    in_maps = []
    for c in range(NCORES):
        m = dict(sh)
        m.update(prep_core(inputs, list(range(c * NS, (c + 1) * NS))))
        in_maps.append(m)
    res = run_bass_kernel_spmd(nc, in_maps, core_ids=list(range(NCORES)))
    out = np.empty((B, SEQ, D), np.float32)
    for c in range(NCORES):
        o = res.results[c]["out"]
        for i in range(NS):
            out[c * NS + i] = from_pm(o[i * 128:(i + 1) * 128], SEQ)
    return out
```

```python
import numpy as np
from contextlib import ExitStack
import concourse.bass as bass
import concourse.mybir as mybir
from concourse.bass_utils import run_bass_kernel_spmd

F32 = mybir.dt.float32
BF16 = mybir.dt.bfloat16
AF = mybir.ActivationFunctionType
ALU = mybir.AluOpType

D = 1024
SEQ = 2048
CTX = 256
DFF = 2816
NFC = 22
GRID_W = 64
ALPHA = 4.0 ** 0.25
LN_EPS = 1e-5 / (ALPHA * ALPHA)
NCORES = 8
ENGS = ("pe", "act", "dve", "pool", "sp")


class Stream:
    def __init__(self, sem):
        self.sem = sem
        self.count = 0


class Prog:
    def __init__(self, nc, stack):
        self.nc = nc
        self.stack = stack
        self.q = {e: [] for e in ENGS}
        self.cnt = {e: 0 for e in ENGS}
        self.esem = {e: stack.enter_context(nc.semaphore("es_" + e)) for e in ENGS if e != "sp"}
        self.waited = {}
        self.lastw = {}
        self.readers = {}
        self.streams = []
        self.scount = {}
        self.n_ops = 0

    def stream(self, name=None):
        s = Stream(self.stack.enter_context(self.nc.semaphore(name or ("ds%d" % len(self.streams)))))
        self.streams.append(s)
        self.scount["s_%d" % id(s)] = (lambda s=s: s.count)
        return s

    def _need(self, eng, tok, waits):
        if tok is None:
            return
        sid, sem, val, teng = tok
        if teng == eng and eng == "pe":
            return
        if teng is None:
            val = max(val, self.scount[sid]())
        k = (eng, sid)
        if self.waited.get(k, 0) >= val:
            return
        self.waited[k] = val
        waits.append((sem, val))

    def _deps(self, eng, reads, writes):
        waits = []
        for k in reads:
            self._need(eng, self.lastw.get(k), waits)
        for k in writes:
            self._need(eng, self.lastw.get(k), waits)
            for t in self.readers.get(k, ()):
                self._need(eng, t, waits)
        best = {}
        for sem, val in waits:
            if id(sem) not in best or best[id(sem)][1] < val:
                best[id(sem)] = (sem, val)
        return list(best.values())

    def _commit(self, tok, reads, writes):
        for k in reads:
            self.readers.setdefault(k, []).append(tok)
        for k in writes:
            self.lastw[k] = tok
            self.readers[k] = []

    def op(self, eng, fn, reads=(), writes=(), strict=False):
        self.group(eng, [fn], reads, writes, strict)

    def group(self, eng, fns, reads=(), writes=(), strict=False):
        waits = self._deps(eng + "_strict" if strict else eng, reads, writes)
        self.cnt[eng] += 1
        tok = ("e_" + eng, self.esem[eng], self.cnt[eng], eng)
        n = len(fns)
        for i, fn in enumerate(fns):
            self.q[eng].append((waits if i == 0 else [], fn, (self.esem[eng], 1) if i == n - 1 else None))
        self._commit(tok, reads, writes)
        self.n_ops += n

    def dma(self, eng, stream, fn, reads=(), writes=()):
        waits = self._deps(eng + "_q", reads, writes)
        stream.count += 16
        tok = ("s_%d" % id(stream), stream.sem, stream.count, None)
        self.q[eng].append((waits, fn, (stream.sem, 16)))
        self._commit(tok, reads, writes)
        self.n_ops += 1

    def barrier(self):
        toks = [("e_" + e, self.esem[e], self.cnt[e], e) for e in self.esem if self.cnt[e] > 0]
        toks += [("s_%d" % id(s), s.sem, s.count, None) for s in self.streams if s.count > 0]
        for e, ident in (("pe", "pe"), ("act", "act"), ("dve", "dve"), ("pool", "pool"), ("pool", "pool_q"),
                         ("sp", "sp_q"), ("act", "act_q")):
            waits = []
            for t in toks:
                self._need(ident, t, waits)
            if waits:
                self.q[e].append((waits, None, None))
        self.lastw = {}
        self.readers = {}

    def emit(self):
        nc = self.nc
        with nc.Block() as block:
            def run(engname):
                def body(e):
                    for waits, fn, inc in self.q[engname]:
                        for sem, val in waits:
                            e.wait_ge(sem, val)
                        if fn is not None:
                            ins = fn(e)
                            if inc is not None:
                                ins.then_inc(inc[0], inc[1])
                return body
            block.sync(run("sp"))
            block.tensor(run("pe"))
            block.scalar(run("act"))
            block.vector(run("dve"))
            block.gpsimd(run("pool"))


def to_pm(a):
    T, Dd = a.shape
    nch = Dd // 128
    return np.ascontiguousarray(a.T.reshape(nch, 128, T).transpose(1, 0, 2).reshape(128, nch * T))


def from_pm(o, T):
    nch = o.shape[1] // T
    return np.ascontiguousarray(o.reshape(128, nch, T).transpose(2, 1, 0).reshape(T, nch * 128))


def lhsT_layout(W):
    K, N = W.shape
    kc, ncc = K // 128, N // 128
    return np.ascontiguousarray(W.reshape(kc, 128, ncc, 128).transpose(2, 1, 0, 3).reshape(ncc * 128, K))


def vec_pm(v):
    return np.ascontiguousarray(v.reshape(-1, 128).T)


ALL_PHASES = ("L0F0", "L0MIX", "L0F1", "L1F0", "L1MIX", "L1F1")


class Builder:
    def __init__(self, NS=2, phases=ALL_PHASES, dump_ctx=False):
        self.NS = NS
        self.phases = phases
        self.dump_ctx = dump_ctx
        self.nc = bass.Bass("TRN2", target_bir_lowering=False)
        self.conv_done = set()
        self.scr_keys = {}
        self.conv_streams = {}
        self._gs = {}

    def dram(self, name, shape, dt, kind):
        return self.nc.dram_tensor(name, shape, dt, kind=kind).ap()

    def gs(self, name):
        if name not in self._gs:
            self._gs[name] = self.P.stream("gs_" + name)
        return self._gs[name]

    def sb(self, st, name, shape, dt):
        self.ntile = getattr(self, "ntile", 0) + 1
        return st.enter_context(self.nc.sbuf_tensor("%s_%d" % (name, self.ntile), shape, dt))

    def declare(self):
        NS = self.NS
        di = lambda n, s: self.dram(n, s, F32, "ExternalInput")
        self.xT = di("xT", [NS * 128, 8 * SEQ])
        self.cxT = di("cxT", [NS * 128, 8 * CTX])
        self.cond = di("cond", [128, 24])
        self.modw = di("modw", [2 * 72 * 128, 1024])
        self.modb3 = di("modb3", [128, 2 * 72 * 3])
        self.lng = di("lng", [128, 48])
        self.lnb = di("lnb", [128, 48])
        self.w1 = di("w1", [4 * NFC * 128, 1024])
        self.w3 = di("w3", [4 * NFC * 128, 1024])
        self.w2 = di("w2", [4 * 8 * 128, DFF])
        self.win0 = di("win0", [20 * 128, 1024])
        self.wout0 = di("wout0", [8 * 128, 1024])
        self.win1 = di("win1", [24 * 128, 1024])
        self.wout1 = di("wout1", [8 * 128, 1024])
        self.scw = di("scw", [128, 24])
        self.rpbg = di("rpbg", [128, 4 * 15 * 64])
        self.nmask = di("nmask", [128, 3840])
        self.id2 = di("id2", [128, 64])
        self.lcw = di("lcw", [128, 16])
        self.lcb = di("lcb", [128, 4])
        self.lwa = di("lwa", [2 * 4 * 128, 128])
        self.lwx = di("lwx", [2 * 4 * 128, 128])
        self.lba = di("lba", [128, 8])
        self.lbx = di("lbx", [128, 8])
        self.llam = di("llam", [128, 8])
        self.out = self.dram("out", [NS * 128, 8 * SEQ], F32, "ExternalOutput")
        if self.dump_ctx:
            self.outc = self.dram("outc", [NS * 128, 8 * CTX], F32, "ExternalOutput")
            self.dbg = self.dram("dbg", [256, 4 * SEQ], BF16, "ExternalOutput")
            self.dbg2 = self.dram("dbg2", [128, 5 * (CTX + SEQ)], F32, "ExternalOutput")
            self.dbg3 = self.dram("dbg3", [128, 8 * SEQ], F32, "ExternalOutput")
            self.dbg4 = self.dram("dbg4", [128, 8 * CTX], F32, "ExternalOutput")
        ds = lambda n, s: self.dram(n, s, BF16, "Internal")
        self.w1s = ds("w1s", [4 * NFC * 128, 1024])
        self.w3s = ds("w3s", [4 * NFC * 128, 1024])
        self.w2s = ds("w2s", [4 * 8 * 128, DFF])
        self.win0s = ds("win0s", [20 * 128, 1024])
        self.wout0s = ds("wout0s", [8 * 128, 1024])
        self.win1s = ds("win1s", [24 * 128, 1024])
        self.wout1s = ds("wout1s", [8 * 128, 1024])

    def convert(self, name, src, dst, r0, r1, step):
        P = self.P
        for a in range(r0, r1, step):
            b = min(a + step, r1)
            key = ("cv", name, a)
            if key in self.conv_done:
                continue
            self.conv_done.add(key)
            self.scr_keys.setdefault(name, []).append(("scr", name, a))
            if name not in self.conv_streams:
                self.conv_streams[name] = self.gs("cv_" + name)
            P.dma("pool", self.conv_streams[name], lambda e, a=a, b=b: e.dma_start(out=dst[a:b, :], in_=src[a:b, :]),
                  writes=[("scr", name, a)])

    def flush_conv(self):
        for p in getattr(self, "pending_conv", []):
            self.convert_for(p)
        self.pending_conv = []

    def convert_for(self, phase):
        if phase in ("L0F0", "L0F1", "L1F0", "L1F1"):
            q = {"L0F0": 0, "L0F1": 1, "L1F0": 2, "L1F1": 3}[phase]
            self.convert("w1_%d" % q, self.w1, self.w1s, q * NFC * 128, (q + 1) * NFC * 128, 704)
            self.convert("w3_%d" % q, self.w3, self.w3s, q * NFC * 128, (q + 1) * NFC * 128, 704)
            self.convert("w2_%d" % q, self.w2, self.w2s, q * 1024, (q + 1) * 1024, 256)
        elif phase == "L0MIX":
            self.convert("win0", self.win0, self.win0s, 0, 20 * 128, 640)
            self.convert("wout0", self.wout0, self.wout0s, 0, 1024, 512)
        elif phase == "L1MIX":
            self.convert("win1", self.win1, self.win1s, 0, 24 * 128, 768)
            self.convert("wout1", self.wout1, self.wout1s, 0, 1024, 512)

    def mv(self, l, n, dc, w):
        c = ((l * 72 + n * 8 + dc) * 3 + w)
        return self.MV[:, c:c + 1]

    def lnv(self, t, l, k, dc):
        c = (l * 3 + k) * 8 + dc
        return t[:, c:c + 1]

    def build(self):
        nc = self.nc
        self.declare()
        with ExitStack() as st:
            self.st = st
            P = self.P = Prog(nc, st)
            self.s_conv = P.stream("s_conv")
            self.s_const = P.stream("s_const")
            self.s_io = P.stream("s_io")
            self.s_ioc = P.stream("s_ioc")
            self.H = self.sb(st, "H", [128, 8 * SEQ], F32)
            self.HC = self.sb(st, "HC", [128, 8 * CTX], F32)
            self.MV = self.sb(st, "MV", [128, 2 * 72 * 3], F32)
            self.LNG = self.sb(st, "LNG", [128, 48], F32)
            self.LNB = self.sb(st, "LNB", [128, 48], F32)
            self.ONES = self.sb(st, "ONES", [128, 128], F32)
            self.PS = [st.enter_context(nc.psum_tensor("ps%d" % i, [128, 512], F32)) for i in range(8)]
            self.prologue()
            for s in range(self.NS):
                self.sequence(s)
            P.barrier()
            P.emit()
        return nc

    def prologue(self):
        P, nc = self.P, self.nc
        first = [p for p in ALL_PHASES if p in self.phases][0]
        P.dma("sp", self.s_const, lambda e: e.dma_start(out=self.LNG[:], in_=self.lng), writes=["LNG"])
        P.dma("sp", self.s_const, lambda e: e.dma_start(out=self.LNB[:], in_=self.lnb), writes=["LNB"])
        P.op("pool", lambda e: e.memset(self.ONES[:], 1.0), writes=["ONES"])
        with ExitStack() as sc:
            CF = self.sb(sc, "CF", [128, 24], F32)
            CS = self.sb(sc, "CS", [128, 24], BF16)
            MB = self.sb(sc, "MB", [128, 2 * 72 * 3], F32)
            G = 8
            MW = [self.sb(sc, "MW%d" % i, [128, G * 1024], BF16) for i in range(2)]
            s_mw = [P.stream("s_mw%d" % i) for i in range(2)]
            P.dma("sp", self.s_const, lambda e: e.dma_start(out=CF[:], in_=self.cond), writes=["CF"])
            P.dma("sp", self.s_const, lambda e: e.dma_start(out=MB[:], in_=self.modb3), writes=["MB"])
            P.op("act", lambda e: e.activation(out=CS[:], in_=CF[:], func=AF.Silu), reads=["CF"], writes=["CS"])
            ngrp = 2 * 72 // G
            for g in range(ngrp):
                if g == 3:
                    self.convert_for(first)
                slot = g % 2
                r0 = g * G * 128
                P.dma("pool", s_mw[slot],
                      lambda e, slot=slot, r0=r0: e.dma_start(
                          out=MW[slot][:].rearrange("p (g c) -> p g c", g=G),
                          in_=self.modw[r0:r0 + G * 128, :].rearrange("(g p) c -> p g c", p=128)),
                      writes=[("MW", slot)])
                ps = self.PS[slot]
                fns = []
                for jl in range(G):
                    for k in range(8):
                        fns.append(lambda e, slot=slot, jl=jl, k=k, ps=ps: e.matmul(
                            ps[:, jl * 3:jl * 3 + 3], MW[slot][:, jl * 1024 + k * 128: jl * 1024 + (k + 1) * 128],
                            CS[:, k * 3:k * 3 + 3], start=(k == 0), stop=(k == 7)))
                P.group("pe", fns, reads=[("MW", slot), "CS"], writes=[("ps", slot)])
                c0 = g * G * 3
                P.op("dve", lambda e, ps=ps, c0=c0: e.tensor_tensor(
                    out=self.MV[:, c0:c0 + G * 3], in0=ps[:, 0:G * 3], in1=MB[:, c0:c0 + G * 3], op=ALU.add),
                    reads=[("ps", slot), "MB"], writes=["MV"])
            for l in range(2):
                for k in range(3):
                    c0 = (l * 72 + (3 * k + 1) * 8) * 3
                    P.op("dve", lambda e, c0=c0: e.tensor_scalar_add(out=self.MV[:, c0:c0 + 24], in0=self.MV[:, c0:c0 + 24], scalar1=1.0),
                         reads=["MV"], writes=["MV"])
                    c1 = (l * 72 + (3 * k + 2) * 8) * 3
                    rw = (1.0 if k == 1 else 0.5) / ALPHA
                    P.op("dve", lambda e, c1=c1, rw=rw: e.tensor_scalar_mul(out=self.MV[:, c1:c1 + 24], in0=self.MV[:, c1:c1 + 24], scalar1=rw),
                         reads=["MV"], writes=["MV"])
            P.barrier()

    def sequence(self, s):
        P = self.P
        ph = [p for p in ALL_PHASES if p in self.phases]
        for dc in range(8):
            P.dma("sp", self.s_io, lambda e, dc=dc: e.dma_start(out=self.H[:, dc * SEQ:(dc + 1) * SEQ],
                                                              in_=self.xT[s * 128:(s + 1) * 128, dc * SEQ:(dc + 1) * SEQ]),
                  writes=[("h", dc, t) for t in range(4)])
        P.dma("sp", self.s_ioc, lambda e: e.dma_start(out=self.HC[:], in_=self.cxT[s * 128:(s + 1) * 128, :]),
              writes=[("hc", dc) for dc in range(8)])
        units = []
        for p in ph:
            if p in ("L0F0", "L0F1", "L1F0", "L1F1") and units and units[-1][0] in ("L0F1",) and p == "L1F0":
                units[-1].append(p)
            else:
                units.append([p])
        for i, u in enumerate(units):
            self.pending_conv = list(units[i + 1]) if (s == 0 and i + 1 < len(units)) else []
            if u[0] != "L0MIX":
                self.flush_conv()
            if u[0] in ("L0F0", "L0F1", "L1F0", "L1F1"):
                self.ffn_phase(s, [(int(p[1]), int(p[3]), p == "L0F0") for p in u])
            elif u[0] == "L1MIX":
                self.l1mix_phase(s)
            elif u[0] == "L0MIX":
                self.l0mix_phase(s)
        for dc in range(8):
            P.dma("sp", self.s_io, lambda e, dc=dc: e.dma_start(out=self.out[s * 128:(s + 1) * 128, dc * SEQ:(dc + 1) * SEQ],
                                                              in_=self.H[:, dc * SEQ:(dc + 1) * SEQ]),
                  reads=[("h", dc, t) for t in range(4)], writes=[("out", s, dc)])
        if self.dump_ctx:
            P.dma("sp", self.s_ioc, lambda e: e.dma_start(out=self.outc[s * 128:(s + 1) * 128, :], in_=self.HC[:]),
                  reads=[("hc", dc) for dc in range(8)], writes=[("outc", s)])

    def hview(self, stream, dc, t0, n):
        if stream == "h":
            return self.H[:, dc * SEQ + t0: dc * SEQ + t0 + n]
        return self.HC[:, dc * CTX + t0: dc * CTX + t0 + n]

    def hkeys(self, stream, dc, t0, n):
        if stream == "h":
            return [("h", dc, t) for t in range(t0 // 512, (t0 + n + 511) // 512)]
        return [("hc", dc)]

    def stats_accum(self, tl, ti, stream, dc, t0, n):
        P = self.P
        SQ, SS, QQ = tl["SQ"], tl["SS"][ti], tl["QQ"][ti]
        hv = self.hview(stream, dc, t0, n)
        hk = self.hkeys(stream, dc, t0, n)
        b = dc % 2
        if dc == 0:
            P.op("act", lambda e: e.activation(out=QQ[:, 0:n], in_=hv, func=AF.Square), reads=hk, writes=[("QQ", ti)])
            return
        P.op("act", lambda e: e.activation(out=SQ[b][:, 0:n], in_=hv, func=AF.Square), reads=hk, writes=[("SQ", b)])
        P.op("dve", lambda e: e.tensor_tensor(out=QQ[:, 0:n], in0=QQ[:, 0:n], in1=SQ[b][:, 0:n], op=ALU.add),
             reads=[("QQ", ti), ("SQ", b)], writes=[("QQ", ti)])
        if dc == 1:
            hv0 = self.hview(stream, 0, t0, n)
            P.op("pool", lambda e: e.tensor_tensor(out=SS[:, 0:n], in0=hv0, in1=hv, op=ALU.add),
                 reads=hk + self.hkeys(stream, 0, t0, n), writes=[("SS", ti)])
        else:
            P.op("pool", lambda e: e.tensor_tensor(out=SS[:, 0:n], in0=SS[:, 0:n], in1=hv, op=ALU.add),
                 reads=hk + [("SS", ti)], writes=[("SS", ti)])

    def ln_tile(self, tl, stream, l, k, w, t0, n, ti=0, pre=False):
        P = self.P
        T1, MEAN, MSQ, VAR, RSTD, EPS = tl["T1"], tl["MEAN"], tl["MSQ"], tl["VAR"], tl["RSTD"], tl["EPS"]
        SS, QQ = tl["SS"][ti], tl["QQ"][ti]
        psS, psQ = self.PS[6], self.PS[7]
        if not pre:
            for dc in range(8):
                self.stats_accum(tl, ti, stream, dc, t0, n)
        P.group("pe", [lambda e: e.matmul(psS[:, 0:n], self.ONES[:], SS[:, 0:n], start=True, stop=True)],
                reads=[("SS", ti), "ONES"], writes=[("ps", 6)])
        P.group("pe", [lambda e: e.matmul(psQ[:, 0:n], self.ONES[:], QQ[:, 0:n], start=True, stop=True)],
                reads=[("QQ", ti), "ONES"], writes=[("ps", 7)])
        P.op("dve", lambda e: e.tensor_scalar_mul(out=MEAN[:, 0:n], in0=psS[:, 0:n], scalar1=1.0 / D),
             reads=[("ps", 6)], writes=["MEAN"])
        P.op("dve", lambda e: e.tensor_tensor(out=MSQ[:, 0:n], in0=MEAN[:, 0:n], in1=MEAN[:, 0:n], op=ALU.mult),
             reads=["MEAN"], writes=["MSQ"])
        P.op("dve", lambda e: e.scalar_tensor_tensor(out=VAR[:, 0:n], in0=psQ[:, 0:n], scalar=1.0 / D, in1=MSQ[:, 0:n],
                                                     op0=ALU.mult, op1=ALU.subtract),
             reads=[("ps", 7), "MSQ"], writes=["VAR"])
        P.op("act", lambda e: e.activation(out=VAR[:, 0:n], in_=VAR[:, 0:n], func=AF.Sqrt, bias=EPS[:, 0:1]),
             reads=["VAR", "EPS"], writes=["VAR"])
        P.op("dve", lambda e: e.reciprocal(out=RSTD[:, 0:n], in_=VAR[:, 0:n]), reads=["VAR"], writes=["RSTD"])
        for dc in range(8):
            hv = self.hview(stream, dc, t0, n)
            hk = self.hkeys(stream, dc, t0, n)
            b = dc % 4
            e1 = "pool" if dc % 2 == 0 else "dve"
            P.op(e1, lambda e, hv=hv, b=b: e.tensor_tensor(out=T1[b][:, 0:n], in0=hv, in1=MEAN[:, 0:n], op=ALU.subtract),
                 reads=hk + ["MEAN"], writes=[("T1", b)])
            P.op("dve", lambda e, b=b: e.tensor_tensor(out=T1[b][:, 0:n], in0=T1[b][:, 0:n], in1=RSTD[:, 0:n], op=ALU.mult),
                 reads=[("T1", b), "RSTD"], writes=[("T1", b)])
            P.op("act", lambda e, hv=hv, b=b, dc=dc: e.activation(out=hv, in_=T1[b][:, 0:n], func=AF.Identity,
                                                                 scale=self.lnv(self.LNG, l, k, dc), bias=self.lnv(self.LNB, l, k, dc)),
                 reads=[("T1", b), "LNG", "LNB"], writes=hk)

    def ln_tiles_alloc(self, sc, ntiles=1):
        tl = {}
        tl["SQ"] = [self.sb(sc, "SQ%d" % i, [128, 512], F32) for i in range(2)]
        tl["T1"] = [self.sb(sc, "T1%d" % i, [128, 512], F32) for i in range(4)]
        for nm in ("MEAN", "MSQ", "VAR", "RSTD"):
            tl[nm] = self.sb(sc, nm, [128, 512], F32)
        tl["SS"] = [self.sb(sc, "SS%d" % i, [128, 512], F32) for i in range(ntiles)]
        tl["QQ"] = [self.sb(sc, "QQ%d" % i, [128, 512], F32) for i in range(ntiles)]
        EPS = tl["EPS"] = self.sb(sc, "EPS", [128, 1], F32)
        self.P.op("pool", lambda e: e.memset(EPS[:], LN_EPS), writes=["EPS"])
        return tl

    def ffn_phase(self, s, specs):
        P = self.P
        with ExitStack() as sc:
            TB = 1024
            Z = self.sb(sc, "Z", [128, 8 * TB], BF16)
            G = self.sb(sc, "G", [128, NFC * TB], BF16)
            W13 = [self.sb(sc, "W13_%d" % i, [128, 4 * 1024], BF16) for i in range(3)]
            W2 = [self.sb(sc, "W2_%d" % i, [128, DFF], BF16) for i in range(2)]
            SL = [self.sb(sc, "SL%d" % i, [128, 512], F32) for i in range(2)]
            tl = self.ln_tiles_alloc(sc, 2)
            s13 = [self.gs("w13_%d" % i) for i in range(len(W13))]
            s2 = [self.gs("w2_%d" % i) for i in range(len(W2))]
            st = {"g13": 0, "g2": 0, "it": 0, "itb": 0}
            blocks = []
            for (l, which, with_ctx) in specs:
                q = l * 2 + which
                k = 0 if which == 0 else 2
                if with_ctx:
                    blocks.append((l, q, k, "hc", 2, 0, CTX))
                blocks += [(l, q, k, "h", s, 0, TB), (l, q, k, "h", s, TB, TB)]
            ctxs = []
            for (l, q, k, stream, w, t0, tb) in blocks:
                ctxs.append(dict(l=l, q=q, k=k, stream=stream, w=(2 if stream == "hc" else s), t0=t0, tb=tb,
                                 Z=Z, G=G, W13=W13, W2=W2, SL=SL, tl=tl, s13=s13, s2=s2, st=st,
                                 tiles=[(a, min(512, tb - a)) for a in range(0, tb, 512)]))
            self.ffn_z(ctxs[0])
            for i, c in enumerate(ctxs):
                self.ffn_ab(c)
                if i + 1 < len(ctxs):
                    self.ffn_z(ctxs[i + 1])
                for ti, (a, n) in enumerate(c["tiles"]):
                    self.ln_tile(tl, c["stream"], c["l"], c["k"], c["w"], c["t0"] + a, n, ti=ti, pre=True)
            P.barrier()

    def ffn_z(self, c):
        P = self.P
        l, k, w, stream, t0, tb, Z = c["l"], c["k"], c["w"], c["stream"], c["t0"], c["tb"], c["Z"]
        for dc in range(8):
            hv = self.hview(stream, dc, t0, tb)
            hk = self.hkeys(stream, dc, t0, tb)
            zv = Z[:, dc * tb: (dc + 1) * tb]
            if dc % 2 == 0:
                P.op("act", lambda e, hv=hv, zv=zv, dc=dc: e.activation(out=zv, in_=hv, func=AF.Identity,
                                                                       scale=self.mv(l, 3 * k + 1, dc, w), bias=self.mv(l, 3 * k, dc, w)),
                     reads=hk + ["MV"], writes=[("Z", dc)])
            else:
                P.op("dve", lambda e, hv=hv, zv=zv, dc=dc: e.tensor_scalar(out=zv, in0=hv, scalar1=self.mv(l, 3 * k + 1, dc, w),
                                                                          scalar2=self.mv(l, 3 * k, dc, w), op0=ALU.mult, op1=ALU.add),
                     reads=hk + ["MV"], writes=[("Z", dc)])

    def ffn_ab(self, c):
        P = self.P
        l, q, k, w, stream, t0, tb = c["l"], c["q"], c["k"], c["w"], c["stream"], c["t0"], c["tb"]
        Z, G, W13, W2, SL, tl, s13, s2, st, tiles = c["Z"], c["G"], c["W13"], c["W2"], c["SL"], c["tl"], c["s13"], c["s2"], c["st"], c["tiles"]
        N13, N2 = len(W13), len(W2)

        def load13(fg):
            slot = st["g13"] % N13
            st["g13"] += 1
            r0 = (q * NFC + fg * 2) * 128
            for wi, src in enumerate((self.w1s, self.w3s)):
                P.dma("sp", s13[slot], lambda e, slot=slot, wi=wi, src=src, r0=r0: e.dma_start(
                    out=W13[slot][:, wi * 2048:(wi + 1) * 2048].rearrange("p (f c) -> p f c", f=2),
                    in_=src[r0:r0 + 256, :].rearrange("(f p) c -> p f c", p=128)),
                    reads=self.scr_keys["w1_%d" % q] + self.scr_keys["w3_%d" % q], writes=[("W13", slot)])
            return slot

        def load2(dc):
            slot = st["g2"] % N2
            st["g2"] += 1
            r0 = (q * 8 + dc) * 128
            P.dma("sp", s2[slot], lambda e, slot=slot, r0=r0: e.dma_start(out=W2[slot][:], in_=self.w2s[r0:r0 + 128, :]),
                  reads=self.scr_keys["w2_%d" % q], writes=[("W2", slot)])
            return slot

        nfg = NFC // 2
        slots13 = {0: load13(0), 1: load13(1)}
        slots2 = {}
        for fg in range(nfg):
            if fg + 2 < nfg:
                slots13[fg + 2] = load13(fg + 2)
            if fg == nfg - 2:
                slots2[0] = load2(0)
            if fg == nfg - 1:
                slots2[1] = load2(1)
            slot = slots13[fg]
            for fl in range(2):
                f = fg * 2 + fl
                for (a, n) in tiles:
                    b = st["it"] % 2
                    st["it"] += 1
                    p1, p3 = self.PS[b], self.PS[2 + b]
                    for wi, pp in ((0, p1), (1, p3)):
                        fns = []
                        for kk in range(8):
                            fns.append(lambda e, slot=slot, wi=wi, fl=fl, kk=kk, pp=pp, a=a, n=n: e.matmul(
                                pp[:, 0:n], W13[slot][:, wi * 2048 + fl * 1024 + kk * 128: wi * 2048 + fl * 1024 + (kk + 1) * 128],
                                Z[:, kk * tb + a: kk * tb + a + n], start=(kk == 0), stop=(kk == 7)))
                        P.group("pe", fns, reads=[("W13", slot)] + [("Z", kk) for kk in range(8)],
                                writes=[("ps", b if wi == 0 else 2 + b)])
                    P.op("act", lambda e, b=b, p1=p1, n=n: e.activation(out=SL[b][:, 0:n], in_=p1[:, 0:n], func=AF.Silu),
                         reads=[("ps", b)], writes=[("SL", b)])
                    P.op("dve", lambda e, b=b, p3=p3, f=f, a=a, n=n: e.tensor_tensor(
                        out=G[:, f * tb + a: f * tb + a + n], in0=SL[b][:, 0:n], in1=p3[:, 0:n], op=ALU.mult),
                        reads=[("SL", b), ("ps", 2 + b)], writes=[("G", f)])
        for dc in range(8):
            if dc + 1 < 8 and dc >= 1:
                slots2[dc + 1] = load2(dc + 1)
            slot = slots2[dc]
            for ti, (a, n) in enumerate(tiles):
                b = st["itb"] % 2
                st["itb"] += 1
                py = self.PS[4 + b]
                fns = []
                for f in range(NFC):
                    fns.append(lambda e, slot=slot, f=f, py=py, a=a, n=n: e.matmul(
                        py[:, 0:n], W2[slot][:, f * 128:(f + 1) * 128], G[:, f * tb + a: f * tb + a + n],
                        start=(f == 0), stop=(f == NFC - 1)))
                P.group("pe", fns, reads=[("W2", slot)] + [("G", f) for f in range(NFC)], writes=[("ps", 4 + b)])
                hv = self.hview(stream, dc, t0 + a, n)
                hk = self.hkeys(stream, dc, t0 + a, n)
                P.op("dve", lambda e, py=py, hv=hv, dc=dc, n=n: e.scalar_tensor_tensor(
                    out=hv, in0=py[:, 0:n], scalar=self.mv(l, 3 * k + 2, dc, w), in1=hv, op0=ALU.mult, op1=ALU.add),
                    reads=[("ps", 4 + b), "MV"] + hk, writes=hk)
                self.stats_accum(tl, ti, stream, dc, t0 + a, n)

    def l1mix_phase(self, s):
        P = self.P
        l, k, w = 1, 1, s
        with ExitStack() as sc:
            Z = self.sb(sc, "Zm", [128, 8 * SEQ], BF16)
            F = self.sb(sc, "Fm", [128, 8 * SEQ], BF16)
            sc2 = ExitStack()
            U = [self.sb(sc2, "U%d" % i, [128, SEQ + 2], F32) for i in range(2)]
            GB = [self.sb(sc2, "GB%d" % i, [128, SEQ], BF16) for i in range(2)]
            ACC = self.sb(sc2, "ACC", [128, SEQ], F32)
            XV = [self.sb(sc2, "XV%d" % i, [128, 512], F32) for i in range(2)]
            WIN = [self.sb(sc2, "WIN%d" % i, [128, 3 * 1024], BF16) for i in range(2)]
            SCW = self.sb(sc2, "SCW", [128, 24], F32)
            swin = [self.gs("win_%d" % i) for i in range(2)]
            swout = [self.gs("wout_%d" % i) for i in range(2)]
            P.dma("sp", self.s_const, lambda e: e.dma_start(out=SCW[:], in_=self.scw), writes=["SCW"])
            for i in range(2):
                P.op("pool", lambda e, i=i: e.memset(U[i][:, 0:1], 0.0), writes=[("U", i)])
                P.op("pool", lambda e, i=i: e.memset(U[i][:, SEQ + 1:SEQ + 2], 0.0), writes=[("U", i)])
            for dc in range(8):
                hv = self.hview("h", dc, 0, SEQ)
                hk = self.hkeys("h", dc, 0, SEQ)
                zv = Z[:, dc * SEQ:(dc + 1) * SEQ]
                if dc % 2 == 0:
                    P.op("act", lambda e, hv=hv, zv=zv, dc=dc: e.activation(out=zv, in_=hv, func=AF.Identity,
                                                                           scale=self.mv(l, 3 * k + 1, dc, w), bias=self.mv(l, 3 * k, dc, w)),
                         reads=hk + ["MV"], writes=[("Z", dc)])
                else:
                    P.op("dve", lambda e, hv=hv, zv=zv, dc=dc: e.tensor_scalar(out=zv, in0=hv, scalar1=self.mv(l, 3 * k + 1, dc, w),
                                                                              scalar2=self.mv(l, 3 * k, dc, w), op0=ALU.mult, op1=ALU.add),
                         reads=hk + ["MV"], writes=[("Z", dc)])

            def loadwin(dc):
                slot = dc % 2
                for j in range(3):
                    r0 = (j * 8 + dc) * 128
                    P.dma("sp", swin[slot], lambda e, slot=slot, j=j, r0=r0: e.dma_start(
                        out=WIN[slot][:, j * 1024:(j + 1) * 1024], in_=self.win1s[r0:r0 + 128, :]),
                        reads=self.scr_keys["win1"], writes=[("WIN", slot)])

            def loadwout(dc):
                slot = dc % 2
                r0 = dc * 128
                P.dma("sp", swout[slot], lambda e, slot=slot, r0=r0: e.dma_start(out=WOUT[slot][:], in_=self.wout1s[r0:r0 + 128, :]),
                      reads=self.scr_keys["wout1"], writes=[("WOUT", slot)])

            loadwin(0)
            it = 0
            for dc in range(8):
                if dc + 1 < 8:
                    loadwin(dc + 1)
                slot = dc % 2
                ub = dc % 2
                for tt in range(4):
                    pb = (it % 2) * 3
                    it += 1
                    xb = tt % 2
                    for j in range(3):
                        pp = self.PS[pb + j]
                        fns = []
                        for kk in range(8):
                            fns.append(lambda e, slot=slot, j=j, kk=kk, pp=pp, tt=tt: e.matmul(
                                pp[:, :], WIN[slot][:, j * 1024 + kk * 128: j * 1024 + (kk + 1) * 128],
                                Z[:, kk * SEQ + tt * 512: kk * SEQ + (tt + 1) * 512], start=(kk == 0), stop=(kk == 7)))
                        P.group("pe", fns, reads=[("WIN", slot)] + [("Z", kk) for kk in range(8)], writes=[("ps", pb + j)])
                    P.op("act", lambda e, pb=pb, ub=ub, tt=tt: e.activation(out=GB[ub][:, tt * 512:(tt + 1) * 512], in_=self.PS[pb][:, :], func=AF.Copy),
                         reads=[("ps", pb)], writes=[("GB", ub)])
                    P.op("act", lambda e, pb=pb, xb=xb: e.activation(out=XV[xb][:, :], in_=self.PS[pb + 2][:, :], func=AF.Copy),
                         reads=[("ps", pb + 2)], writes=[("XV", xb)])
                    P.op("dve", lambda e, pb=pb, xb=xb, ub=ub, tt=tt: e.tensor_tensor(
                        out=U[ub][:, 1 + tt * 512: 1 + (tt + 1) * 512], in0=self.PS[pb + 1][:, :], in1=XV[xb][:, :], op=ALU.mult),
                        reads=[("ps", pb + 1), ("XV", xb)], writes=[("U", ub)])
                sw = lambda tap, dc=dc: SCW[:, dc * 3 + tap: dc * 3 + tap + 1]
                P.op("act", lambda e, ub=ub, sw=sw: e.activation(out=ACC[:, :], in_=U[ub][:, 0:SEQ], func=AF.Copy, scale=sw(0)),
                     reads=[("U", ub), "SCW"], writes=["ACC"])
                for tap in (1, 2):
                    P.op("dve", lambda e, ub=ub, sw=sw, tap=tap: e.scalar_tensor_tensor(
                        out=ACC[:, :], in0=U[ub][:, tap:tap + SEQ], scalar=sw(tap), in1=ACC[:, :], op0=ALU.mult, op1=ALU.add),
                        reads=[("U", ub), "SCW", "ACC"], writes=["ACC"])
                P.op("dve", lambda e, ub=ub, dc=dc: e.tensor_tensor(out=F[:, dc * SEQ:(dc + 1) * SEQ], in0=ACC[:, :], in1=GB[ub][:, :], op=ALU.mult),
                     reads=["ACC", ("GB", ub)], writes=[("F", dc)])
            P.barrier()
            sc2.close()
            WOUT = [self.sb(sc, "WOUT%d" % i, [128, 1024], BF16) for i in range(2)]
            tl = self.ln_tiles_alloc(sc, 4)
            loadwout(0)
            it = 0
            for dc in range(8):
                if dc + 1 < 8:
                    loadwout(dc + 1)
                slot = dc % 2
                for tt in range(4):
                    b = 6 + it % 2
                    it += 1
                    py = self.PS[b]
                    fns = []
                    for fc in range(8):
                        fns.append(lambda e, slot=slot, fc=fc, py=py, tt=tt: e.matmul(
                            py[:, :], WOUT[slot][:, fc * 128:(fc + 1) * 128], F[:, fc * SEQ + tt * 512: fc * SEQ + (tt + 1) * 512],
                            start=(fc == 0), stop=(fc == 7)))
                    P.group("pe", fns, reads=[("WOUT", slot)] + [("F", fc) for fc in range(8)], writes=[("ps", b)])
                    hv = self.hview("h", dc, tt * 512, 512)
                    hk = self.hkeys("h", dc, tt * 512, 512)
                    P.op("dve", lambda e, py=py, hv=hv, dc=dc: e.scalar_tensor_tensor(
                        out=hv, in0=py[:, :], scalar=self.mv(l, 3 * k + 2, dc, w), in1=hv, op0=ALU.mult, op1=ALU.add),
                        reads=[("ps", b), "MV"] + hk, writes=hk)
                    self.stats_accum(tl, tt, "h", dc, tt * 512, 512)
            for tt in range(4):
                self.ln_tile(tl, "h", l, k, w, tt * 512, 512, ti=tt, pre=True)
            P.barrier()

    def l0mix_phase(self, s):
        P = self.P
        l, k, w = 0, 1, s
        ZT = CTX + SEQ
        with ExitStack() as sc:
            Z = self.sb(sc, "Z0", [128, 8 * ZT], BF16)
            F = self.sb(sc, "F0", [128, 4 * SEQ], BF16)
            NWS = 4
            WS = [self.sb(sc, "WS%d" % i, [128, 1024], BF16) for i in range(NWS)]
            sws = [self.gs("ws_%d" % i) for i in range(NWS)]
            WO = [self.sb(sc, "WO%d" % i, [128, 512], BF16) for i in range(2)]
            swo = [self.gs("wo_%d" % i) for i in range(2)]
            wst = {"n": 0}

            def loadw(cc):
                slot = wst["n"] % NWS
                wst["n"] += 1
                P.dma("sp", sws[slot], lambda e, slot=slot, cc=cc: e.dma_start(out=WS[slot][:], in_=self.win0s[cc * 128:(cc + 1) * 128, :]),
                      reads=self.scr_keys["win0"], writes=[("WS", slot)])
                return slot

            if self.dump_ctx:
                P.dma("sp", self.s_ioc, lambda e: e.dma_start(out=self.dbg3[:, :], in_=self.H[:]),
                      reads=[("h", dc, t) for dc in range(8) for t in range(4)], writes=["dbg3"])
                P.dma("sp", self.s_ioc, lambda e: e.dma_start(out=self.dbg4[:, :], in_=self.HC[:]),
                      reads=[("hc", dc) for dc in range(8)], writes=["dbg4"])
            for dc in range(8):
                for (stream, ww, off, n) in (("hc", 2, 0, CTX), ("h", s, CTX, SEQ)):
                    hv = self.hview(stream, dc, 0, n)
                    hk = self.hkeys(stream, dc, 0, n)
                    zv = Z[:, dc * ZT + off: dc * ZT + off + n]
                    if dc % 2 == 0:
                        P.op("act", lambda e, hv=hv, zv=zv, dc=dc, ww=ww: e.activation(
                            out=zv, in_=hv, func=AF.Identity, scale=self.mv(l, 3 * k + 1, dc, ww), bias=self.mv(l, 3 * k, dc, ww)),
                            reads=hk + ["MV"], writes=[("Z", dc)])
                    else:
                        P.op("dve", lambda e, hv=hv, zv=zv, dc=dc, ww=ww: e.tensor_scalar(
                            out=zv, in0=hv, scalar1=self.mv(l, 3 * k + 1, dc, ww), scalar2=self.mv(l, 3 * k, dc, ww),
                            op0=ALU.mult, op1=ALU.add), reads=hk + ["MV"], writes=[("Z", dc)])
            zkeys = [("Z", kk) for kk in range(8)]
            pst = {"n": 0}

            def proj(slot, c0, n, evac, bank=None):
                if bank is None:
                    bank = 2 + pst["n"] % 2
                    pst["n"] += 1
                pp = self.PS[bank]
                fns = []
                for kk in range(8):
                    fns.append(lambda e, kk=kk, pp=pp: e.matmul(
                        pp[:, 0:n], WS[slot][:, kk * 128:(kk + 1) * 128], Z[:, kk * ZT + c0: kk * ZT + c0 + n],
                        start=(kk == 0), stop=(kk == 7)))
                P.group("pe", fns, reads=[("WS", slot)] + zkeys, writes=[("ps", bank)])
                evac(pp, bank)

            def halfproj(half, tl=None):
                it = 0
                def loadwo(dc):
                    slot = dc % 2
                    P.dma("sp", swo[slot], lambda e, slot=slot, dc=dc: e.dma_start(
                        out=WO[slot][:], in_=self.wout0s[dc * 128:(dc + 1) * 128, half * 512:(half + 1) * 512]),
                        reads=self.scr_keys["wout0"], writes=[("WO", slot)])
                loadwo(0)
                for dc in range(8):
                    if dc + 1 < 8:
                        loadwo(dc + 1)
                    slot = dc % 2
                    for tt in range(4):
                        b = 2 + it % 2
                        it += 1
                        py = self.PS[b]
                        fns = []
                        for fc in range(4):
                            fns.append(lambda e, slot=slot, fc=fc, py=py, tt=tt: e.matmul(
                                py[:, :], WO[slot][:, fc * 128:(fc + 1) * 128], F[:, fc * SEQ + tt * 512: fc * SEQ + (tt + 1) * 512],
                                start=(fc == 0), stop=(fc == 3)))
                        P.group("pe", fns, reads=[("WO", slot)] + [("F", fc) for fc in range(4)], writes=[("ps", b)])
                        hv = self.hview("h", dc, tt * 512, 512)
                        hk = self.hkeys("h", dc, tt * 512, 512)
                        P.op("dve", lambda e, py=py, hv=hv, dc=dc: e.scalar_tensor_tensor(
                            out=hv, in0=py[:, :], scalar=self.mv(l, 3 * k + 2, dc, w), in1=hv, op0=ALU.mult, op1=ALU.add),
                            reads=[("ps", b), "MV"] + hk, writes=hk)
                        if tl is not None:
                            self.stats_accum(tl, tt, "h", dc, tt * 512, 512)

            with ExitStack() as sn:
                Tt = self.sb(sn, "Tt", [128, 3840], BF16)
                ID2 = self.sb(sn, "ID2", [128, 64], BF16)
                ONB = self.sb(sn, "ONB", [128, 64], BF16)
                with ExitStack() as stmp:
                    RPb = self.sb(stmp, "RPb", [128, 3840], BF16)
                    NMb = self.sb(stmp, "NMb", [128, 3840], BF16)
                    P.dma("pool", self.s_const, lambda e: e.dma_start(out=RPb[:], in_=self.rpbg), writes=["RPb"])
                    P.dma("pool", self.s_const, lambda e: e.dma_start(out=NMb[:], in_=self.nmask), writes=["NMb"])
                    P.dma("pool", self.s_const, lambda e: e.dma_start(out=ID2[:], in_=self.id2), writes=["ID2"])
                    P.op("dve", lambda e: e.tensor_tensor(out=Tt[:], in0=RPb[:], in1=NMb[:], op=ALU.add), reads=["RPb", "NMb"], writes=["Tt"])
                    P.op("pool", lambda e: e.memset(ONB[:], 1.0), writes=["ONB"])
                    P.barrier()
                QT = self.sb(sn, "QT", [128, SEQ], BF16)
                KT = self.sb(sn, "KT", [128, ZT], BF16)
                V = self.sb(sn, "V", [128, 18 * 128], BF16)
                V2 = self.sb(sn, "V2", [128, 15 * 128], BF16)
                PC = [self.sb(sn, "PC%d" % i, [128, 2 * SEQ], BF16) for i in range(2)]
                PL = [self.sb(sn, "PL%d" % i, [128, 256], BF16) for i in range(4)]
                RD = [self.sb(sn, "RD%d" % i, [128, 512], F32) for i in range(2)]
                PLS = [self.sb(sn, "PLS%d" % i, [128, 512], F32) for i in range(2)]
                for hp in range(4):
                    sq_, sk_, sv_ = loadw(hp), loadw(4 + hp), loadw(8 + hp)
                    for tt in range(4):
                        proj(sq_, CTX + tt * 512, 512, lambda pp, bank, tt=tt: P.op(
                            "act", lambda e, pp=pp, tt=tt: e.activation(out=QT[:, tt * 512:(tt + 1) * 512], in_=pp[:, :], func=AF.Copy, scale=0.125),
                            reads=[("ps", bank)], writes=["QT"]))
                    proj(sk_, 0, CTX, lambda pp, bank: P.op(
                        "dve", lambda e, pp=pp: e.tensor_copy(out=KT[:, 0:CTX], in_=pp[:, 0:CTX]), reads=[("ps", bank)], writes=["KT"]))
                    for tt in range(4):
                        proj(sk_, CTX + tt * 512, 512, lambda pp, bank, tt=tt: P.op(
                            "dve", lambda e, pp=pp, tt=tt: e.tensor_copy(out=KT[:, CTX + tt * 512: CTX + (tt + 1) * 512], in_=pp[:, :]),
                            reads=[("ps", bank)], writes=["KT"]))
                    def vproj(dst, dkey, chunks):
                        for g0 in range(0, len(chunks), 4):
                            grp = chunks[g0:g0 + 4]
                            bank = 2 + pst["n"] % 2
                            pst["n"] += 1
                            pp = self.PS[bank]
                            fns = []
                            for gi, (ci, tok0) in enumerate(grp):
                                for kk in range(8):
                                    fns.append(lambda e, gi=gi, tok0=tok0, kk=kk, pp=pp, sv_=sv_: e.matmul(
                                        pp[:, gi * 128:(gi + 1) * 128], Z[:, kk * ZT + tok0: kk * ZT + tok0 + 128],
                                        WS[sv_][:, kk * 128:(kk + 1) * 128], start=(kk == 0), stop=(kk == 7)))
                            P.group("pe", fns, reads=[("WS", sv_)] + zkeys, writes=[("ps", bank)])
                            c0 = grp[0][0]
                            nn = len(grp) * 128
                            P.op("act", lambda e, pp=pp, c0=c0, nn=nn, dst=dst: e.activation(out=dst[:, c0 * 128: c0 * 128 + nn], in_=pp[:, 0:nn], func=AF.Copy),
                                 reads=[("ps", bank)], writes=[dkey])
                    vproj(V, "V", [(ci, ci * 128) for ci in range(18)])
                    vproj(V2, "V2", [(ci, CTX + 64 + ci * 128) for ci in range(15)])
                    for hh in range(2):
                        hs = slice(hh * 64, (hh + 1) * 64)
                        for cc in range(2):
                            for qt in range(4):
                                bank = 2 + pst["n"] % 2
                                pst["n"] += 1
                                pp = self.PS[bank]
                                P.group("pe", [lambda e, pp=pp, hs=hs, cc=cc, qt=qt: e.matmul(
                                    pp[:, :], KT[hs, cc * 128:(cc + 1) * 128], QT[hs, qt * 512:(qt + 1) * 512], start=True, stop=True)],
                                    reads=["KT", "QT"], writes=[("ps", bank)])
                                P.op("act", lambda e, pp=pp, hh=hh, cc=cc, qt=qt: e.activation(
                                    out=PC[hh][:, cc * SEQ + qt * 512: cc * SEQ + (qt + 1) * 512], in_=pp[:, :], func=AF.Exp),
                                    reads=[("ps", bank)], writes=[("PC", hh)])
                    units = [(rg, hh, rr) for rg in range(4) for hh in range(2) for rr in range(8)]

                    def qk(ui):
                        rg, hh, rr = units[ui]
                        r = rg * 8 + rr
                        r0 = min(max(r - 4, 0), 24)
                        hs = slice(hh * 64, (hh + 1) * 64)
                        bank = ui % 4
                        pp = self.PS[bank]
                        o = 0
                        fns = []
                        for c in range(4):
                            k0 = CTX + (r0 + 2 * c) * 64
                            ro0 = r0 + 2 * c - r + 7
                            t0 = hp * 960 + ro0 * 64
                            fns.append(lambda e, pp=pp, hs=hs, c=c, k0=k0, r=r, o=o: e.matmul(
                                pp[:, o + c * 64: o + (c + 1) * 64], KT[hs, k0:k0 + 128], QT[hs, r * 64:(r + 1) * 64], start=True, stop=False))
                            fns.append(lambda e, pp=pp, hs=hs, c=c, t0=t0, o=o: e.matmul(
                                pp[:, o + c * 64: o + (c + 1) * 64], Tt[hs, t0:t0 + 128], ID2[hs, 0:64], start=False, stop=True))
                        P.group("pe", fns, reads=["KT", "QT", "Tt", "ID2"], writes=[("ps", bank)])
                        P.op("act", lambda e, pp=pp, bank=bank, o=o: e.activation(out=PL[bank][:, :], in_=pp[:, o:o + 256], func=AF.Exp),
                             reads=[("ps", bank)], writes=[("PL", bank)])

                    def pv(ui):
                        rg, hh, rr = units[ui]
                        r = rg * 8 + rr
                        r0 = min(max(r - 4, 0), 24)
                        bank = ui % 4
                        nb, db = 4 + (rg % 2), 6 + (rg % 2)
                        hs = slice(hh * 64, (hh + 1) * 64)
                        sl_ = (rg * 2 + hh) % 2
                        if rr == 0:
                            fns = []
                            for cc in range(2):
                                fns.append(lambda e, cc=cc, nb=nb, hs=hs, hh=hh, rg=rg: e.matmul(
                                    self.PS[nb][hs, :], V[:, cc * 128 + hh * 64: cc * 128 + (hh + 1) * 64],
                                    PC[hh][:, cc * SEQ + rg * 512: cc * SEQ + (rg + 1) * 512], start=(cc == 0), stop=False))
                            for cc in range(2):
                                fns.append(lambda e, cc=cc, db=db, hs=hs, hh=hh, rg=rg: e.matmul(
                                    self.PS[db][hs, :], ONB[:, 0:64],
                                    PC[hh][:, cc * SEQ + rg * 512: cc * SEQ + (rg + 1) * 512], start=(cc == 0), stop=False))
                            P.group("pe", fns, reads=["V", ("PC", hh), "ONB"], writes=[("ps", nb), ("ps", db)])
                        P.op("dve", lambda e, bank=bank, sl_=sl_, rr=rr: e.tensor_reduce(
                            out=PLS[sl_][:, rr * 64:(rr + 1) * 64], in_=PL[bank][:, :].rearrange("p (c q) -> p q c", c=4),
                            axis=mybir.AxisListType.X, op=ALU.add), reads=[("PL", bank)], writes=[("PLS", sl_)])
                        fns = []
                        for c in range(4):
                            if r0 % 2 == 0:
                                vsrc, ci = V, 2 + r0 // 2 + c
                            else:
                                vsrc, ci = V2, (r0 - 1) // 2 + c
                            lv = vsrc[:, ci * 128 + hh * 64: ci * 128 + (hh + 1) * 64]
                            rv = PL[bank][:, c * 64:(c + 1) * 64]
                            fns.append(lambda e, lv=lv, rv=rv, c=c, nb=nb, hs=hs, rr=rr: e.matmul(
                                self.PS[nb][hs, rr * 64:(rr + 1) * 64], lv, rv, start=False, stop=(c == 3)))
                        P.group("pe", fns, reads=["V", "V2", ("PL", bank)], writes=[("ps", nb)])
                        if rr == 7:
                            P.group("pe", [lambda e, db=db, hs=hs, sl_=sl_: e.matmul(
                                self.PS[db][hs, :], self.ONES[:, 0:64], PLS[sl_][:, :], start=False, stop=True)],
                                reads=[("PLS", sl_), "ONES"], writes=[("ps", db)])
                        if hh == 1 and rr == 7:
                            rb = rg % 2
                            P.op("dve", lambda e, rb=rb, db=db: e.reciprocal(out=RD[rb][:, :], in_=self.PS[db][:, :]),
                                 reads=[("ps", db)], writes=[("RD", rb)])
                            P.op("dve", lambda e, rb=rb, nb=nb, rg=rg, hp=hp: e.tensor_tensor(
                                out=F[:, hp * SEQ + rg * 512: hp * SEQ + (rg + 1) * 512], in0=self.PS[nb][:, :], in1=RD[rb][:, :], op=ALU.mult),
                                reads=[("ps", nb), ("RD", rb)], writes=[("F", hp)])

                    for ui in range(3):
                        qk(ui)
                    for ui in range(len(units)):
                        if ui + 3 < len(units):
                            qk(ui + 3)
                        pv(ui)
                P.barrier()
            if self.dump_ctx:
                P.dma("sp", self.s_ioc, lambda e: e.dma_start(out=self.dbg[0:128, :], in_=F[:]), reads=[("F", i) for i in range(4)], writes=["dbg0"])
                P.barrier()
            halfproj(0)
            P.barrier()

            with ExitStack() as sl:
                XRT = self.sb(sl, "XRT", [128, ZT + 8], F32)
                XC = self.sb(sl, "XC", [128, ZT], F32)
                XCB = self.sb(sl, "XCB", [128, ZT], BF16)
                GG = self.sb(sl, "GG", [128, SEQ], BF16)
                A = self.sb(sl, "A", [128, ZT], F32)
                B = self.sb(sl, "B", [128, ZT], F32)
                TR = [self.sb(sl, "TR%d" % i, [128, 512], F32) for i in range(2)]
                T2 = self.sb(sl, "T2", [128, ZT], F32)
                BD = self.sb(sl, "BD", [128, 16 * 128], BF16)
                LCW = self.sb(sl, "LCW", [128, 16], F32)
                LCB = self.sb(sl, "LCB", [128, 4], F32)
                LBA = self.sb(sl, "LBA", [128, 8], F32)
                LBX = self.sb(sl, "LBX", [128, 8], F32)
                CL = self.sb(sl, "CL", [128, 8], F32)
                CLH = self.sb(sl, "CLH", [128, 8], F32)
                QRT = self.sb(sl, "QRT", [128, 1], F32)
                P.dma("pool", self.s_const, lambda e: e.dma_start(out=BD[:, 0:1024].rearrange("p (g c) -> p g c", g=8),
                                                                  in_=self.lwa.rearrange("(g p) c -> p g c", p=128)), writes=["BD"])
                P.dma("pool", self.s_const, lambda e: e.dma_start(out=BD[:, 1024:2048].rearrange("p (g c) -> p g c", g=8),
                                                                  in_=self.lwx.rearrange("(g p) c -> p g c", p=128)), writes=["BD"])
                for (t, src, nm) in ((LCW, self.lcw, "LCW"), (LCB, self.lcb, "LCB"), (LBA, self.lba, "LBA"), (LBX, self.lbx, "LBX"), (CL, self.llam, "CL")):
                    P.dma("sp", self.s_const, lambda e, t=t, src=src: e.dma_start(out=t[:], in_=src), writes=[nm])
                self.flush_conv()
                P.op("pool", lambda e: e.memset(QRT[:], 0.25), writes=["QRT"])
                P.op("act", lambda e: e.activation(out=CL[:], in_=CL[:], func=AF.Exp, scale=-1.0), reads=["CL"], writes=["CL"])
                P.op("dve", lambda e: e.tensor_scalar_add(out=CL[:], in0=CL[:], scalar1=1.0), reads=["CL"], writes=["CL"])
                P.op("act", lambda e: e.activation(out=CL[:], in_=CL[:], func=AF.Ln), reads=["CL"], writes=["CL"])
                P.op("dve", lambda e: e.tensor_scalar_mul(out=CLH[:], in0=CL[:], scalar1=-4.0), reads=["CL"], writes=["CLH"])
                P.op("dve", lambda e: e.tensor_scalar_mul(out=CL[:], in0=CL[:], scalar1=-8.0), reads=["CL", "CLH"], writes=["CL"])
                P.op("dve", lambda e: e.tensor_scalar_mul(out=LBA[:], in0=LBA[:], scalar1=0.5), reads=["LBA"], writes=["LBA"])
                P.op("dve", lambda e: e.tensor_scalar_mul(out=LBX[:], in0=LBX[:], scalar1=0.5), reads=["LBX"], writes=["LBX"])
                segs = ((0, 0, CTX), (CTX + 4, CTX, SEQ))
                tiles5 = [(0, CTX)] + [(CTX + tt * 512, 512) for tt in range(4)]
                gst = {"n": 0}
                GK = 0.7978845608028654
                for j in range(4):
                    sx, sg_ = loadw(12 + j), loadw(16 + j)
                    for (xb, cb_, n) in segs:
                        P.op("pool", lambda e, xb=xb: e.memset(XRT[:, xb:xb + 2], 0.0), writes=["XRT"])
                        P.op("pool", lambda e, xb=xb, n=n: e.memset(XRT[:, xb + 2 + n: xb + 4 + n], 0.0), writes=["XRT"])
                    proj(sx, 0, CTX, lambda pp, bank: P.op(
                        "dve", lambda e, pp=pp: e.tensor_copy(out=XRT[:, 2:2 + CTX], in_=pp[:, 0:CTX]), reads=[("ps", bank)], writes=["XRT"]))
                    for tt in range(4):
                        proj(sx, CTX + tt * 512, 512, lambda pp, bank, tt=tt: P.op(
                            "dve", lambda e, pp=pp, tt=tt: e.tensor_copy(out=XRT[:, CTX + 6 + tt * 512: CTX + 6 + (tt + 1) * 512], in_=pp[:, :]),
                            reads=[("ps", bank)], writes=["XRT"]))
                    for tt in range(4):
                        def gevac(pp, bank, tt=tt):
                            b = gst["n"] % 2
                            gst["n"] += 1
                            P.op("act", lambda e, pp=pp, b=b: e.activation(out=TR[b][:, :], in_=pp[:, :], func=AF.Square), reads=[("ps", bank)], writes=[("TR", b)])
                            P.op("dve", lambda e, b=b: e.tensor_scalar(out=TR[b][:, :], in0=TR[b][:, :], scalar1=0.044715, scalar2=1.0, op0=ALU.mult, op1=ALU.add),
                                 reads=[("TR", b)], writes=[("TR", b)])
                            P.op("dve", lambda e, pp=pp, b=b: e.tensor_tensor(out=TR[b][:, :], in0=TR[b][:, :], in1=pp[:, :], op=ALU.mult),
                                 reads=[("TR", b), ("ps", bank)], writes=[("TR", b)])
                            P.op("act", lambda e, b=b: e.activation(out=TR[b][:, :], in_=TR[b][:, :], func=AF.Tanh, scale=GK), reads=[("TR", b)], writes=[("TR", b)])
                            P.op("dve", lambda e, pp=pp, b=b, tt=tt: e.scalar_tensor_tensor(out=GG[:, tt * 512:(tt + 1) * 512], in0=TR[b][:, :], scalar=1.0, in1=pp[:, :],
                                                                                           op0=ALU.add, op1=ALU.mult),
                                 reads=[("TR", b), ("ps", bank)], writes=["GG"])
                        proj(sg_, CTX + tt * 512, 512, gevac)
                    for (xb, cb_, n) in segs:
                        P.op("act", lambda e, xb=xb, cb_=cb_, n=n, j=j: e.activation(
                            out=XC[:, cb_:cb_ + n], in_=XRT[:, xb:xb + n], func=AF.Identity, scale=LCW[:, j * 4:j * 4 + 1], bias=LCB[:, j:j + 1]),
                            reads=["XRT", "LCW", "LCB"], writes=["XC"])
                        for tap in (1, 2, 3):
                            P.op("dve", lambda e, xb=xb, cb_=cb_, n=n, j=j, tap=tap: e.scalar_tensor_tensor(
                                out=XC[:, cb_:cb_ + n], in0=XRT[:, xb + tap: xb + tap + n], scalar=LCW[:, j * 4 + tap: j * 4 + tap + 1],
                                in1=XC[:, cb_:cb_ + n], op0=ALU.mult, op1=ALU.add), reads=["XRT", "LCW", "XC"], writes=["XC"])
                    P.op("act", lambda e: e.activation(out=XCB[:, :], in_=XC[:, :], func=AF.Copy), reads=["XC"], writes=["XCB"])
                    for d in range(2):
                        col = d * 4 + j
                        for (c0, n) in tiles5:
                            b = gst["n"] % 2
                            gst["n"] += 1
                            ba, bx = 2 + b, 4 + b
                            for (bank, kind) in ((ba, 0), (bx, 1)):
                                P.group("pe", [lambda e, bank=bank, kind=kind, col=col, c0=c0, n=n: e.matmul(
                                    self.PS[bank][:, 0:n], BD[:, (kind * 8 + col) * 128:(kind * 8 + col + 1) * 128], XCB[:, c0:c0 + n], start=True, stop=True)],
                                    reads=["BD", "XCB"], writes=[("ps", bank)])
                            o0 = c0 if d == 0 else (c0 - CTX if c0 >= CTX else SEQ)
                            P.op("act", lambda e, ba=ba, b=b, n=n, col=col: e.activation(out=TR[b][:, 0:n], in_=self.PS[ba][:, 0:n], func=AF.Tanh, scale=0.5, bias=LBA[:, col:col + 1]),
                                 reads=[("ps", ba), "LBA"], writes=[("TR", b)])
                            P.op("act", lambda e, bx=bx, n=n, col=col, o0=o0: e.activation(out=T2[:, o0:o0 + n], in_=self.PS[bx][:, 0:n], func=AF.Tanh, scale=0.5, bias=LBX[:, col:col + 1]),
                                 reads=[("ps", bx), "LBX"], writes=["T2"])
                            P.op("act", lambda e, b=b, n=n, col=col, o0=o0: e.activation(out=A[:, o0:o0 + n], in_=TR[b][:, 0:n], func=AF.Exp, scale=CLH[:, col:col + 1], bias=CLH[:, col:col + 1]),
                                 reads=[("TR", b), "CLH"], writes=["A"])
                            P.op("act", lambda e, b=b, n=n, col=col, o0=o0: e.activation(out=B[:, o0:o0 + n], in_=TR[b][:, 0:n], func=AF.Exp, scale=CL[:, col:col + 1], bias=CL[:, col:col + 1]),
                                 reads=[("TR", b), "CL"], writes=["B"])
                        P.op("act", lambda e: e.activation(out=B[:, :], in_=B[:, :], func=AF.Sqrt, scale=-0.25, bias=QRT[:, 0:1]), reads=["B", "QRT"], writes=["B"])
                        P.op("dve", lambda e: e.scalar_tensor_tensor(out=B[:, :], in0=T2[:, :], scalar=1.0, in1=B[:, :], op0=ALU.add, op1=ALU.mult),
                             reads=["T2", "B"], writes=["B"])
                        if d == 0:
                            P.op("dve", lambda e: e.tensor_tensor(out=B[:, :], in0=B[:, :], in1=XC[:, :], op=ALU.mult), reads=["B", "XC"], writes=["B"])
                            P.op("dve", lambda e: e.tensor_tensor_scan(out=XRT[:, 0:ZT], data0=A[:, 0:ZT], data1=B[:, 0:ZT], initial=0.0,
                                                                       op0=ALU.mult, op1=ALU.add), reads=["A", "B"], writes=["XRT"])
                        else:
                            P.op("dve", lambda e: e.tensor_tensor(out=B[:, 0:SEQ], in0=B[:, 0:SEQ], in1=XC[:, CTX:ZT], op=ALU.mult), reads=["B", "XC"], writes=["B"])
                            P.op("dve", lambda e: e.tensor_tensor(out=B[:, SEQ:ZT], in0=B[:, SEQ:ZT], in1=XC[:, 0:CTX], op=ALU.mult), reads=["B", "XC"], writes=["B"])
                            P.op("dve", lambda e: e.tensor_tensor_scan(out=XC[:, 0:ZT][:, ::-1], data0=A[:, 0:ZT][:, ::-1], data1=B[:, 0:ZT][:, ::-1],
                                                                       initial=0.0, op0=ALU.mult, op1=ALU.add), reads=["A", "B"], writes=["XC"])
                    P.op("dve", lambda e: e.tensor_tensor(out=A[:, 0:SEQ], in0=XRT[:, CTX:ZT], in1=XC[:, 0:SEQ], op=ALU.add),
                         reads=["XRT", "XC", "A"], writes=["A"], strict=True)
                    P.op("dve", lambda e, j=j: e.scalar_tensor_tensor(out=F[:, j * SEQ:(j + 1) * SEQ], in0=A[:, 0:SEQ], scalar=0.5, in1=GG[:, :],
                                                                      op0=ALU.mult, op1=ALU.mult),
                         reads=["A", "GG"], writes=[("F", j)])
                P.barrier()
            if self.dump_ctx:
                P.dma("sp", self.s_ioc, lambda e: e.dma_start(out=self.dbg[128:256, :], in_=F[:]), reads=[("F", i) for i in range(4)], writes=["dbg1"])
                P.barrier()
            with ExitStack() as sln:
                tl = self.ln_tiles_alloc(sln, 4)
                halfproj(1, tl)
                for tt in range(4):
                    self.ln_tile(tl, "h", l, k, w, tt * 512, 512, ti=tt, pre=True)
                P.barrier()


def prep_shared(inp):
    f = lambda a: np.ascontiguousarray(np.asarray(a, dtype=np.float32))
    sh = {}
    sh["modw"] = np.concatenate([lhsT_layout(f(inp["mod_w"][l])) for l in range(2)], axis=0)
    mb = np.concatenate([vec_pm(f(inp["mod_b"][l])) for l in range(2)], axis=1)
    sh["modb3"] = np.ascontiguousarray(np.repeat(mb, 3, axis=1))
    sh["lng"] = np.concatenate([vec_pm(f(inp["ln_g"][l, k])) for l in range(2) for k in range(3)], axis=1)
    sh["lnb"] = np.concatenate([vec_pm(f(inp["ln_b"][l, k])) for l in range(2) for k in range(3)], axis=1)
    sh["w1"] = np.concatenate([lhsT_layout(f(inp["ffn_w1"][l, j])) for l in range(2) for j in range(2)], axis=0)
    sh["w3"] = np.concatenate([lhsT_layout(f(inp["ffn_w3"][l, j])) for l in range(2) for j in range(2)], axis=0)
    sh["w2"] = np.concatenate([lhsT_layout(f(inp["ffn_w2"][l, j])) for l in range(2) for j in range(2)], axis=0)
    sh["win0"] = lhsT_layout(f(inp["mix0_w_in"][0]))
    sh["wout0"] = lhsT_layout(f(inp["mix0_w_out"][0]))
    sh["win1"] = lhsT_layout(f(inp["mix1_w_in"][0]))
    sh["wout1"] = lhsT_layout(f(inp["mix1_w_out"][0]))
    scw = f(inp["sconv_w"][0])
    sh["scw"] = np.ascontiguousarray(np.stack([vec_pm(scw[t]) for t in range(3)], axis=2).reshape(128, 24))
    rpb = f(inp["na_rpb"][0])
    j = np.arange(64)[:, None]
    kc = np.arange(64)[None, :]
    ci = np.clip(kc - j + 15, 0, 30)
    g = rpb[:, :, ci]
    g = g.transpose(0, 2, 1, 3)
    rp = np.zeros((128, 4 * 15 * 64), np.float32)
    for h in range(8):
        half, idx = h % 2, h // 2
        rp[half * 64:(half + 1) * 64, idx * 960:(idx + 1) * 960] = g[h].reshape(64, 960)
    sh["rpbg"] = rp
    start = np.clip(j - 8, 0, 48)
    valid = (kc >= start) & (kc < start + 16)
    nm = np.where(valid, 0.0, -30000.0).astype(np.float32)
    sh["nmask"] = np.ascontiguousarray(np.tile(np.concatenate([nm, nm], axis=0), (1, 60)))
    eye = np.eye(64, dtype=np.float32)
    sh["id2"] = np.ascontiguousarray(np.concatenate([eye, eye], axis=0))
    cw = f(inp["lru_conv_w"][0])
    sh["lcw"] = np.ascontiguousarray(np.stack([vec_pm(cw[t]) for t in range(4)], axis=2).reshape(128, 16))
    sh["lcb"] = vec_pm(f(inp["lru_conv_b"][0]))
    def bd(wm):
        o = np.zeros((2, 4, 128, 128), np.float32)
        for d in range(2):
            for n in range(8):
                c, hh = n // 2, n % 2
                o[d, c, hh * 64:(hh + 1) * 64, hh * 64:(hh + 1) * 64] = wm[d, n]
        return o.reshape(2 * 4 * 128, 128)
    sh["lwa"] = bd(f(inp["lru_w_a"][0]))
    sh["lwx"] = bd(f(inp["lru_w_x"][0]))
    sh["lba"] = np.concatenate([vec_pm(f(inp["lru_b_a"][0, d])) for d in range(2)], axis=1)
    sh["lbx"] = np.concatenate([vec_pm(f(inp["lru_b_x"][0, d])) for d in range(2)], axis=1)
    sh["llam"] = np.concatenate([vec_pm(f(inp["lru_lambda"][0, d])) for d in range(2)], axis=1)
    return sh


def prep_core(inp, bidx):
    f = lambda a: np.asarray(a, dtype=np.float32)
    m = {}
    m["xT"] = np.concatenate([to_pm(f(inp["x"][b])) for b in bidx], axis=0)
    m["cxT"] = np.concatenate([to_pm(f(inp["ctx"][b])) for b in bidx], axis=0)
    cols = [f(inp["c"][b]) for b in bidx]
    while len(cols) < 2:
        cols.append(cols[0])
    cols.append(f(inp["c_ctx"]))
    cm = np.stack([vec_pm(c) for c in cols], axis=2)
    m["cond"] = np.ascontiguousarray(cm.reshape(128, 24))
    return m


_CACHE = {}


def get_program(NS, phases=ALL_PHASES, dump_ctx=False):
    key = (NS, tuple(phases), dump_ctx)
    if key not in _CACHE:
        _CACHE[key] = Builder(NS, phases, dump_ctx).build()
    return _CACHE[key]


def kernel(**inputs):
    B = inputs["x"].shape[0]
    NS = B // NCORES
    nc = get_program(NS)
    sh = prep_shared(inputs)
    in_maps = []
    for c in range(NCORES):
        m = dict(sh)
        m.update(prep_core(inputs, list(range(c * NS, (c + 1) * NS))))
        in_maps.append(m)
    res = run_bass_kernel_spmd(nc, in_maps, core_ids=list(range(NCORES)))
    out = np.empty((B, SEQ, D), np.float32)
    for c in range(NCORES):
        o = res.results[c]["out"]
        for i in range(NS):
            out[c * NS + i] = from_pm(o[i * 128:(i + 1) * 128], SEQ)
    return out
```

```python
import numpy as np
from contextlib import ExitStack
import concourse.bass as bass
import concourse.mybir as mybir
from concourse.bass_utils import run_bass_kernel_spmd

F32 = mybir.dt.float32
BF16 = mybir.dt.bfloat16
AF = mybir.ActivationFunctionType
ALU = mybir.AluOpType

D = 1024
SEQ = 2048
CTX = 256
DFF = 2816
NFC = 22
GRID_W = 64
ALPHA = 4.0 ** 0.25
LN_EPS = 1e-5 / (ALPHA * ALPHA)
NCORES = 8
ENGS = ("pe", "act", "dve", "pool", "sp")


class Stream:
    def __init__(self, sem):
        self.sem = sem
        self.count = 0


class Prog:
    def __init__(self, nc, stack):
        self.nc = nc
        self.stack = stack
        self.q = {e: [] for e in ENGS}
        self.cnt = {e: 0 for e in ENGS}
        self.esem = {e: stack.enter_context(nc.semaphore("es_" + e)) for e in ENGS if e != "sp"}
        self.waited = {}
        self.lastw = {}
        self.readers = {}
        self.streams = []
        self.scount = {}
        self.n_ops = 0

    def stream(self, name=None):
        s = Stream(self.stack.enter_context(self.nc.semaphore(name or ("ds%d" % len(self.streams)))))
        self.streams.append(s)
        self.scount["s_%d" % id(s)] = (lambda s=s: s.count)
        return s

    def _need(self, eng, tok, waits):
        if tok is None:
            return
        sid, sem, val, teng = tok
        if teng == eng and eng == "pe":
            return
        if teng is None:
            val = max(val, self.scount[sid]())
        k = (eng, sid)
        if self.waited.get(k, 0) >= val:
            return
        self.waited[k] = val
        waits.append((sem, val))

    def _deps(self, eng, reads, writes):
        waits = []
        for k in reads:
            self._need(eng, self.lastw.get(k), waits)
        for k in writes:
            self._need(eng, self.lastw.get(k), waits)
            for t in self.readers.get(k, ()):
                self._need(eng, t, waits)
        best = {}
        for sem, val in waits:
            if id(sem) not in best or best[id(sem)][1] < val:
                best[id(sem)] = (sem, val)
        return list(best.values())

    def _commit(self, tok, reads, writes):
        for k in reads:
            self.readers.setdefault(k, []).append(tok)
        for k in writes:
            self.lastw[k] = tok
            self.readers[k] = []

    def op(self, eng, fn, reads=(), writes=(), strict=False):
        self.group(eng, [fn], reads, writes, strict)

    def group(self, eng, fns, reads=(), writes=(), strict=False):
        waits = self._deps(eng + "_strict" if strict else eng, reads, writes)
        self.cnt[eng] += 1
        tok = ("e_" + eng, self.esem[eng], self.cnt[eng], eng)
        n = len(fns)
        for i, fn in enumerate(fns):
            self.q[eng].append((waits if i == 0 else [], fn, (self.esem[eng], 1) if i == n - 1 else None))
        self._commit(tok, reads, writes)
        self.n_ops += n

    def dma(self, eng, stream, fn, reads=(), writes=()):
        waits = self._deps(eng + "_q", reads, writes)
        stream.count += 16
        tok = ("s_%d" % id(stream), stream.sem, stream.count, None)
        self.q[eng].append((waits, fn, (stream.sem, 16)))
        self._commit(tok, reads, writes)
        self.n_ops += 1

    def barrier(self):
        toks = [("e_" + e, self.esem[e], self.cnt[e], e) for e in self.esem if self.cnt[e] > 0]
        toks += [("s_%d" % id(s), s.sem, s.count, None) for s in self.streams if s.count > 0]
        for e, ident in (("pe", "pe"), ("act", "act"), ("dve", "dve"), ("pool", "pool"), ("pool", "pool_q"),
                         ("sp", "sp_q"), ("act", "act_q")):
            waits = []
            for t in toks:
                self._need(ident, t, waits)
            if waits:
                self.q[e].append((waits, None, None))
        self.lastw = {}
        self.readers = {}

    def emit(self):
        nc = self.nc
        with nc.Block() as block:
            def run(engname):
                def body(e):
                    for waits, fn, inc in self.q[engname]:
                        for sem, val in waits:
                            e.wait_ge(sem, val)
                        if fn is not None:
                            ins = fn(e)
                            if inc is not None:
                                ins.then_inc(inc[0], inc[1])
                return body
            block.sync(run("sp"))
            block.tensor(run("pe"))
            block.scalar(run("act"))
            block.vector(run("dve"))
            block.gpsimd(run("pool"))


def to_pm(a):
    T, Dd = a.shape
    nch = Dd // 128
    return np.ascontiguousarray(a.T.reshape(nch, 128, T).transpose(1, 0, 2).reshape(128, nch * T))


def from_pm(o, T):
    nch = o.shape[1] // T
    return np.ascontiguousarray(o.reshape(128, nch, T).transpose(2, 1, 0).reshape(T, nch * 128))


def lhsT_layout(W):
    K, N = W.shape
    kc, ncc = K // 128, N // 128
    return np.ascontiguousarray(W.reshape(kc, 128, ncc, 128).transpose(2, 1, 0, 3).reshape(ncc * 128, K))


def vec_pm(v):
    return np.ascontiguousarray(v.reshape(-1, 128).T)


ALL_PHASES = ("L0F0", "L0MIX", "L0F1", "L1F0", "L1MIX", "L1F1")


class Builder:
    def __init__(self, NS=2, phases=ALL_PHASES, dump_ctx=False):
        self.NS = NS
        self.phases = phases
        self.dump_ctx = dump_ctx
        self.nc = bass.Bass("TRN2", target_bir_lowering=False)
        self.conv_done = set()
        self.scr_keys = {}
        self.conv_streams = {}
        self._gs = {}

    def dram(self, name, shape, dt, kind):
        return self.nc.dram_tensor(name, shape, dt, kind=kind).ap()

    def gs(self, name):
        if name not in self._gs:
            self._gs[name] = self.P.stream("gs_" + name)
        return self._gs[name]

    def sb(self, st, name, shape, dt):
        self.ntile = getattr(self, "ntile", 0) + 1
        return st.enter_context(self.nc.sbuf_tensor("%s_%d" % (name, self.ntile), shape, dt))

    def declare(self):
        NS = self.NS
        di = lambda n, s: self.dram(n, s, F32, "ExternalInput")
        self.xT = di("xT", [NS * 128, 8 * SEQ])
        self.cxT = di("cxT", [NS * 128, 8 * CTX])
        self.cond = di("cond", [128, 24])
        self.modw = di("modw", [2 * 72 * 128, 1024])
        self.modb3 = di("modb3", [128, 2 * 72 * 3])
        self.lng = di("lng", [128, 48])
        self.lnb = di("lnb", [128, 48])
        self.w1 = di("w1", [4 * NFC * 128, 1024])
        self.w3 = di("w3", [4 * NFC * 128, 1024])
        self.w2 = di("w2", [4 * 8 * 128, DFF])
        self.win0 = di("win0", [20 * 128, 1024])
        self.wout0 = di("wout0", [8 * 128, 1024])
        self.win1 = di("win1", [24 * 128, 1024])
        self.wout1 = di("wout1", [8 * 128, 1024])
        self.scw = di("scw", [128, 24])
        self.rpbg = di("rpbg", [128, 4 * 15 * 64])
        self.nmask = di("nmask", [128, 3840])
        self.id2 = di("id2", [128, 64])
        self.lcw = di("lcw", [128, 16])
        self.lcb = di("lcb", [128, 4])
        self.lwa = di("lwa", [2 * 4 * 128, 128])
        self.lwx = di("lwx", [2 * 4 * 128, 128])
        self.lba = di("lba", [128, 8])
        self.lbx = di("lbx", [128, 8])
        self.llam = di("llam", [128, 8])
        self.out = self.dram("out", [NS * 128, 8 * SEQ], F32, "ExternalOutput")
        if self.dump_ctx:
            self.outc = self.dram("outc", [NS * 128, 8 * CTX], F32, "ExternalOutput")
            self.dbg = self.dram("dbg", [256, 4 * SEQ], BF16, "ExternalOutput")
            self.dbg2 = self.dram("dbg2", [128, 5 * (CTX + SEQ)], F32, "ExternalOutput")
            self.dbg3 = self.dram("dbg3", [128, 8 * SEQ], F32, "ExternalOutput")
            self.dbg4 = self.dram("dbg4", [128, 8 * CTX], F32, "ExternalOutput")
        ds = lambda n, s: self.dram(n, s, BF16, "Internal")
        self.w1s = ds("w1s", [4 * NFC * 128, 1024])
        self.w3s = ds("w3s", [4 * NFC * 128, 1024])
        self.w2s = ds("w2s", [4 * 8 * 128, DFF])
        self.win0s = ds("win0s", [20 * 128, 1024])
        self.wout0s = ds("wout0s", [8 * 128, 1024])
        self.win1s = ds("win1s", [24 * 128, 1024])
        self.wout1s = ds("wout1s", [8 * 128, 1024])

    def convert(self, name, src, dst, r0, r1, step):
        P = self.P
        for a in range(r0, r1, step):
            b = min(a + step, r1)
            key = ("cv", name, a)
            if key in self.conv_done:
                continue
            self.conv_done.add(key)
            self.scr_keys.setdefault(name, []).append(("scr", name, a))
            if name not in self.conv_streams:
                self.conv_streams[name] = self.gs("cv_" + name)
            P.dma("pool", self.conv_streams[name], lambda e, a=a, b=b: e.dma_start(out=dst[a:b, :], in_=src[a:b, :]),
                  writes=[("scr", name, a)])

    def flush_conv(self):
        for p in getattr(self, "pending_conv", []):
            self.convert_for(p)
        self.pending_conv = []

    def convert_for(self, phase):
        if phase in ("L0F0", "L0F1", "L1F0", "L1F1"):
            q = {"L0F0": 0, "L0F1": 1, "L1F0": 2, "L1F1": 3}[phase]
            self.convert("w1_%d" % q, self.w1, self.w1s, q * NFC * 128, (q + 1) * NFC * 128, 704)
            self.convert("w3_%d" % q, self.w3, self.w3s, q * NFC * 128, (q + 1) * NFC * 128, 704)
            self.convert("w2_%d" % q, self.w2, self.w2s, q * 1024, (q + 1) * 1024, 256)
        elif phase == "L0MIX":
            self.convert("win0", self.win0, self.win0s, 0, 20 * 128, 640)
            self.convert("wout0", self.wout0, self.wout0s, 0, 1024, 512)
        elif phase == "L1MIX":
            self.convert("win1", self.win1, self.win1s, 0, 24 * 128, 768)
            self.convert("wout1", self.wout1, self.wout1s, 0, 1024, 512)

    def mv(self, l, n, dc, w):
        c = ((l * 72 + n * 8 + dc) * 3 + w)
        return self.MV[:, c:c + 1]

    def lnv(self, t, l, k, dc):
        c = (l * 3 + k) * 8 + dc
        return t[:, c:c + 1]

    def build(self):
        nc = self.nc
        self.declare()
        with ExitStack() as st:
            self.st = st
            P = self.P = Prog(nc, st)
            self.s_conv = P.stream("s_conv")
            self.s_const = P.stream("s_const")
            self.s_io = P.stream("s_io")
            self.s_ioc = P.stream("s_ioc")
            self.H = self.sb(st, "H", [128, 8 * SEQ], F32)
            self.HC = self.sb(st, "HC", [128, 8 * CTX], F32)
            self.MV = self.sb(st, "MV", [128, 2 * 72 * 3], F32)
            self.LNG = self.sb(st, "LNG", [128, 48], F32)
            self.LNB = self.sb(st, "LNB", [128, 48], F32)
            self.ONES = self.sb(st, "ONES", [128, 128], F32)
            self.PS = [st.enter_context(nc.psum_tensor("ps%d" % i, [128, 512], F32)) for i in range(8)]
            self.prologue()
            for s in range(self.NS):
                self.sequence(s)
            P.barrier()
            P.emit()
        return nc

    def prologue(self):
        P, nc = self.P, self.nc
        first = [p for p in ALL_PHASES if p in self.phases][0]
        P.dma("sp", self.s_const, lambda e: e.dma_start(out=self.LNG[:], in_=self.lng), writes=["LNG"])
        P.dma("sp", self.s_const, lambda e: e.dma_start(out=self.LNB[:], in_=self.lnb), writes=["LNB"])
        P.op("pool", lambda e: e.memset(self.ONES[:], 1.0), writes=["ONES"])
        with ExitStack() as sc:
            CF = self.sb(sc, "CF", [128, 24], F32)
            CS = self.sb(sc, "CS", [128, 24], BF16)
            MB = self.sb(sc, "MB", [128, 2 * 72 * 3], F32)
            G = 8
            MW = [self.sb(sc, "MW%d" % i, [128, G * 1024], BF16) for i in range(2)]
            s_mw = [P.stream("s_mw%d" % i) for i in range(2)]
            P.dma("sp", self.s_const, lambda e: e.dma_start(out=CF[:], in_=self.cond), writes=["CF"])
            P.dma("sp", self.s_const, lambda e: e.dma_start(out=MB[:], in_=self.modb3), writes=["MB"])
            P.op("act", lambda e: e.activation(out=CS[:], in_=CF[:], func=AF.Silu), reads=["CF"], writes=["CS"])
            ngrp = 2 * 72 // G
            for g in range(ngrp):
                if g == 3:
                    self.convert_for(first)
                slot = g % 2
                r0 = g * G * 128
                P.dma("pool", s_mw[slot],
                      lambda e, slot=slot, r0=r0: e.dma_start(
                          out=MW[slot][:].rearrange("p (g c) -> p g c", g=G),
                          in_=self.modw[r0:r0 + G * 128, :].rearrange("(g p) c -> p g c", p=128)),
                      writes=[("MW", slot)])
                ps = self.PS[slot]
                fns = []
                for jl in range(G):
                    for k in range(8):
                        fns.append(lambda e, slot=slot, jl=jl, k=k, ps=ps: e.matmul(
                            ps[:, jl * 3:jl * 3 + 3], MW[slot][:, jl * 1024 + k * 128: jl * 1024 + (k + 1) * 128],
                            CS[:, k * 3:k * 3 + 3], start=(k == 0), stop=(k == 7)))
                P.group("pe", fns, reads=[("MW", slot), "CS"], writes=[("ps", slot)])
                c0 = g * G * 3
                P.op("dve", lambda e, ps=ps, c0=c0: e.tensor_tensor(
                    out=self.MV[:, c0:c0 + G * 3], in0=ps[:, 0:G * 3], in1=MB[:, c0:c0 + G * 3], op=ALU.add),
                    reads=[("ps", slot), "MB"], writes=["MV"])
            for l in range(2):
                for k in range(3):
                    c0 = (l * 72 + (3 * k + 1) * 8) * 3
                    P.op("dve", lambda e, c0=c0: e.tensor_scalar_add(out=self.MV[:, c0:c0 + 24], in0=self.MV[:, c0:c0 + 24], scalar1=1.0),
                         reads=["MV"], writes=["MV"])
                    c1 = (l * 72 + (3 * k + 2) * 8) * 3
                    rw = (1.0 if k == 1 else 0.5) / ALPHA
                    P.op("dve", lambda e, c1=c1, rw=rw: e.tensor_scalar_mul(out=self.MV[:, c1:c1 + 24], in0=self.MV[:, c1:c1 + 24], scalar1=rw),
                         reads=["MV"], writes=["MV"])
            P.barrier()

    def sequence(self, s):
        P = self.P
        ph = [p for p in ALL_PHASES if p in self.phases]
        for dc in range(8):
            P.dma("sp", self.s_io, lambda e, dc=dc: e.dma_start(out=self.H[:, dc * SEQ:(dc + 1) * SEQ],
                                                              in_=self.xT[s * 128:(s + 1) * 128, dc * SEQ:(dc + 1) * SEQ]),
                  writes=[("h", dc, t) for t in range(4)])
        P.dma("sp", self.s_ioc, lambda e: e.dma_start(out=self.HC[:], in_=self.cxT[s * 128:(s + 1) * 128, :]),
              writes=[("hc", dc) for dc in range(8)])
        units = []
        for p in ph:
            if p in ("L0F0", "L0F1", "L1F0", "L1F1") and units and units[-1][0] in ("L0F1",) and p == "L1F0":
                units[-1].append(p)
            else:
                units.append([p])
        for i, u in enumerate(units):
            self.pending_conv = list(units[i + 1]) if (s == 0 and i + 1 < len(units)) else []
            if u[0] != "L0MIX":
                self.flush_conv()
            if u[0] in ("L0F0", "L0F1", "L1F0", "L1F1"):
                self.ffn_phase(s, [(int(p[1]), int(p[3]), p == "L0F0") for p in u])
            elif u[0] == "L1MIX":
                self.l1mix_phase(s)
            elif u[0] == "L0MIX":
                self.l0mix_phase(s)
        for dc in range(8):
            P.dma("sp", self.s_io, lambda e, dc=dc: e.dma_start(out=self.out[s * 128:(s + 1) * 128, dc * SEQ:(dc + 1) * SEQ],
                                                              in_=self.H[:, dc * SEQ:(dc + 1) * SEQ]),
                  reads=[("h", dc, t) for t in range(4)], writes=[("out", s, dc)])
        if self.dump_ctx:
            P.dma("sp", self.s_ioc, lambda e: e.dma_start(out=self.outc[s * 128:(s + 1) * 128, :], in_=self.HC[:]),
                  reads=[("hc", dc) for dc in range(8)], writes=[("outc", s)])

    def hview(self, stream, dc, t0, n):
        if stream == "h":
            return self.H[:, dc * SEQ + t0: dc * SEQ + t0 + n]
        return self.HC[:, dc * CTX + t0: dc * CTX + t0 + n]

    def hkeys(self, stream, dc, t0, n):
        if stream == "h":
            return [("h", dc, t) for t in range(t0 // 512, (t0 + n + 511) // 512)]
        return [("hc", dc)]

    def stats_accum(self, tl, ti, stream, dc, t0, n):
        P = self.P
        SQ, SS, QQ = tl["SQ"], tl["SS"][ti], tl["QQ"][ti]
        hv = self.hview(stream, dc, t0, n)
        hk = self.hkeys(stream, dc, t0, n)
        b = dc % 2
        if dc == 0:
            P.op("act", lambda e: e.activation(out=QQ[:, 0:n], in_=hv, func=AF.Square), reads=hk, writes=[("QQ", ti)])
            return
        P.op("act", lambda e: e.activation(out=SQ[b][:, 0:n], in_=hv, func=AF.Square), reads=hk, writes=[("SQ", b)])
        P.op("dve", lambda e: e.tensor_tensor(out=QQ[:, 0:n], in0=QQ[:, 0:n], in1=SQ[b][:, 0:n], op=ALU.add),
             reads=[("QQ", ti), ("SQ", b)], writes=[("QQ", ti)])
        if dc == 1:
            hv0 = self.hview(stream, 0, t0, n)
            P.op("pool", lambda e: e.tensor_tensor(out=SS[:, 0:n], in0=hv0, in1=hv, op=ALU.add),
                 reads=hk + self.hkeys(stream, 0, t0, n), writes=[("SS", ti)])
        else:
            P.op("pool", lambda e: e.tensor_tensor(out=SS[:, 0:n], in0=SS[:, 0:n], in1=hv, op=ALU.add),
                 reads=hk + [("SS", ti)], writes=[("SS", ti)])

    def ln_tile(self, tl, stream, l, k, w, t0, n, ti=0, pre=False, defer=False):
        P = self.P
        T1, EPS = tl["T1"], tl["EPS"]
        MEAN, RSTD = tl["MEAN"][ti], tl["RSTD"][ti]
        SS, QQ = tl["SS"][ti], tl["QQ"][ti]
        psS, psQ = self.PS[6], self.PS[7]
        if not pre:
            for dc in range(8):
                self.stats_accum(tl, ti, stream, dc, t0, n)
        P.group("pe", [lambda e: e.matmul(psS[:, 0:n], self.ONES[:], SS[:, 0:n], start=True, stop=True)],
                reads=[("SS", ti), "ONES"], writes=[("ps", 6)])
        P.group("pe", [lambda e: e.matmul(psQ[:, 0:n], self.ONES[:], QQ[:, 0:n], start=True, stop=True)],
                reads=[("QQ", ti), "ONES"], writes=[("ps", 7)])
        P.op("dve", lambda e: e.tensor_scalar_mul(out=MEAN[:, 0:n], in0=psS[:, 0:n], scalar1=1.0 / D),
             reads=[("ps", 6)], writes=[("MEAN", ti)])
        P.op("dve", lambda e: e.tensor_tensor(out=RSTD[:, 0:n], in0=MEAN[:, 0:n], in1=MEAN[:, 0:n], op=ALU.mult),
             reads=[("MEAN", ti)], writes=[("RSTD", ti)])
        P.op("dve", lambda e: e.scalar_tensor_tensor(out=RSTD[:, 0:n], in0=psQ[:, 0:n], scalar=1.0 / D, in1=RSTD[:, 0:n],
                                                     op0=ALU.mult, op1=ALU.subtract),
             reads=[("ps", 7), ("RSTD", ti)], writes=[("RSTD", ti)])
        P.op("act", lambda e: e.activation(out=RSTD[:, 0:n], in_=RSTD[:, 0:n], func=AF.Sqrt, bias=EPS[:, 0:1]),
             reads=[("RSTD", ti), "EPS"], writes=[("RSTD", ti)])
        P.op("dve", lambda e: e.reciprocal(out=RSTD[:, 0:n], in_=RSTD[:, 0:n]), reads=[("RSTD", ti)], writes=[("RSTD", ti)])

        def norm(dc):
            hv = self.hview(stream, dc, t0, n)
            hk = self.hkeys(stream, dc, t0, n)
            b = dc % 4
            e1 = "pool" if dc % 2 == 0 else "dve"
            P.op(e1, lambda e: e.tensor_tensor(out=T1[b][:, 0:n], in0=hv, in1=MEAN[:, 0:n], op=ALU.subtract),
                 reads=hk + [("MEAN", ti)], writes=[("T1", b)])
            P.op("dve", lambda e: e.tensor_tensor(out=T1[b][:, 0:n], in0=T1[b][:, 0:n], in1=RSTD[:, 0:n], op=ALU.mult),
                 reads=[("T1", b), ("RSTD", ti)], writes=[("T1", b)])
            P.op("act", lambda e: e.activation(out=hv, in_=T1[b][:, 0:n], func=AF.Identity,
                                               scale=self.lnv(self.LNG, l, k, dc), bias=self.lnv(self.LNB, l, k, dc)),
                 reads=[("T1", b), "LNG", "LNB"], writes=hk)

        todo = [(lambda dc=dc: norm(dc)) for dc in range(8)]
        if defer:
            return todo
        for f in todo:
            f()
        return []

    def ln_tiles_alloc(self, sc, ntiles=1):
        tl = {}
        tl["SQ"] = [self.sb(sc, "SQ%d" % i, [128, 512], F32) for i in range(2)]
        tl["T1"] = [self.sb(sc, "T1%d" % i, [128, 512], F32) for i in range(4)]
        tl["MEAN"] = [self.sb(sc, "MEAN%d" % i, [128, 512], F32) for i in range(max(2, ntiles))]
        tl["RSTD"] = [self.sb(sc, "RSTD%d" % i, [128, 512], F32) for i in range(max(2, ntiles))]
        tl["SS"] = [self.sb(sc, "SS%d" % i, [128, 512], F32) for i in range(ntiles)]
        tl["QQ"] = [self.sb(sc, "QQ%d" % i, [128, 512], F32) for i in range(ntiles)]
        EPS = tl["EPS"] = self.sb(sc, "EPS", [128, 1], F32)
        self.P.op("pool", lambda e: e.memset(EPS[:], LN_EPS), writes=["EPS"])
        return tl

    def ffn_phase(self, s, specs):
        P = self.P
        with ExitStack() as sc:
            TB = 1024
            Z = self.sb(sc, "Z", [128, 8 * TB], BF16)
            G = self.sb(sc, "G", [128, NFC * TB], BF16)
            W13 = [self.sb(sc, "W13_%d" % i, [128, 4 * 1024], BF16) for i in range(3)]
            W2 = [self.sb(sc, "W2_%d" % i, [128, DFF], BF16) for i in range(2)]
            SL = [self.sb(sc, "SL%d" % i, [128, 512], F32) for i in range(2)]
            tl = self.ln_tiles_alloc(sc, 2)
            s13 = [self.gs("w13_%d" % i) for i in range(len(W13))]
            s2 = [self.gs("w2_%d" % i) for i in range(len(W2))]
            st = {"g13": 0, "g2": 0, "it": 0, "itb": 0}
            blocks = []
            for (l, which, with_ctx) in specs:
                q = l * 2 + which
                k = 0 if which == 0 else 2
                if with_ctx:
                    blocks.append((l, q, k, "hc", 2, 0, CTX))
                blocks += [(l, q, k, "h", s, 0, TB), (l, q, k, "h", s, TB, TB)]
            ctxs = []
            for (l, q, k, stream, w, t0, tb) in blocks:
                ctxs.append(dict(l=l, q=q, k=k, stream=stream, w=(2 if stream == "hc" else s), t0=t0, tb=tb,
                                 Z=Z, G=G, W13=W13, W2=W2, SL=SL, tl=tl, s13=s13, s2=s2, st=st,
                                 tiles=[(a, min(512, tb - a)) for a in range(0, tb, 512)]))
            self.ffn_z(ctxs[0])
            pending = []
            for i, c in enumerate(ctxs):
                nxt = ctxs[i + 1] if i + 1 < len(ctxs) else None
                self.ffn_ab(c, pending, (lambda nxt=nxt: self.ffn_z(nxt)) if nxt is not None else None)
                while pending:
                    pending.pop(0)()
                for ti, (a, n) in enumerate(c["tiles"]):
                    pending += self.ln_tile(tl, c["stream"], c["l"], c["k"], c["w"], c["t0"] + a, n, ti=ti, pre=True, defer=True)
            while pending:
                pending.pop(0)()
            P.barrier()

    def ffn_z(self, c):
        P = self.P
        l, k, w, stream, t0, tb, Z = c["l"], c["k"], c["w"], c["stream"], c["t0"], c["tb"], c["Z"]
        for dc in range(8):
            hv = self.hview(stream, dc, t0, tb)
            hk = self.hkeys(stream, dc, t0, tb)
            zv = Z[:, dc * tb: (dc + 1) * tb]
            if dc % 2 == 0:
                P.op("act", lambda e, hv=hv, zv=zv, dc=dc: e.activation(out=zv, in_=hv, func=AF.Identity,
                                                                       scale=self.mv(l, 3 * k + 1, dc, w), bias=self.mv(l, 3 * k, dc, w)),
                     reads=hk + ["MV"], writes=[("Z", dc)])
            else:
                P.op("dve", lambda e, hv=hv, zv=zv, dc=dc: e.tensor_scalar(out=zv, in0=hv, scalar1=self.mv(l, 3 * k + 1, dc, w),
                                                                          scalar2=self.mv(l, 3 * k, dc, w), op0=ALU.mult, op1=ALU.add),
                     reads=hk + ["MV"], writes=[("Z", dc)])

    def ffn_ab(self, c, pending=(), mid=None):
        P = self.P
        l, q, k, w, stream, t0, tb = c["l"], c["q"], c["k"], c["w"], c["stream"], c["t0"], c["tb"]
        Z, G, W13, W2, SL, tl, s13, s2, st, tiles = c["Z"], c["G"], c["W13"], c["W2"], c["SL"], c["tl"], c["s13"], c["s2"], c["st"], c["tiles"]
        N13, N2 = len(W13), len(W2)

        def load13(fg):
            slot = st["g13"] % N13
            st["g13"] += 1
            r0 = (q * NFC + fg * 2) * 128
            for wi, src in enumerate((self.w1s, self.w3s)):
                P.dma("sp", s13[slot], lambda e, slot=slot, wi=wi, src=src, r0=r0: e.dma_start(
                    out=W13[slot][:, wi * 2048:(wi + 1) * 2048].rearrange("p (f c) -> p f c", f=2),
                    in_=src[r0:r0 + 256, :].rearrange("(f p) c -> p f c", p=128)),
                    reads=self.scr_keys["w1_%d" % q] + self.scr_keys["w3_%d" % q], writes=[("W13", slot)])
            return slot

        def load2(dc):
            slot = st["g2"] % N2
            st["g2"] += 1
            r0 = (q * 8 + dc) * 128
            P.dma("sp", s2[slot], lambda e, slot=slot, r0=r0: e.dma_start(out=W2[slot][:], in_=self.w2s[r0:r0 + 128, :]),
                  reads=self.scr_keys["w2_%d" % q], writes=[("W2", slot)])
            return slot

        nfg = NFC // 2
        slots13 = {0: load13(0), 1: load13(1)}
        slots2 = {}
        for fg in range(nfg):
            if fg + 2 < nfg:
                slots13[fg + 2] = load13(fg + 2)
            if fg == nfg - 2:
                slots2[0] = load2(0)
            if fg == nfg - 1:
                slots2[1] = load2(1)
            slot = slots13[fg]
            for fl in range(2):
                f = fg * 2 + fl
                for (a, n) in tiles:
                    b = st["it"] % 2
                    st["it"] += 1
                    p1, p3 = self.PS[b], self.PS[2 + b]
                    for wi, pp in ((0, p1), (1, p3)):
                        fns = []
                        for kk in range(8):
                            fns.append(lambda e, slot=slot, wi=wi, fl=fl, kk=kk, pp=pp, a=a, n=n: e.matmul(
                                pp[:, 0:n], W13[slot][:, wi * 2048 + fl * 1024 + kk * 128: wi * 2048 + fl * 1024 + (kk + 1) * 128],
                                Z[:, kk * tb + a: kk * tb + a + n], start=(kk == 0), stop=(kk == 7)))
                        P.group("pe", fns, reads=[("W13", slot)] + [("Z", kk) for kk in range(8)],
                                writes=[("ps", b if wi == 0 else 2 + b)])
                    P.op("act", lambda e, b=b, p1=p1, n=n: e.activation(out=SL[b][:, 0:n], in_=p1[:, 0:n], func=AF.Silu),
                         reads=[("ps", b)], writes=[("SL", b)])
                    P.op("dve", lambda e, b=b, p3=p3, f=f, a=a, n=n: e.tensor_tensor(
                        out=G[:, f * tb + a: f * tb + a + n], in0=SL[b][:, 0:n], in1=p3[:, 0:n], op=ALU.mult),
                        reads=[("SL", b), ("ps", 2 + b)], writes=[("G", f)])
                    if pending:
                        pending.pop(0)()
        if mid is not None:
            mid()
        for dc in range(8):
            if dc + 1 < 8 and dc >= 1:
                slots2[dc + 1] = load2(dc + 1)
            slot = slots2[dc]
            for ti, (a, n) in enumerate(tiles):
                b = st["itb"] % 2
                st["itb"] += 1
                py = self.PS[4 + b]
                fns = []
                for f in range(NFC):
                    fns.append(lambda e, slot=slot, f=f, py=py, a=a, n=n: e.matmul(
                        py[:, 0:n], W2[slot][:, f * 128:(f + 1) * 128], G[:, f * tb + a: f * tb + a + n],
                        start=(f == 0), stop=(f == NFC - 1)))
                P.group("pe", fns, reads=[("W2", slot)] + [("G", f) for f in range(NFC)], writes=[("ps", 4 + b)])
                hv = self.hview(stream, dc, t0 + a, n)
                hk = self.hkeys(stream, dc, t0 + a, n)
                P.op("dve", lambda e, py=py, hv=hv, dc=dc, n=n: e.scalar_tensor_tensor(
                    out=hv, in0=py[:, 0:n], scalar=self.mv(l, 3 * k + 2, dc, w), in1=hv, op0=ALU.mult, op1=ALU.add),
                    reads=[("ps", 4 + b), "MV"] + hk, writes=hk)
                self.stats_accum(tl, ti, stream, dc, t0 + a, n)

    def l1mix_phase(self, s):
        P = self.P
        l, k, w = 1, 1, s
        with ExitStack() as sc:
            Z = self.sb(sc, "Zm", [128, 8 * SEQ], BF16)
            F = self.sb(sc, "Fm", [128, 8 * SEQ], BF16)
            sc2 = ExitStack()
            U = [self.sb(sc2, "U%d" % i, [128, SEQ + 2], F32) for i in range(2)]
            GB = [self.sb(sc2, "GB%d" % i, [128, SEQ], BF16) for i in range(2)]
            ACC = self.sb(sc2, "ACC", [128, SEQ], F32)
            XV = [self.sb(sc2, "XV%d" % i, [128, 512], F32) for i in range(2)]
            WIN = [self.sb(sc2, "WIN%d" % i, [128, 3 * 1024], BF16) for i in range(2)]
            SCW = self.sb(sc2, "SCW", [128, 24], F32)
            swin = [self.gs("win_%d" % i) for i in range(2)]
            swout = [self.gs("wout_%d" % i) for i in range(2)]
            P.dma("sp", self.s_const, lambda e: e.dma_start(out=SCW[:], in_=self.scw), writes=["SCW"])
            for i in range(2):
                P.op("pool", lambda e, i=i: e.memset(U[i][:, 0:1], 0.0), writes=[("U", i)])
                P.op("pool", lambda e, i=i: e.memset(U[i][:, SEQ + 1:SEQ + 2], 0.0), writes=[("U", i)])
            for dc in range(8):
                hv = self.hview("h", dc, 0, SEQ)
                hk = self.hkeys("h", dc, 0, SEQ)
                zv = Z[:, dc * SEQ:(dc + 1) * SEQ]
                if dc % 2 == 0:
                    P.op("act", lambda e, hv=hv, zv=zv, dc=dc: e.activation(out=zv, in_=hv, func=AF.Identity,
                                                                           scale=self.mv(l, 3 * k + 1, dc, w), bias=self.mv(l, 3 * k, dc, w)),
                         reads=hk + ["MV"], writes=[("Z", dc)])
                else:
                    P.op("dve", lambda e, hv=hv, zv=zv, dc=dc: e.tensor_scalar(out=zv, in0=hv, scalar1=self.mv(l, 3 * k + 1, dc, w),
                                                                              scalar2=self.mv(l, 3 * k, dc, w), op0=ALU.mult, op1=ALU.add),
                         reads=hk + ["MV"], writes=[("Z", dc)])

            def loadwin(dc):
                slot = dc % 2
                for j in range(3):
                    r0 = (j * 8 + dc) * 128
                    P.dma("sp", swin[slot], lambda e, slot=slot, j=j, r0=r0: e.dma_start(
                        out=WIN[slot][:, j * 1024:(j + 1) * 1024], in_=self.win1s[r0:r0 + 128, :]),
                        reads=self.scr_keys["win1"], writes=[("WIN", slot)])

            def loadwout(dc):
                slot = dc % 2
                r0 = dc * 128
                P.dma("sp", swout[slot], lambda e, slot=slot, r0=r0: e.dma_start(out=WOUT[slot][:], in_=self.wout1s[r0:r0 + 128, :]),
                      reads=self.scr_keys["wout1"], writes=[("WOUT", slot)])

            loadwin(0)
            it = 0
            for dc in range(8):
                if dc + 1 < 8:
                    loadwin(dc + 1)
                slot = dc % 2
                ub = dc % 2
                for tt in range(4):
                    pb = (it % 2) * 3
                    it += 1
                    xb = tt % 2
                    for j in range(3):
                        pp = self.PS[pb + j]
                        fns = []
                        for kk in range(8):
                            fns.append(lambda e, slot=slot, j=j, kk=kk, pp=pp, tt=tt: e.matmul(
                                pp[:, :], WIN[slot][:, j * 1024 + kk * 128: j * 1024 + (kk + 1) * 128],
                                Z[:, kk * SEQ + tt * 512: kk * SEQ + (tt + 1) * 512], start=(kk == 0), stop=(kk == 7)))
                        P.group("pe", fns, reads=[("WIN", slot)] + [("Z", kk) for kk in range(8)], writes=[("ps", pb + j)])
                    P.op("act", lambda e, pb=pb, ub=ub, tt=tt: e.activation(out=GB[ub][:, tt * 512:(tt + 1) * 512], in_=self.PS[pb][:, :], func=AF.Copy),
                         reads=[("ps", pb)], writes=[("GB", ub)])
                    P.op("act", lambda e, pb=pb, xb=xb: e.activation(out=XV[xb][:, :], in_=self.PS[pb + 2][:, :], func=AF.Copy),
                         reads=[("ps", pb + 2)], writes=[("XV", xb)])
                    P.op("dve", lambda e, pb=pb, xb=xb, ub=ub, tt=tt: e.tensor_tensor(
                        out=U[ub][:, 1 + tt * 512: 1 + (tt + 1) * 512], in0=self.PS[pb + 1][:, :], in1=XV[xb][:, :], op=ALU.mult),
                        reads=[("ps", pb + 1), ("XV", xb)], writes=[("U", ub)])
                sw = lambda tap, dc=dc: SCW[:, dc * 3 + tap: dc * 3 + tap + 1]
                P.op("act", lambda e, ub=ub, sw=sw: e.activation(out=ACC[:, :], in_=U[ub][:, 0:SEQ], func=AF.Copy, scale=sw(0)),
                     reads=[("U", ub), "SCW"], writes=["ACC"])
                for tap in (1, 2):
                    P.op("dve", lambda e, ub=ub, sw=sw, tap=tap: e.scalar_tensor_tensor(
                        out=ACC[:, :], in0=U[ub][:, tap:tap + SEQ], scalar=sw(tap), in1=ACC[:, :], op0=ALU.mult, op1=ALU.add),
                        reads=[("U", ub), "SCW", "ACC"], writes=["ACC"])
                P.op("dve", lambda e, ub=ub, dc=dc: e.tensor_tensor(out=F[:, dc * SEQ:(dc + 1) * SEQ], in0=ACC[:, :], in1=GB[ub][:, :], op=ALU.mult),
                     reads=["ACC", ("GB", ub)], writes=[("F", dc)])
            P.barrier()
            sc2.close()
            WOUT = [self.sb(sc, "WOUT%d" % i, [128, 1024], BF16) for i in range(2)]
            tl = self.ln_tiles_alloc(sc, 4)
            loadwout(0)
            it = 0
            for dc in range(8):
                if dc + 1 < 8:
                    loadwout(dc + 1)
                slot = dc % 2
                for tt in range(4):
                    b = 6 + it % 2
                    it += 1
                    py = self.PS[b]
                    fns = []
                    for fc in range(8):
                        fns.append(lambda e, slot=slot, fc=fc, py=py, tt=tt: e.matmul(
                            py[:, :], WOUT[slot][:, fc * 128:(fc + 1) * 128], F[:, fc * SEQ + tt * 512: fc * SEQ + (tt + 1) * 512],
                            start=(fc == 0), stop=(fc == 7)))
                    P.group("pe", fns, reads=[("WOUT", slot)] + [("F", fc) for fc in range(8)], writes=[("ps", b)])
                    hv = self.hview("h", dc, tt * 512, 512)
                    hk = self.hkeys("h", dc, tt * 512, 512)
                    P.op("dve", lambda e, py=py, hv=hv, dc=dc: e.scalar_tensor_tensor(
                        out=hv, in0=py[:, :], scalar=self.mv(l, 3 * k + 2, dc, w), in1=hv, op0=ALU.mult, op1=ALU.add),
                        reads=[("ps", b), "MV"] + hk, writes=hk)
                    self.stats_accum(tl, tt, "h", dc, tt * 512, 512)
            for tt in range(4):
                self.ln_tile(tl, "h", l, k, w, tt * 512, 512, ti=tt, pre=True)
            P.barrier()

    def l0mix_phase(self, s):
        P = self.P
        l, k, w = 0, 1, s
        ZT = CTX + SEQ
        with ExitStack() as sc:
            Z = self.sb(sc, "Z0", [128, 8 * ZT], BF16)
            F = self.sb(sc, "F0", [128, 4 * SEQ], BF16)
            NWS = 4
            WS = [self.sb(sc, "WS%d" % i, [128, 1024], BF16) for i in range(NWS)]
            sws = [self.gs("ws_%d" % i) for i in range(NWS)]
            WO = [self.sb(sc, "WO%d" % i, [128, 512], BF16) for i in range(2)]
            swo = [self.gs("wo_%d" % i) for i in range(2)]
            wst = {"n": 0}

            def loadw(cc):
                slot = wst["n"] % NWS
                wst["n"] += 1
                P.dma("sp", sws[slot], lambda e, slot=slot, cc=cc: e.dma_start(out=WS[slot][:], in_=self.win0s[cc * 128:(cc + 1) * 128, :]),
                      reads=self.scr_keys["win0"], writes=[("WS", slot)])
                return slot

            if self.dump_ctx:
                P.dma("sp", self.s_ioc, lambda e: e.dma_start(out=self.dbg3[:, :], in_=self.H[:]),
                      reads=[("h", dc, t) for dc in range(8) for t in range(4)], writes=["dbg3"])
                P.dma("sp", self.s_ioc, lambda e: e.dma_start(out=self.dbg4[:, :], in_=self.HC[:]),
                      reads=[("hc", dc) for dc in range(8)], writes=["dbg4"])
            for dc in range(8):
                for (stream, ww, off, n) in (("hc", 2, 0, CTX), ("h", s, CTX, SEQ)):
                    hv = self.hview(stream, dc, 0, n)
                    hk = self.hkeys(stream, dc, 0, n)
                    zv = Z[:, dc * ZT + off: dc * ZT + off + n]
                    if dc % 2 == 0:
                        P.op("act", lambda e, hv=hv, zv=zv, dc=dc, ww=ww: e.activation(
                            out=zv, in_=hv, func=AF.Identity, scale=self.mv(l, 3 * k + 1, dc, ww), bias=self.mv(l, 3 * k, dc, ww)),
                            reads=hk + ["MV"], writes=[("Z", dc)])
                    else:
                        P.op("dve", lambda e, hv=hv, zv=zv, dc=dc, ww=ww: e.tensor_scalar(
                            out=zv, in0=hv, scalar1=self.mv(l, 3 * k + 1, dc, ww), scalar2=self.mv(l, 3 * k, dc, ww),
                            op0=ALU.mult, op1=ALU.add), reads=hk + ["MV"], writes=[("Z", dc)])
            zkeys = [("Z", kk) for kk in range(8)]
            pst = {"n": 0}

            def proj(slot, c0, n, evac, bank=None):
                if bank is None:
                    bank = 2 + pst["n"] % 2
                    pst["n"] += 1
                pp = self.PS[bank]
                fns = []
                for kk in range(8):
                    fns.append(lambda e, kk=kk, pp=pp: e.matmul(
                        pp[:, 0:n], WS[slot][:, kk * 128:(kk + 1) * 128], Z[:, kk * ZT + c0: kk * ZT + c0 + n],
                        start=(kk == 0), stop=(kk == 7)))
                P.group("pe", fns, reads=[("WS", slot)] + zkeys, writes=[("ps", bank)])
                evac(pp, bank)

            def halfproj(half, tl=None):
                it = 0
                def loadwo(dc):
                    slot = dc % 2
                    P.dma("sp", swo[slot], lambda e, slot=slot, dc=dc: e.dma_start(
                        out=WO[slot][:], in_=self.wout0s[dc * 128:(dc + 1) * 128, half * 512:(half + 1) * 512]),
                        reads=self.scr_keys["wout0"], writes=[("WO", slot)])
                loadwo(0)
                for dc in range(8):
                    if dc + 1 < 8:
                        loadwo(dc + 1)
                    slot = dc % 2
                    for tt in range(4):
                        b = 2 + it % 2
                        it += 1
                        py = self.PS[b]
                        fns = []
                        for fc in range(4):
                            fns.append(lambda e, slot=slot, fc=fc, py=py, tt=tt: e.matmul(
                                py[:, :], WO[slot][:, fc * 128:(fc + 1) * 128], F[:, fc * SEQ + tt * 512: fc * SEQ + (tt + 1) * 512],
                                start=(fc == 0), stop=(fc == 3)))
                        P.group("pe", fns, reads=[("WO", slot)] + [("F", fc) for fc in range(4)], writes=[("ps", b)])
                        hv = self.hview("h", dc, tt * 512, 512)
                        hk = self.hkeys("h", dc, tt * 512, 512)
                        P.op("dve", lambda e, py=py, hv=hv, dc=dc: e.scalar_tensor_tensor(
                            out=hv, in0=py[:, :], scalar=self.mv(l, 3 * k + 2, dc, w), in1=hv, op0=ALU.mult, op1=ALU.add),
                            reads=[("ps", b), "MV"] + hk, writes=hk)
                        if tl is not None:
                            self.stats_accum(tl, tt, "h", dc, tt * 512, 512)

            with ExitStack() as sn:
                Tt = self.sb(sn, "Tt", [128, 3840], BF16)
                ID2 = self.sb(sn, "ID2", [128, 64], BF16)
                ONB = self.sb(sn, "ONB", [128, 64], BF16)
                with ExitStack() as stmp:
                    RPb = self.sb(stmp, "RPb", [128, 3840], BF16)
                    NMb = self.sb(stmp, "NMb", [128, 3840], BF16)
                    P.dma("pool", self.gs("constp"), lambda e: e.dma_start(out=RPb[:], in_=self.rpbg), writes=["RPb"])
                    P.dma("pool", self.gs("constp"), lambda e: e.dma_start(out=NMb[:], in_=self.nmask), writes=["NMb"])
                    P.dma("pool", self.gs("constp"), lambda e: e.dma_start(out=ID2[:], in_=self.id2), writes=["ID2"])
                    P.op("dve", lambda e: e.tensor_tensor(out=Tt[:], in0=RPb[:], in1=NMb[:], op=ALU.add), reads=["RPb", "NMb"], writes=["Tt"])
                    P.op("pool", lambda e: e.memset(ONB[:], 1.0), writes=["ONB"])
                    P.barrier()
                QT = self.sb(sn, "QT", [128, SEQ], BF16)
                KT = self.sb(sn, "KT", [128, ZT], BF16)
                V = self.sb(sn, "V", [128, 18 * 128], BF16)
                V2 = self.sb(sn, "V2", [128, 15 * 128], BF16)
                PC = [self.sb(sn, "PC%d" % i, [128, 2 * SEQ], BF16) for i in range(2)]
                PL = [self.sb(sn, "PL%d" % i, [128, 256], BF16) for i in range(4)]
                RD = [self.sb(sn, "RD%d" % i, [128, 512], F32) for i in range(2)]
                PLS = [self.sb(sn, "PLS%d" % i, [128, 512], F32) for i in range(2)]
                for hp in range(4):
                    sq_, sk_, sv_ = loadw(hp), loadw(4 + hp), loadw(8 + hp)
                    for tt in range(4):
                        proj(sq_, CTX + tt * 512, 512, lambda pp, bank, tt=tt: P.op(
                            "act", lambda e, pp=pp, tt=tt: e.activation(out=QT[:, tt * 512:(tt + 1) * 512], in_=pp[:, :], func=AF.Copy, scale=0.125),
                            reads=[("ps", bank)], writes=["QT"]))
                    proj(sk_, 0, CTX, lambda pp, bank: P.op(
                        "dve", lambda e, pp=pp: e.tensor_copy(out=KT[:, 0:CTX], in_=pp[:, 0:CTX]), reads=[("ps", bank)], writes=["KT"]))
                    for tt in range(4):
                        proj(sk_, CTX + tt * 512, 512, lambda pp, bank, tt=tt: P.op(
                            "dve", lambda e, pp=pp, tt=tt: e.tensor_copy(out=KT[:, CTX + tt * 512: CTX + (tt + 1) * 512], in_=pp[:, :]),
                            reads=[("ps", bank)], writes=["KT"]))
                    def vproj(dst, dkey, chunks):
                        for g0 in range(0, len(chunks), 4):
                            grp = chunks[g0:g0 + 4]
                            bank = 2 + pst["n"] % 2
                            pst["n"] += 1
                            pp = self.PS[bank]
                            fns = []
                            for gi, (ci, tok0) in enumerate(grp):
                                for kk in range(8):
                                    fns.append(lambda e, gi=gi, tok0=tok0, kk=kk, pp=pp, sv_=sv_: e.matmul(
                                        pp[:, gi * 128:(gi + 1) * 128], Z[:, kk * ZT + tok0: kk * ZT + tok0 + 128],
                                        WS[sv_][:, kk * 128:(kk + 1) * 128], start=(kk == 0), stop=(kk == 7)))
                            P.group("pe", fns, reads=[("WS", sv_)] + zkeys, writes=[("ps", bank)])
                            c0 = grp[0][0]
                            nn = len(grp) * 128
                            P.op("act", lambda e, pp=pp, c0=c0, nn=nn, dst=dst: e.activation(out=dst[:, c0 * 128: c0 * 128 + nn], in_=pp[:, 0:nn], func=AF.Copy),
                                 reads=[("ps", bank)], writes=[dkey])
                    vproj(V, "V", [(ci, ci * 128) for ci in range(18)])
                    vproj(V2, "V2", [(ci, CTX + 64 + ci * 128) for ci in range(15)])
                    for hh in range(2):
                        hs = slice(hh * 64, (hh + 1) * 64)
                        for cc in range(2):
                            for qt in range(4):
                                bank = 2 + pst["n"] % 2
                                pst["n"] += 1
                                pp = self.PS[bank]
                                P.group("pe", [lambda e, pp=pp, hs=hs, cc=cc, qt=qt: e.matmul(
                                    pp[:, :], KT[hs, cc * 128:(cc + 1) * 128], QT[hs, qt * 512:(qt + 1) * 512], start=True, stop=True)],
                                    reads=["KT", "QT"], writes=[("ps", bank)])
                                P.op("act", lambda e, pp=pp, hh=hh, cc=cc, qt=qt: e.activation(
                                    out=PC[hh][:, cc * SEQ + qt * 512: cc * SEQ + (qt + 1) * 512], in_=pp[:, :], func=AF.Exp),
                                    reads=[("ps", bank)], writes=[("PC", hh)])
                    units = [(rg, hh, rr) for rg in range(4) for hh in range(2) for rr in range(8)]

                    def qk(ui):
                        rg, hh, rr = units[ui]
                        r = rg * 8 + rr
                        r0 = min(max(r - 4, 0), 24)
                        hs = slice(hh * 64, (hh + 1) * 64)
                        bank = ui % 4
                        pp = self.PS[bank]
                        o = 0
                        fns = []
                        for c in range(4):
                            k0 = CTX + (r0 + 2 * c) * 64
                            ro0 = r0 + 2 * c - r + 7
                            t0 = hp * 960 + ro0 * 64
                            fns.append(lambda e, pp=pp, hs=hs, c=c, k0=k0, r=r, o=o: e.matmul(
                                pp[:, o + c * 64: o + (c + 1) * 64], KT[hs, k0:k0 + 128], QT[hs, r * 64:(r + 1) * 64], start=True, stop=False))
                            fns.append(lambda e, pp=pp, hs=hs, c=c, t0=t0, o=o: e.matmul(
                                pp[:, o + c * 64: o + (c + 1) * 64], Tt[hs, t0:t0 + 128], ID2[hs, 0:64], start=False, stop=True))
                        P.group("pe", fns, reads=["KT", "QT", "Tt", "ID2"], writes=[("ps", bank)])
                        P.op("act", lambda e, pp=pp, bank=bank, o=o: e.activation(out=PL[bank][:, :], in_=pp[:, o:o + 256], func=AF.Exp),
                             reads=[("ps", bank)], writes=[("PL", bank)])

                    def pv(ui):
                        rg, hh, rr = units[ui]
                        r = rg * 8 + rr
                        r0 = min(max(r - 4, 0), 24)
                        bank = ui % 4
                        nb, db = 4 + (rg % 2), 6 + (rg % 2)
                        hs = slice(hh * 64, (hh + 1) * 64)
                        sl_ = (rg * 2 + hh) % 2
                        if rr == 0:
                            fns = []
                            for cc in range(2):
                                fns.append(lambda e, cc=cc, nb=nb, hs=hs, hh=hh, rg=rg: e.matmul(
                                    self.PS[nb][hs, :], V[:, cc * 128 + hh * 64: cc * 128 + (hh + 1) * 64],
                                    PC[hh][:, cc * SEQ + rg * 512: cc * SEQ + (rg + 1) * 512], start=(cc == 0), stop=False))
                            for cc in range(2):
                                fns.append(lambda e, cc=cc, db=db, hs=hs, hh=hh, rg=rg: e.matmul(
                                    self.PS[db][hs, :], ONB[:, 0:64],
                                    PC[hh][:, cc * SEQ + rg * 512: cc * SEQ + (rg + 1) * 512], start=(cc == 0), stop=False))
                            P.group("pe", fns, reads=["V", ("PC", hh), "ONB"], writes=[("ps", nb), ("ps", db)])
                        P.op("dve", lambda e, bank=bank, sl_=sl_, rr=rr: e.tensor_reduce(
                            out=PLS[sl_][:, rr * 64:(rr + 1) * 64], in_=PL[bank][:, :].rearrange("p (c q) -> p q c", c=4),
                            axis=mybir.AxisListType.X, op=ALU.add), reads=[("PL", bank)], writes=[("PLS", sl_)])
                        fns = []
                        for c in range(4):
                            if r0 % 2 == 0:
                                vsrc, ci = V, 2 + r0 // 2 + c
                            else:
                                vsrc, ci = V2, (r0 - 1) // 2 + c
                            lv = vsrc[:, ci * 128 + hh * 64: ci * 128 + (hh + 1) * 64]
                            rv = PL[bank][:, c * 64:(c + 1) * 64]
                            fns.append(lambda e, lv=lv, rv=rv, c=c, nb=nb, hs=hs, rr=rr: e.matmul(
                                self.PS[nb][hs, rr * 64:(rr + 1) * 64], lv, rv, start=False, stop=(c == 3)))
                        P.group("pe", fns, reads=["V", "V2", ("PL", bank)], writes=[("ps", nb)])
                        if rr == 7:
                            P.group("pe", [lambda e, db=db, hs=hs, sl_=sl_: e.matmul(
                                self.PS[db][hs, :], self.ONES[:, 0:64], PLS[sl_][:, :], start=False, stop=True)],
                                reads=[("PLS", sl_), "ONES"], writes=[("ps", db)])
                        if hh == 1 and rr == 7:
                            rb = rg % 2
                            P.op("dve", lambda e, rb=rb, db=db: e.reciprocal(out=RD[rb][:, :], in_=self.PS[db][:, :]),
                                 reads=[("ps", db)], writes=[("RD", rb)])
                            P.op("dve", lambda e, rb=rb, nb=nb, rg=rg, hp=hp: e.tensor_tensor(
                                out=F[:, hp * SEQ + rg * 512: hp * SEQ + (rg + 1) * 512], in0=self.PS[nb][:, :], in1=RD[rb][:, :], op=ALU.mult),
                                reads=[("ps", nb), ("RD", rb)], writes=[("F", hp)])

                    for ui in range(3):
                        qk(ui)
                    for ui in range(len(units)):
                        if ui + 3 < len(units):
                            qk(ui + 3)
                        pv(ui)
                P.barrier()
            if self.dump_ctx:
                P.dma("sp", self.s_ioc, lambda e: e.dma_start(out=self.dbg[0:128, :], in_=F[:]), reads=[("F", i) for i in range(4)], writes=["dbg0"])
                P.barrier()
            halfproj(0)
            P.barrier()

            with ExitStack() as sl:
                XRT = self.sb(sl, "XRT", [128, ZT + 8], F32)
                XC = self.sb(sl, "XC", [128, ZT], F32)
                XCB = self.sb(sl, "XCB", [128, ZT], BF16)
                GG = self.sb(sl, "GG", [128, SEQ], BF16)
                A = self.sb(sl, "A", [128, ZT], F32)
                B = self.sb(sl, "B", [128, ZT], F32)
                TR = [self.sb(sl, "TR%d" % i, [128, 512], F32) for i in range(2)]
                T2 = self.sb(sl, "T2", [128, ZT], F32)
                BD = self.sb(sl, "BD", [128, 16 * 128], BF16)
                LCW = self.sb(sl, "LCW", [128, 16], F32)
                LCB = self.sb(sl, "LCB", [128, 4], F32)
                LBA = self.sb(sl, "LBA", [128, 8], F32)
                LBX = self.sb(sl, "LBX", [128, 8], F32)
                CL = self.sb(sl, "CL", [128, 8], F32)
                CLH = self.sb(sl, "CLH", [128, 8], F32)
                QRT = self.sb(sl, "QRT", [128, 1], F32)
                P.dma("pool", self.gs("constp"), lambda e: e.dma_start(out=BD[:, 0:1024].rearrange("p (g c) -> p g c", g=8),
                                                                  in_=self.lwa.rearrange("(g p) c -> p g c", p=128)), writes=["BD"])
                P.dma("pool", self.gs("constp"), lambda e: e.dma_start(out=BD[:, 1024:2048].rearrange("p (g c) -> p g c", g=8),
                                                                  in_=self.lwx.rearrange("(g p) c -> p g c", p=128)), writes=["BD"])
                for (t, src, nm) in ((LCW, self.lcw, "LCW"), (LCB, self.lcb, "LCB"), (LBA, self.lba, "LBA"), (LBX, self.lbx, "LBX"), (CL, self.llam, "CL")):
                    P.dma("sp", self.s_const, lambda e, t=t, src=src: e.dma_start(out=t[:], in_=src), writes=[nm])
                self.flush_conv()
                P.op("pool", lambda e: e.memset(QRT[:], 0.25), writes=["QRT"])
                P.op("act", lambda e: e.activation(out=CL[:], in_=CL[:], func=AF.Exp, scale=-1.0), reads=["CL"], writes=["CL"])
                P.op("dve", lambda e: e.tensor_scalar_add(out=CL[:], in0=CL[:], scalar1=1.0), reads=["CL"], writes=["CL"])
                P.op("act", lambda e: e.activation(out=CL[:], in_=CL[:], func=AF.Ln), reads=["CL"], writes=["CL"])
                P.op("dve", lambda e: e.tensor_scalar_mul(out=CLH[:], in0=CL[:], scalar1=-4.0), reads=["CL"], writes=["CLH"])
                P.op("dve", lambda e: e.tensor_scalar_mul(out=CL[:], in0=CL[:], scalar1=-8.0), reads=["CL", "CLH"], writes=["CL"])
                P.op("dve", lambda e: e.tensor_scalar_mul(out=LBA[:], in0=LBA[:], scalar1=0.5), reads=["LBA"], writes=["LBA"])
                P.op("dve", lambda e: e.tensor_scalar_mul(out=LBX[:], in0=LBX[:], scalar1=0.5), reads=["LBX"], writes=["LBX"])
                segs = ((0, 0, CTX), (CTX + 4, CTX, SEQ))
                tiles5 = [(0, CTX)] + [(CTX + tt * 512, 512) for tt in range(4)]
                gst = {"n": 0}
                GK = 0.7978845608028654
                for j in range(4):
                    sx, sg_ = loadw(12 + j), loadw(16 + j)
                    for (xb, cb_, n) in segs:
                        P.op("pool", lambda e, xb=xb: e.memset(XRT[:, xb:xb + 2], 0.0), writes=["XRT"])
                        P.op("pool", lambda e, xb=xb, n=n: e.memset(XRT[:, xb + 2 + n: xb + 4 + n], 0.0), writes=["XRT"])
                    proj(sx, 0, CTX, lambda pp, bank: P.op(
                        "dve", lambda e, pp=pp: e.tensor_copy(out=XRT[:, 2:2 + CTX], in_=pp[:, 0:CTX]), reads=[("ps", bank)], writes=["XRT"]))
                    for tt in range(4):
                        proj(sx, CTX + tt * 512, 512, lambda pp, bank, tt=tt: P.op(
                            "dve", lambda e, pp=pp, tt=tt: e.tensor_copy(out=XRT[:, CTX + 6 + tt * 512: CTX + 6 + (tt + 1) * 512], in_=pp[:, :]),
                            reads=[("ps", bank)], writes=["XRT"]))
                    for tt in range(4):
                        def gevac(pp, bank, tt=tt):
                            b = gst["n"] % 2
                            gst["n"] += 1
                            P.op("act", lambda e, pp=pp, b=b: e.activation(out=TR[b][:, :], in_=pp[:, :], func=AF.Square), reads=[("ps", bank)], writes=[("TR", b)])
                            P.op("dve", lambda e, b=b: e.tensor_scalar(out=TR[b][:, :], in0=TR[b][:, :], scalar1=0.044715, scalar2=1.0, op0=ALU.mult, op1=ALU.add),
                                 reads=[("TR", b)], writes=[("TR", b)])
                            P.op("dve", lambda e, pp=pp, b=b: e.tensor_tensor(out=TR[b][:, :], in0=TR[b][:, :], in1=pp[:, :], op=ALU.mult),
                                 reads=[("TR", b), ("ps", bank)], writes=[("TR", b)])
                            P.op("act", lambda e, b=b: e.activation(out=TR[b][:, :], in_=TR[b][:, :], func=AF.Tanh, scale=GK), reads=[("TR", b)], writes=[("TR", b)])
                            P.op("dve", lambda e, pp=pp, b=b, tt=tt: e.scalar_tensor_tensor(out=GG[:, tt * 512:(tt + 1) * 512], in0=TR[b][:, :], scalar=1.0, in1=pp[:, :],
                                                                                           op0=ALU.add, op1=ALU.mult),
                                 reads=[("TR", b), ("ps", bank)], writes=["GG"])
                        proj(sg_, CTX + tt * 512, 512, gevac)
                    for (xb, cb_, n) in segs:
                        P.op("act", lambda e, xb=xb, cb_=cb_, n=n, j=j: e.activation(
                            out=XC[:, cb_:cb_ + n], in_=XRT[:, xb:xb + n], func=AF.Identity, scale=LCW[:, j * 4:j * 4 + 1], bias=LCB[:, j:j + 1]),
                            reads=["XRT", "LCW", "LCB"], writes=["XC"])
                        for tap in (1, 2, 3):
                            P.op("dve", lambda e, xb=xb, cb_=cb_, n=n, j=j, tap=tap: e.scalar_tensor_tensor(
                                out=XC[:, cb_:cb_ + n], in0=XRT[:, xb + tap: xb + tap + n], scalar=LCW[:, j * 4 + tap: j * 4 + tap + 1],
                                in1=XC[:, cb_:cb_ + n], op0=ALU.mult, op1=ALU.add), reads=["XRT", "LCW", "XC"], writes=["XC"])
                    P.op("act", lambda e: e.activation(out=XCB[:, :], in_=XC[:, :], func=AF.Copy), reads=["XC"], writes=["XCB"])
                    for d in range(2):
                        col = d * 4 + j
                        for (c0, n) in tiles5:
                            b = gst["n"] % 2
                            gst["n"] += 1
                            ba, bx = 2 + b, 4 + b
                            for (bank, kind) in ((ba, 0), (bx, 1)):
                                P.group("pe", [lambda e, bank=bank, kind=kind, col=col, c0=c0, n=n: e.matmul(
                                    self.PS[bank][:, 0:n], BD[:, (kind * 8 + col) * 128:(kind * 8 + col + 1) * 128], XCB[:, c0:c0 + n], start=True, stop=True)],
                                    reads=["BD", "XCB"], writes=[("ps", bank)])
                            o0 = c0 if d == 0 else (c0 - CTX if c0 >= CTX else SEQ)
                            P.op("act", lambda e, ba=ba, b=b, n=n, col=col: e.activation(out=TR[b][:, 0:n], in_=self.PS[ba][:, 0:n], func=AF.Tanh, scale=0.5, bias=LBA[:, col:col + 1]),
                                 reads=[("ps", ba), "LBA"], writes=[("TR", b)])
                            P.op("act", lambda e, bx=bx, n=n, col=col, o0=o0: e.activation(out=T2[:, o0:o0 + n], in_=self.PS[bx][:, 0:n], func=AF.Tanh, scale=0.5, bias=LBX[:, col:col + 1]),
                                 reads=[("ps", bx), "LBX"], writes=["T2"])
                            P.op("act", lambda e, b=b, n=n, col=col, o0=o0: e.activation(out=A[:, o0:o0 + n], in_=TR[b][:, 0:n], func=AF.Exp, scale=CLH[:, col:col + 1], bias=CLH[:, col:col + 1]),
                                 reads=[("TR", b), "CLH"], writes=["A"])
                            P.op("act", lambda e, b=b, n=n, col=col, o0=o0: e.activation(out=B[:, o0:o0 + n], in_=TR[b][:, 0:n], func=AF.Exp, scale=CL[:, col:col + 1], bias=CL[:, col:col + 1]),
                                 reads=[("TR", b), "CL"], writes=["B"])
                        P.op("act", lambda e: e.activation(out=B[:, :], in_=B[:, :], func=AF.Sqrt, scale=-0.25, bias=QRT[:, 0:1]), reads=["B", "QRT"], writes=["B"])
                        P.op("dve", lambda e: e.scalar_tensor_tensor(out=B[:, :], in0=T2[:, :], scalar=1.0, in1=B[:, :], op0=ALU.add, op1=ALU.mult),
                             reads=["T2", "B"], writes=["B"])
                        if d == 0:
                            P.op("dve", lambda e: e.tensor_tensor(out=B[:, :], in0=B[:, :], in1=XC[:, :], op=ALU.mult), reads=["B", "XC"], writes=["B"])
                            P.op("dve", lambda e: e.tensor_tensor_scan(out=XRT[:, 0:ZT], data0=A[:, 0:ZT], data1=B[:, 0:ZT], initial=0.0,
                                                                       op0=ALU.mult, op1=ALU.add), reads=["A", "B"], writes=["XRT"])
                        else:
                            P.op("dve", lambda e: e.tensor_tensor(out=B[:, 0:SEQ], in0=B[:, 0:SEQ], in1=XC[:, CTX:ZT], op=ALU.mult), reads=["B", "XC"], writes=["B"])
                            P.op("dve", lambda e: e.tensor_tensor(out=B[:, SEQ:ZT], in0=B[:, SEQ:ZT], in1=XC[:, 0:CTX], op=ALU.mult), reads=["B", "XC"], writes=["B"])
                            P.op("dve", lambda e: e.tensor_tensor_scan(out=XC[:, 0:ZT][:, ::-1], data0=A[:, 0:ZT][:, ::-1], data1=B[:, 0:ZT][:, ::-1],
                                                                       initial=0.0, op0=ALU.mult, op1=ALU.add), reads=["A", "B"], writes=["XC"])
                    P.op("dve", lambda e: e.tensor_tensor(out=A[:, 0:SEQ], in0=XRT[:, CTX:ZT], in1=XC[:, 0:SEQ], op=ALU.add),
                         reads=["XRT", "XC", "A"], writes=["A"], strict=True)
                    P.op("dve", lambda e, j=j: e.scalar_tensor_tensor(out=F[:, j * SEQ:(j + 1) * SEQ], in0=A[:, 0:SEQ], scalar=0.5, in1=GG[:, :],
                                                                      op0=ALU.mult, op1=ALU.mult),
                         reads=["A", "GG"], writes=[("F", j)])
                P.barrier()
            if self.dump_ctx:
                P.dma("sp", self.s_ioc, lambda e: e.dma_start(out=self.dbg[128:256, :], in_=F[:]), reads=[("F", i) for i in range(4)], writes=["dbg1"])
                P.barrier()
            with ExitStack() as sln:
                tl = self.ln_tiles_alloc(sln, 4)
                halfproj(1, tl)
                for tt in range(4):
                    self.ln_tile(tl, "h", l, k, w, tt * 512, 512, ti=tt, pre=True)
                P.barrier()


def prep_shared(inp):
    f = lambda a: np.ascontiguousarray(np.asarray(a, dtype=np.float32))
    sh = {}
    sh["modw"] = np.concatenate([lhsT_layout(f(inp["mod_w"][l])) for l in range(2)], axis=0)
    mb = np.concatenate([vec_pm(f(inp["mod_b"][l])) for l in range(2)], axis=1)
    sh["modb3"] = np.ascontiguousarray(np.repeat(mb, 3, axis=1))
    sh["lng"] = np.concatenate([vec_pm(f(inp["ln_g"][l, k])) for l in range(2) for k in range(3)], axis=1)
    sh["lnb"] = np.concatenate([vec_pm(f(inp["ln_b"][l, k])) for l in range(2) for k in range(3)], axis=1)
    sh["w1"] = np.concatenate([lhsT_layout(f(inp["ffn_w1"][l, j])) for l in range(2) for j in range(2)], axis=0)
    sh["w3"] = np.concatenate([lhsT_layout(f(inp["ffn_w3"][l, j])) for l in range(2) for j in range(2)], axis=0)
    sh["w2"] = np.concatenate([lhsT_layout(f(inp["ffn_w2"][l, j])) for l in range(2) for j in range(2)], axis=0)
    sh["win0"] = lhsT_layout(f(inp["mix0_w_in"][0]))
    sh["wout0"] = lhsT_layout(f(inp["mix0_w_out"][0]))
    sh["win1"] = lhsT_layout(f(inp["mix1_w_in"][0]))
    sh["wout1"] = lhsT_layout(f(inp["mix1_w_out"][0]))
    scw = f(inp["sconv_w"][0])
    sh["scw"] = np.ascontiguousarray(np.stack([vec_pm(scw[t]) for t in range(3)], axis=2).reshape(128, 24))
    rpb = f(inp["na_rpb"][0])
    j = np.arange(64)[:, None]
    kc = np.arange(64)[None, :]
    ci = np.clip(kc - j + 15, 0, 30)
    g = rpb[:, :, ci]
    g = g.transpose(0, 2, 1, 3)
    rp = np.zeros((128, 4 * 15 * 64), np.float32)
    for h in range(8):
        half, idx = h % 2, h // 2
        rp[half * 64:(half + 1) * 64, idx * 960:(idx + 1) * 960] = g[h].reshape(64, 960)
    sh["rpbg"] = rp
    start = np.clip(j - 8, 0, 48)
    valid = (kc >= start) & (kc < start + 16)
    nm = np.where(valid, 0.0, -30000.0).astype(np.float32)
    sh["nmask"] = np.ascontiguousarray(np.tile(np.concatenate([nm, nm], axis=0), (1, 60)))
    eye = np.eye(64, dtype=np.float32)
    sh["id2"] = np.ascontiguousarray(np.concatenate([eye, eye], axis=0))
    cw = f(inp["lru_conv_w"][0])
    sh["lcw"] = np.ascontiguousarray(np.stack([vec_pm(cw[t]) for t in range(4)], axis=2).reshape(128, 16))
    sh["lcb"] = vec_pm(f(inp["lru_conv_b"][0]))
    def bd(wm):
        o = np.zeros((2, 4, 128, 128), np.float32)
        for d in range(2):
            for n in range(8):
                c, hh = n // 2, n % 2
                o[d, c, hh * 64:(hh + 1) * 64, hh * 64:(hh + 1) * 64] = wm[d, n]
        return o.reshape(2 * 4 * 128, 128)
    sh["lwa"] = bd(f(inp["lru_w_a"][0]))
    sh["lwx"] = bd(f(inp["lru_w_x"][0]))
    sh["lba"] = np.concatenate([vec_pm(f(inp["lru_b_a"][0, d])) for d in range(2)], axis=1)
    sh["lbx"] = np.concatenate([vec_pm(f(inp["lru_b_x"][0, d])) for d in range(2)], axis=1)
    sh["llam"] = np.concatenate([vec_pm(f(inp["lru_lambda"][0, d])) for d in range(2)], axis=1)
    return sh


def prep_core(inp, bidx):
    f = lambda a: np.asarray(a, dtype=np.float32)
    m = {}
    m["xT"] = np.concatenate([to_pm(f(inp["x"][b])) for b in bidx], axis=0)
    m["cxT"] = np.concatenate([to_pm(f(inp["ctx"][b])) for b in bidx], axis=0)
    cols = [f(inp["c"][b]) for b in bidx]
    while len(cols) < 2:
        cols.append(cols[0])
    cols.append(f(inp["c_ctx"]))
    cm = np.stack([vec_pm(c) for c in cols], axis=2)
    m["cond"] = np.ascontiguousarray(cm.reshape(128, 24))
    return m


_CACHE = {}


def get_program(NS, phases=ALL_PHASES, dump_ctx=False):
    key = (NS, tuple(phases), dump_ctx)
    if key not in _CACHE:
        _CACHE[key] = Builder(NS, phases, dump_ctx).build()
    return _CACHE[key]


def kernel(**inputs):
    B = inputs["x"].shape[0]
    NS = B // NCORES
    nc = get_program(NS)
    sh = prep_shared(inputs)
    in_maps = []
    for c in range(NCORES):
        m = dict(sh)
        m.update(prep_core(inputs, list(range(c * NS, (c + 1) * NS))))
        in_maps.append(m)
    res = run_bass_kernel_spmd(nc, in_maps, core_ids=list(range(NCORES)))
    out = np.empty((B, SEQ, D), np.float32)
    for c in range(NCORES):
        o = res.results[c]["out"]
        for i in range(NS):
            out[c * NS + i] = from_pm(o[i * 128:(i + 1) * 128], SEQ)
    return out
```

```python
import numpy as np
from contextlib import ExitStack
import concourse.bass as bass
import concourse.mybir as mybir
from concourse.bass_utils import run_bass_kernel_spmd

F32 = mybir.dt.float32
BF16 = mybir.dt.bfloat16
AF = mybir.ActivationFunctionType
ALU = mybir.AluOpType

D = 1024
SEQ = 2048
CTX = 256
DFF = 2816
NFC = 22
GRID_W = 64
ALPHA = 4.0 ** 0.25
LN_EPS = 1e-5 / (ALPHA * ALPHA)
NCORES = 8
ENGS = ("pe", "act", "dve", "pool", "sp")


class Stream:
    def __init__(self, sem):
        self.sem = sem
        self.count = 0


class Prog:
    def __init__(self, nc, stack):
        self.nc = nc
        self.stack = stack
        self.q = {e: [] for e in ENGS}
        self.cnt = {e: 0 for e in ENGS}
        self.esem = {e: stack.enter_context(nc.semaphore("es_" + e)) for e in ENGS if e != "sp"}
        self.waited = {}
        self.lastw = {}
        self.readers = {}
        self.streams = []
        self.scount = {}
        self.n_ops = 0

    def stream(self, name=None):
        s = Stream(self.stack.enter_context(self.nc.semaphore(name or ("ds%d" % len(self.streams)))))
        self.streams.append(s)
        self.scount["s_%d" % id(s)] = (lambda s=s: s.count)
        return s

    def _need(self, eng, tok, waits):
        if tok is None:
            return
        sid, sem, val, teng = tok
        if teng == eng and eng == "pe":
            return
        if teng is None:
            val = max(val, self.scount[sid]())
        k = (eng, sid)
        if self.waited.get(k, 0) >= val:
            return
        self.waited[k] = val
        waits.append((sem, val))

    def _deps(self, eng, reads, writes):
        waits = []
        for k in reads:
            self._need(eng, self.lastw.get(k), waits)
        for k in writes:
            self._need(eng, self.lastw.get(k), waits)
            for t in self.readers.get(k, ()):
                self._need(eng, t, waits)
        best = {}
        for sem, val in waits:
            if id(sem) not in best or best[id(sem)][1] < val:
                best[id(sem)] = (sem, val)
        return list(best.values())

    def _commit(self, tok, reads, writes):
        for k in reads:
            self.readers.setdefault(k, []).append(tok)
        for k in writes:
            self.lastw[k] = tok
            self.readers[k] = []

    def op(self, eng, fn, reads=(), writes=(), strict=False):
        self.group(eng, [fn], reads, writes, strict)

    def group(self, eng, fns, reads=(), writes=(), strict=False):
        waits = self._deps(eng + "_strict" if strict else eng, reads, writes)
        self.cnt[eng] += 1
        tok = ("e_" + eng, self.esem[eng], self.cnt[eng], eng)
        n = len(fns)
        for i, fn in enumerate(fns):
            self.q[eng].append((waits if i == 0 else [], fn, (self.esem[eng], 1) if i == n - 1 else None))
        self._commit(tok, reads, writes)
        self.n_ops += n

    def dma(self, eng, stream, fn, reads=(), writes=()):
        waits = self._deps(eng + "_q", reads, writes)
        stream.count += 16
        tok = ("s_%d" % id(stream), stream.sem, stream.count, None)
        self.q[eng].append((waits, fn, (stream.sem, 16)))
        self._commit(tok, reads, writes)
        self.n_ops += 1

    def barrier(self):
        toks = [("e_" + e, self.esem[e], self.cnt[e], e) for e in self.esem if self.cnt[e] > 0]
        toks += [("s_%d" % id(s), s.sem, s.count, None) for s in self.streams if s.count > 0]
        for e, ident in (("pe", "pe"), ("act", "act"), ("dve", "dve"), ("pool", "pool"), ("pool", "pool_q"),
                         ("sp", "sp_q"), ("act", "act_q")):
            waits = []
            for t in toks:
                self._need(ident, t, waits)
            if waits:
                self.q[e].append((waits, None, None))
        self.lastw = {}
        self.readers = {}

    def emit(self):
        nc = self.nc
        with nc.Block() as block:
            def run(engname):
                def body(e):
                    for waits, fn, inc in self.q[engname]:
                        for sem, val in waits:
                            e.wait_ge(sem, val)
                        if fn is not None:
                            ins = fn(e)
                            if inc is not None:
                                ins.then_inc(inc[0], inc[1])
                return body
            block.sync(run("sp"))
            block.tensor(run("pe"))
            block.scalar(run("act"))
            block.vector(run("dve"))
            block.gpsimd(run("pool"))


def to_pm(a):
    T, Dd = a.shape
    nch = Dd // 128
    return np.ascontiguousarray(a.T.reshape(nch, 128, T).transpose(1, 0, 2).reshape(128, nch * T))


def from_pm(o, T):
    nch = o.shape[1] // T
    return np.ascontiguousarray(o.reshape(128, nch, T).transpose(2, 1, 0).reshape(T, nch * 128))


def lhsT_layout(W):
    K, N = W.shape
    kc, ncc = K // 128, N // 128
    return np.ascontiguousarray(W.reshape(kc, 128, ncc, 128).transpose(2, 1, 0, 3).reshape(ncc * 128, K))


def vec_pm(v):
    return np.ascontiguousarray(v.reshape(-1, 128).T)


ALL_PHASES = ("L0F0", "L0MIX", "L0F1", "L1F0", "L1MIX", "L1F1")


class Builder:
    def __init__(self, NS=2, phases=ALL_PHASES, dump_ctx=False):
        self.NS = NS
        self.phases = phases
        self.dump_ctx = dump_ctx
        self.nc = bass.Bass("TRN2", target_bir_lowering=False)
        self.conv_done = set()
        self.scr_keys = {}
        self.conv_streams = {}
        self._gs = {}

    def dram(self, name, shape, dt, kind):
        return self.nc.dram_tensor(name, shape, dt, kind=kind).ap()

    def gs(self, name):
        if name not in self._gs:
            self._gs[name] = self.P.stream("gs_" + name)
        return self._gs[name]

    def sb(self, st, name, shape, dt):
        self.ntile = getattr(self, "ntile", 0) + 1
        return st.enter_context(self.nc.sbuf_tensor("%s_%d" % (name, self.ntile), shape, dt))

    def declare(self):
        NS = self.NS
        di = lambda n, s: self.dram(n, s, F32, "ExternalInput")
        self.xT = di("xT", [NS * 128, 8 * SEQ])
        self.cxT = di("cxT", [NS * 128, 8 * CTX])
        self.cond = di("cond", [128, 24])
        self.modw = di("modw", [2 * 72 * 128, 1024])
        self.modb3 = di("modb3", [128, 2 * 72 * 3])
        self.lng = di("lng", [128, 48])
        self.lnb = di("lnb", [128, 48])
        self.w1 = di("w1", [4 * NFC * 128, 1024])
        self.w3 = di("w3", [4 * NFC * 128, 1024])
        self.w2 = di("w2", [4 * 8 * 128, DFF])
        self.win0 = di("win0", [20 * 128, 1024])
        self.wout0 = di("wout0", [8 * 128, 1024])
        self.win1 = di("win1", [24 * 128, 1024])
        self.wout1 = di("wout1", [8 * 128, 1024])
        self.scw = di("scw", [128, 24])
        self.rpbg = di("rpbg", [128, 4 * 15 * 64])
        self.nmask = di("nmask", [128, 3840])
        self.id2 = di("id2", [128, 64])
        self.lcw = di("lcw", [128, 16])
        self.lcb = di("lcb", [128, 4])
        self.lwa = di("lwa", [2 * 4 * 128, 128])
        self.lwx = di("lwx", [2 * 4 * 128, 128])
        self.lba = di("lba", [128, 8])
        self.lbx = di("lbx", [128, 8])
        self.llam = di("llam", [128, 8])
        self.out = self.dram("out", [NS * 128, 8 * SEQ], F32, "ExternalOutput")
        if self.dump_ctx:
            self.outc = self.dram("outc", [NS * 128, 8 * CTX], F32, "ExternalOutput")
            self.dbg = self.dram("dbg", [256, 4 * SEQ], BF16, "ExternalOutput")
            self.dbg2 = self.dram("dbg2", [128, 5 * (CTX + SEQ)], F32, "ExternalOutput")
            self.dbg3 = self.dram("dbg3", [128, 8 * SEQ], F32, "ExternalOutput")
            self.dbg4 = self.dram("dbg4", [128, 8 * CTX], F32, "ExternalOutput")
        ds = lambda n, s: self.dram(n, s, BF16, "Internal")
        self.w1s = ds("w1s", [4 * NFC * 128, 1024])
        self.w3s = ds("w3s", [4 * NFC * 128, 1024])
        self.w2s = ds("w2s", [4 * 8 * 128, DFF])
        self.win0s = ds("win0s", [20 * 128, 1024])
        self.wout0s = ds("wout0s", [8 * 128, 1024])
        self.win1s = ds("win1s", [24 * 128, 1024])
        self.wout1s = ds("wout1s", [8 * 128, 1024])

    def convert(self, name, src, dst, r0, r1, step):
        P = self.P
        for a in range(r0, r1, step):
            b = min(a + step, r1)
            key = ("cv", name, a)
            if key in self.conv_done:
                continue
            self.conv_done.add(key)
            self.scr_keys.setdefault(name, []).append(("scr", name, a))
            if name not in self.conv_streams:
                self.conv_streams[name] = self.gs("cv_" + name)
            P.dma("pool", self.conv_streams[name], lambda e, a=a, b=b: e.dma_start(out=dst[a:b, :], in_=src[a:b, :]),
                  writes=[("scr", name, a)])

    def flush_conv(self):
        for p in getattr(self, "pending_conv", []):
            self.convert_for(p)
        self.pending_conv = []

    def convert_for(self, phase):
        if phase in ("L0F0", "L0F1", "L1F0", "L1F1"):
            q = {"L0F0": 0, "L0F1": 1, "L1F0": 2, "L1F1": 3}[phase]
            self.convert("w1_%d" % q, self.w1, self.w1s, q * NFC * 128, (q + 1) * NFC * 128, 704)
            self.convert("w3_%d" % q, self.w3, self.w3s, q * NFC * 128, (q + 1) * NFC * 128, 704)
            self.convert("w2_%d" % q, self.w2, self.w2s, q * 1024, (q + 1) * 1024, 256)
        elif phase == "L0MIX":
            self.convert("win0", self.win0, self.win0s, 0, 20 * 128, 640)
            self.convert("wout0", self.wout0, self.wout0s, 0, 1024, 512)
        elif phase == "L1MIX":
            self.convert("win1", self.win1, self.win1s, 0, 24 * 128, 768)
            self.convert("wout1", self.wout1, self.wout1s, 0, 1024, 512)

    def mv(self, l, n, dc, w):
        c = ((l * 72 + n * 8 + dc) * 3 + w)
        return self.MV[:, c:c + 1]

    def lnv(self, t, l, k, dc):
        c = (l * 3 + k) * 8 + dc
        return t[:, c:c + 1]

    def build(self):
        nc = self.nc
        self.declare()
        with ExitStack() as st:
            self.st = st
            P = self.P = Prog(nc, st)
            self.s_conv = P.stream("s_conv")
            self.s_const = P.stream("s_const")
            self.s_io = P.stream("s_io")
            self.s_ioc = P.stream("s_ioc")
            self.H = self.sb(st, "H", [128, 8 * SEQ], F32)
            self.HC = self.sb(st, "HC", [128, 8 * CTX], F32)
            self.MV = self.sb(st, "MV", [128, 2 * 72 * 3], F32)
            self.LNG = self.sb(st, "LNG", [128, 48], F32)
            self.LNB = self.sb(st, "LNB", [128, 48], F32)
            self.ONES = self.sb(st, "ONES", [128, 128], F32)
            self.PS = [st.enter_context(nc.psum_tensor("ps%d" % i, [128, 512], F32)) for i in range(8)]
            self.prologue()
            for s in range(self.NS):
                self.sequence(s)
            P.barrier()
            P.emit()
        return nc

    def prologue(self):
        P, nc = self.P, self.nc
        first = [p for p in ALL_PHASES if p in self.phases][0]
        P.dma("sp", self.s_const, lambda e: e.dma_start(out=self.LNG[:], in_=self.lng), writes=["LNG"])
        P.dma("sp", self.s_const, lambda e: e.dma_start(out=self.LNB[:], in_=self.lnb), writes=["LNB"])
        P.op("pool", lambda e: e.memset(self.ONES[:], 1.0), writes=["ONES"])
        with ExitStack() as sc:
            CF = self.sb(sc, "CF", [128, 24], F32)
            CS = self.sb(sc, "CS", [128, 24], BF16)
            MB = self.sb(sc, "MB", [128, 2 * 72 * 3], F32)
            G = 8
            MW = [self.sb(sc, "MW%d" % i, [128, G * 1024], BF16) for i in range(2)]
            s_mw = [P.stream("s_mw%d" % i) for i in range(2)]
            P.dma("sp", self.s_const, lambda e: e.dma_start(out=CF[:], in_=self.cond), writes=["CF"])
            P.dma("sp", self.s_const, lambda e: e.dma_start(out=MB[:], in_=self.modb3), writes=["MB"])
            P.op("act", lambda e: e.activation(out=CS[:], in_=CF[:], func=AF.Silu), reads=["CF"], writes=["CS"])
            ngrp = 2 * 72 // G
            for g in range(ngrp):
                if g == 3:
                    self.convert_for(first)
                slot = g % 2
                r0 = g * G * 128
                P.dma("pool", s_mw[slot],
                      lambda e, slot=slot, r0=r0: e.dma_start(
                          out=MW[slot][:].rearrange("p (g c) -> p g c", g=G),
                          in_=self.modw[r0:r0 + G * 128, :].rearrange("(g p) c -> p g c", p=128)),
                      writes=[("MW", slot)])
                ps = self.PS[slot]
                fns = []
                for jl in range(G):
                    for k in range(8):
                        fns.append(lambda e, slot=slot, jl=jl, k=k, ps=ps: e.matmul(
                            ps[:, jl * 3:jl * 3 + 3], MW[slot][:, jl * 1024 + k * 128: jl * 1024 + (k + 1) * 128],
                            CS[:, k * 3:k * 3 + 3], start=(k == 0), stop=(k == 7)))
                P.group("pe", fns, reads=[("MW", slot), "CS"], writes=[("ps", slot)])
                c0 = g * G * 3
                P.op("dve", lambda e, ps=ps, c0=c0: e.tensor_tensor(
                    out=self.MV[:, c0:c0 + G * 3], in0=ps[:, 0:G * 3], in1=MB[:, c0:c0 + G * 3], op=ALU.add),
                    reads=[("ps", slot), "MB"], writes=["MV"])
            for l in range(2):
                for k in range(3):
                    c0 = (l * 72 + (3 * k + 1) * 8) * 3
                    P.op("dve", lambda e, c0=c0: e.tensor_scalar_add(out=self.MV[:, c0:c0 + 24], in0=self.MV[:, c0:c0 + 24], scalar1=1.0),
                         reads=["MV"], writes=["MV"])
                    c1 = (l * 72 + (3 * k + 2) * 8) * 3
                    rw = (1.0 if k == 1 else 0.5) / ALPHA
                    P.op("dve", lambda e, c1=c1, rw=rw: e.tensor_scalar_mul(out=self.MV[:, c1:c1 + 24], in0=self.MV[:, c1:c1 + 24], scalar1=rw),
                         reads=["MV"], writes=["MV"])
            P.barrier()

    def sequence(self, s):
        P = self.P
        ph = [p for p in ALL_PHASES if p in self.phases]
        for dc in range(8):
            P.dma("sp", self.s_io, lambda e, dc=dc: e.dma_start(out=self.H[:, dc * SEQ:(dc + 1) * SEQ],
                                                              in_=self.xT[s * 128:(s + 1) * 128, dc * SEQ:(dc + 1) * SEQ]),
                  writes=[("h", dc, t) for t in range(4)])
        P.dma("sp", self.s_ioc, lambda e: e.dma_start(out=self.HC[:], in_=self.cxT[s * 128:(s + 1) * 128, :]),
              writes=[("hc", dc) for dc in range(8)])
        units = []
        for p in ph:
            if p in ("L0F0", "L0F1", "L1F0", "L1F1") and units and units[-1][0] in ("L0F1",) and p == "L1F0":
                units[-1].append(p)
            else:
                units.append([p])
        for i, u in enumerate(units):
            self.pending_conv = list(units[i + 1]) if (s == 0 and i + 1 < len(units)) else []
            if u[0] != "L0MIX":
                self.flush_conv()
            if u[0] in ("L0F0", "L0F1", "L1F0", "L1F1"):
                self.ffn_phase(s, [(int(p[1]), int(p[3]), p == "L0F0") for p in u])
            elif u[0] == "L1MIX":
                self.l1mix_phase(s)
            elif u[0] == "L0MIX":
                self.l0mix_phase(s)
        for dc in range(8):
            P.dma("sp", self.s_io, lambda e, dc=dc: e.dma_start(out=self.out[s * 128:(s + 1) * 128, dc * SEQ:(dc + 1) * SEQ],
                                                              in_=self.H[:, dc * SEQ:(dc + 1) * SEQ]),
                  reads=[("h", dc, t) for t in range(4)], writes=[("out", s, dc)])
        if self.dump_ctx:
            P.dma("sp", self.s_ioc, lambda e: e.dma_start(out=self.outc[s * 128:(s + 1) * 128, :], in_=self.HC[:]),
                  reads=[("hc", dc) for dc in range(8)], writes=[("outc", s)])

    def hview(self, stream, dc, t0, n):
        if stream == "h":
            return self.H[:, dc * SEQ + t0: dc * SEQ + t0 + n]
        return self.HC[:, dc * CTX + t0: dc * CTX + t0 + n]

    def hkeys(self, stream, dc, t0, n):
        if stream == "h":
            return [("h", dc, t) for t in range(t0 // 512, (t0 + n + 511) // 512)]
        return [("hc", dc)]

    def stats_accum(self, tl, ti, stream, dc, t0, n):
        P = self.P
        SQ, SS, QQ = tl["SQ"], tl["SS"][ti], tl["QQ"][ti]
        hv = self.hview(stream, dc, t0, n)
        hk = self.hkeys(stream, dc, t0, n)
        b = dc % 2
        if dc == 0:
            P.op("act", lambda e: e.activation(out=QQ[:, 0:n], in_=hv, func=AF.Square), reads=hk, writes=[("QQ", ti)])
            return
        P.op("act", lambda e: e.activation(out=SQ[b][:, 0:n], in_=hv, func=AF.Square), reads=hk, writes=[("SQ", b)])
        P.op("dve", lambda e: e.tensor_tensor(out=QQ[:, 0:n], in0=QQ[:, 0:n], in1=SQ[b][:, 0:n], op=ALU.add),
             reads=[("QQ", ti), ("SQ", b)], writes=[("QQ", ti)])
        if dc == 1:
            hv0 = self.hview(stream, 0, t0, n)
            P.op("pool", lambda e: e.tensor_tensor(out=SS[:, 0:n], in0=hv0, in1=hv, op=ALU.add),
                 reads=hk + self.hkeys(stream, 0, t0, n), writes=[("SS", ti)])
        else:
            P.op("pool", lambda e: e.tensor_tensor(out=SS[:, 0:n], in0=SS[:, 0:n], in1=hv, op=ALU.add),
                 reads=hk + [("SS", ti)], writes=[("SS", ti)])

    def ln_tile(self, tl, stream, l, k, w, t0, n, ti=0, pre=False, defer=False):
        P = self.P
        T1, EPS = tl["T1"], tl["EPS"]
        MEAN, RSTD = tl["MEAN"][ti], tl["RSTD"][ti]
        SS, QQ = tl["SS"][ti], tl["QQ"][ti]
        psS, psQ = self.PS[6], self.PS[7]
        if not pre:
            for dc in range(8):
                self.stats_accum(tl, ti, stream, dc, t0, n)
        P.group("pe", [lambda e: e.matmul(psS[:, 0:n], self.ONES[:], SS[:, 0:n], start=True, stop=True)],
                reads=[("SS", ti), "ONES"], writes=[("ps", 6)])
        P.group("pe", [lambda e: e.matmul(psQ[:, 0:n], self.ONES[:], QQ[:, 0:n], start=True, stop=True)],
                reads=[("QQ", ti), "ONES"], writes=[("ps", 7)])
        P.op("dve", lambda e: e.tensor_scalar_mul(out=MEAN[:, 0:n], in0=psS[:, 0:n], scalar1=1.0 / D),
             reads=[("ps", 6)], writes=[("MEAN", ti)])
        P.op("dve", lambda e: e.tensor_tensor(out=RSTD[:, 0:n], in0=MEAN[:, 0:n], in1=MEAN[:, 0:n], op=ALU.mult),
             reads=[("MEAN", ti)], writes=[("RSTD", ti)])
        P.op("dve", lambda e: e.scalar_tensor_tensor(out=RSTD[:, 0:n], in0=psQ[:, 0:n], scalar=1.0 / D, in1=RSTD[:, 0:n],
                                                     op0=ALU.mult, op1=ALU.subtract),
             reads=[("ps", 7), ("RSTD", ti)], writes=[("RSTD", ti)])
        P.op("act", lambda e: e.activation(out=RSTD[:, 0:n], in_=RSTD[:, 0:n], func=AF.Sqrt, bias=EPS[:, 0:1]),
             reads=[("RSTD", ti), "EPS"], writes=[("RSTD", ti)])
        P.op("dve", lambda e: e.reciprocal(out=RSTD[:, 0:n], in_=RSTD[:, 0:n]), reads=[("RSTD", ti)], writes=[("RSTD", ti)])

        def norm(dc):
            hv = self.hview(stream, dc, t0, n)
            hk = self.hkeys(stream, dc, t0, n)
            b = dc % 4
            e1 = "pool" if dc % 2 == 0 else "dve"
            P.op(e1, lambda e: e.tensor_tensor(out=T1[b][:, 0:n], in0=hv, in1=MEAN[:, 0:n], op=ALU.subtract),
                 reads=hk + [("MEAN", ti)], writes=[("T1", b)])
            P.op("dve", lambda e: e.tensor_tensor(out=T1[b][:, 0:n], in0=T1[b][:, 0:n], in1=RSTD[:, 0:n], op=ALU.mult),
                 reads=[("T1", b), ("RSTD", ti)], writes=[("T1", b)])
            P.op("act", lambda e: e.activation(out=hv, in_=T1[b][:, 0:n], func=AF.Identity,
                                               scale=self.lnv(self.LNG, l, k, dc), bias=self.lnv(self.LNB, l, k, dc)),
                 reads=[("T1", b), "LNG", "LNB"], writes=hk)

        todo = [(lambda dc=dc: norm(dc)) for dc in range(8)]
        if defer:
            return todo
        for f in todo:
            f()
        return []

    def ln_tiles_alloc(self, sc, ntiles=1):
        tl = {}
        tl["SQ"] = [self.sb(sc, "SQ%d" % i, [128, 512], F32) for i in range(2)]
        tl["T1"] = [self.sb(sc, "T1%d" % i, [128, 512], F32) for i in range(4)]
        tl["MEAN"] = [self.sb(sc, "MEAN%d" % i, [128, 512], F32) for i in range(max(2, ntiles))]
        tl["RSTD"] = [self.sb(sc, "RSTD%d" % i, [128, 512], F32) for i in range(max(2, ntiles))]
        tl["SS"] = [self.sb(sc, "SS%d" % i, [128, 512], F32) for i in range(ntiles)]
        tl["QQ"] = [self.sb(sc, "QQ%d" % i, [128, 512], F32) for i in range(ntiles)]
        EPS = tl["EPS"] = self.sb(sc, "EPS", [128, 1], F32)
        self.P.op("pool", lambda e: e.memset(EPS[:], LN_EPS), writes=["EPS"])
        return tl

    def ffn_phase(self, s, specs):
        P = self.P
        with ExitStack() as sc:
            TB = 1024
            Z = self.sb(sc, "Z", [128, 8 * TB], BF16)
            G = self.sb(sc, "G", [128, NFC * TB], BF16)
            W13 = [self.sb(sc, "W13_%d" % i, [128, 4 * 1024], BF16) for i in range(3)]
            W2 = [self.sb(sc, "W2_%d" % i, [128, DFF], BF16) for i in range(2)]
            SL = [self.sb(sc, "SL%d" % i, [128, 512], F32) for i in range(2)]
            tl = self.ln_tiles_alloc(sc, 2)
            s13 = [self.gs("w13_%d" % i) for i in range(len(W13))]
            s2 = [self.gs("w2_%d" % i) for i in range(len(W2))]
            st = {"g13": 0, "g2": 0, "it": 0, "itb": 0}
            blocks = []
            for (l, which, with_ctx) in specs:
                q = l * 2 + which
                k = 0 if which == 0 else 2
                if with_ctx:
                    blocks.append((l, q, k, "hc", 2, 0, CTX))
                blocks += [(l, q, k, "h", s, 0, TB), (l, q, k, "h", s, TB, TB)]
            ctxs = []
            for (l, q, k, stream, w, t0, tb) in blocks:
                ctxs.append(dict(l=l, q=q, k=k, stream=stream, w=(2 if stream == "hc" else s), t0=t0, tb=tb,
                                 Z=Z, G=G, W13=W13, W2=W2, SL=SL, tl=tl, s13=s13, s2=s2, st=st,
                                 tiles=[(a, min(512, tb - a)) for a in range(0, tb, 512)]))
            self.ffn_z(ctxs[0])
            pending = []
            for i, c in enumerate(ctxs):
                nxt = ctxs[i + 1] if i + 1 < len(ctxs) else None
                self.ffn_ab(c, pending, (lambda nxt=nxt: self.ffn_z(nxt)) if nxt is not None else None)
                while pending:
                    pending.pop(0)()
                for ti, (a, n) in enumerate(c["tiles"]):
                    pending += self.ln_tile(tl, c["stream"], c["l"], c["k"], c["w"], c["t0"] + a, n, ti=ti, pre=True, defer=True)
            while pending:
                pending.pop(0)()
            P.barrier()

    def ffn_z(self, c):
        P = self.P
        l, k, w, stream, t0, tb, Z = c["l"], c["k"], c["w"], c["stream"], c["t0"], c["tb"], c["Z"]
        for dc in range(8):
            hv = self.hview(stream, dc, t0, tb)
            hk = self.hkeys(stream, dc, t0, tb)
            zv = Z[:, dc * tb: (dc + 1) * tb]
            if dc % 2 == 0:
                P.op("act", lambda e, hv=hv, zv=zv, dc=dc: e.activation(out=zv, in_=hv, func=AF.Identity,
                                                                       scale=self.mv(l, 3 * k + 1, dc, w), bias=self.mv(l, 3 * k, dc, w)),
                     reads=hk + ["MV"], writes=[("Z", dc)])
            else:
                P.op("dve", lambda e, hv=hv, zv=zv, dc=dc: e.tensor_scalar(out=zv, in0=hv, scalar1=self.mv(l, 3 * k + 1, dc, w),
                                                                          scalar2=self.mv(l, 3 * k, dc, w), op0=ALU.mult, op1=ALU.add),
                     reads=hk + ["MV"], writes=[("Z", dc)])

    def ffn_ab(self, c, pending=(), mid=None):
        P = self.P
        l, q, k, w, stream, t0, tb = c["l"], c["q"], c["k"], c["w"], c["stream"], c["t0"], c["tb"]
        Z, G, W13, W2, SL, tl, s13, s2, st, tiles = c["Z"], c["G"], c["W13"], c["W2"], c["SL"], c["tl"], c["s13"], c["s2"], c["st"], c["tiles"]
        N13, N2 = len(W13), len(W2)

        def load13(fg):
            slot = st["g13"] % N13
            st["g13"] += 1
            r0 = (q * NFC + fg * 2) * 128
            for wi, src in enumerate((self.w1s, self.w3s)):
                P.dma("sp", s13[slot], lambda e, slot=slot, wi=wi, src=src, r0=r0: e.dma_start(
                    out=W13[slot][:, wi * 2048:(wi + 1) * 2048].rearrange("p (f c) -> p f c", f=2),
                    in_=src[r0:r0 + 256, :].rearrange("(f p) c -> p f c", p=128)),
                    reads=self.scr_keys["w1_%d" % q] + self.scr_keys["w3_%d" % q], writes=[("W13", slot)])
            return slot

        def load2(dc):
            slot = st["g2"] % N2
            st["g2"] += 1
            r0 = (q * 8 + dc) * 128
            P.dma("sp", s2[slot], lambda e, slot=slot, r0=r0: e.dma_start(out=W2[slot][:], in_=self.w2s[r0:r0 + 128, :]),
                  reads=self.scr_keys["w2_%d" % q], writes=[("W2", slot)])
            return slot

        nfg = NFC // 2
        slots13 = {0: load13(0), 1: load13(1)}
        slots2 = {}
        for fg in range(nfg):
            if fg + 2 < nfg:
                slots13[fg + 2] = load13(fg + 2)
            if fg == nfg - 2:
                slots2[0] = load2(0)
            if fg == nfg - 1:
                slots2[1] = load2(1)
            slot = slots13[fg]
            for fl in range(2):
                f = fg * 2 + fl
                for (a, n) in tiles:
                    b = st["it"] % 2
                    st["it"] += 1
                    p1, p3 = self.PS[b], self.PS[2 + b]
                    for wi, pp in ((0, p1), (1, p3)):
                        fns = []
                        for kk in range(8):
                            fns.append(lambda e, slot=slot, wi=wi, fl=fl, kk=kk, pp=pp, a=a, n=n: e.matmul(
                                pp[:, 0:n], W13[slot][:, wi * 2048 + fl * 1024 + kk * 128: wi * 2048 + fl * 1024 + (kk + 1) * 128],
                                Z[:, kk * tb + a: kk * tb + a + n], start=(kk == 0), stop=(kk == 7)))
                        P.group("pe", fns, reads=[("W13", slot)] + [("Z", kk) for kk in range(8)],
                                writes=[("ps", b if wi == 0 else 2 + b)])
                    P.op("act", lambda e, b=b, p1=p1, n=n: e.activation(out=SL[b][:, 0:n], in_=p1[:, 0:n], func=AF.Silu),
                         reads=[("ps", b)], writes=[("SL", b)])
                    P.op("dve", lambda e, b=b, p3=p3, f=f, a=a, n=n: e.tensor_tensor(
                        out=G[:, f * tb + a: f * tb + a + n], in0=SL[b][:, 0:n], in1=p3[:, 0:n], op=ALU.mult),
                        reads=[("SL", b), ("ps", 2 + b)], writes=[("G", f)])
                    if pending:
                        pending.pop(0)()
        if mid is not None:
            mid()
        for dc in range(8):
            if dc + 1 < 8 and dc >= 1:
                slots2[dc + 1] = load2(dc + 1)
            slot = slots2[dc]
            for ti, (a, n) in enumerate(tiles):
                b = st["itb"] % 2
                st["itb"] += 1
                py = self.PS[4 + b]
                fns = []
                for f in range(NFC):
                    fns.append(lambda e, slot=slot, f=f, py=py, a=a, n=n: e.matmul(
                        py[:, 0:n], W2[slot][:, f * 128:(f + 1) * 128], G[:, f * tb + a: f * tb + a + n],
                        start=(f == 0), stop=(f == NFC - 1)))
                P.group("pe", fns, reads=[("W2", slot)] + [("G", f) for f in range(NFC)], writes=[("ps", 4 + b)])
                hv = self.hview(stream, dc, t0 + a, n)
                hk = self.hkeys(stream, dc, t0 + a, n)
                P.op("dve", lambda e, py=py, hv=hv, dc=dc, n=n: e.scalar_tensor_tensor(
                    out=hv, in0=py[:, 0:n], scalar=self.mv(l, 3 * k + 2, dc, w), in1=hv, op0=ALU.mult, op1=ALU.add),
                    reads=[("ps", 4 + b), "MV"] + hk, writes=hk)
                self.stats_accum(tl, ti, stream, dc, t0 + a, n)

    def l1mix_phase(self, s):
        P = self.P
        l, k, w = 1, 1, s
        with ExitStack() as sc:
            Z = self.sb(sc, "Zm", [128, 8 * SEQ], BF16)
            F = self.sb(sc, "Fm", [128, 8 * SEQ], BF16)
            sc2 = ExitStack()
            U = [self.sb(sc2, "U%d" % i, [128, SEQ + 2], F32) for i in range(2)]
            GB = [self.sb(sc2, "GB%d" % i, [128, SEQ], BF16) for i in range(2)]
            ACC = self.sb(sc2, "ACC", [128, SEQ], F32)
            XV = [self.sb(sc2, "XV%d" % i, [128, 512], F32) for i in range(2)]
            WIN = [self.sb(sc2, "WIN%d" % i, [128, 3 * 1024], BF16) for i in range(2)]
            SCW = self.sb(sc2, "SCW", [128, 24], F32)
            swin = [self.gs("win_%d" % i) for i in range(2)]
            swout = [self.gs("wout_%d" % i) for i in range(2)]
            P.dma("sp", self.s_const, lambda e: e.dma_start(out=SCW[:], in_=self.scw), writes=["SCW"])
            for i in range(2):
                P.op("pool", lambda e, i=i: e.memset(U[i][:, 0:1], 0.0), writes=[("U", i)])
                P.op("pool", lambda e, i=i: e.memset(U[i][:, SEQ + 1:SEQ + 2], 0.0), writes=[("U", i)])
            for dc in range(8):
                hv = self.hview("h", dc, 0, SEQ)
                hk = self.hkeys("h", dc, 0, SEQ)
                zv = Z[:, dc * SEQ:(dc + 1) * SEQ]
                if dc % 2 == 0:
                    P.op("act", lambda e, hv=hv, zv=zv, dc=dc: e.activation(out=zv, in_=hv, func=AF.Identity,
                                                                           scale=self.mv(l, 3 * k + 1, dc, w), bias=self.mv(l, 3 * k, dc, w)),
                         reads=hk + ["MV"], writes=[("Z", dc)])
                else:
                    P.op("dve", lambda e, hv=hv, zv=zv, dc=dc: e.tensor_scalar(out=zv, in0=hv, scalar1=self.mv(l, 3 * k + 1, dc, w),
                                                                              scalar2=self.mv(l, 3 * k, dc, w), op0=ALU.mult, op1=ALU.add),
                         reads=hk + ["MV"], writes=[("Z", dc)])

            def loadwin(dc):
                slot = dc % 2
                for j in range(3):
                    r0 = (j * 8 + dc) * 128
                    P.dma("sp", swin[slot], lambda e, slot=slot, j=j, r0=r0: e.dma_start(
                        out=WIN[slot][:, j * 1024:(j + 1) * 1024], in_=self.win1s[r0:r0 + 128, :]),
                        reads=self.scr_keys["win1"], writes=[("WIN", slot)])

            def loadwout(dc):
                slot = dc % 2
                r0 = dc * 128
                P.dma("sp", swout[slot], lambda e, slot=slot, r0=r0: e.dma_start(out=WOUT[slot][:], in_=self.wout1s[r0:r0 + 128, :]),
                      reads=self.scr_keys["wout1"], writes=[("WOUT", slot)])

            loadwin(0)
            it = 0
            for dc in range(8):
                if dc + 1 < 8:
                    loadwin(dc + 1)
                slot = dc % 2
                ub = dc % 2
                for tt in range(4):
                    pb = (it % 2) * 3
                    it += 1
                    xb = tt % 2
                    for j in range(3):
                        pp = self.PS[pb + j]
                        fns = []
                        for kk in range(8):
                            fns.append(lambda e, slot=slot, j=j, kk=kk, pp=pp, tt=tt: e.matmul(
                                pp[:, :], WIN[slot][:, j * 1024 + kk * 128: j * 1024 + (kk + 1) * 128],
                                Z[:, kk * SEQ + tt * 512: kk * SEQ + (tt + 1) * 512], start=(kk == 0), stop=(kk == 7)))
                        P.group("pe", fns, reads=[("WIN", slot)] + [("Z", kk) for kk in range(8)], writes=[("ps", pb + j)])
                    P.op("act", lambda e, pb=pb, ub=ub, tt=tt: e.activation(out=GB[ub][:, tt * 512:(tt + 1) * 512], in_=self.PS[pb][:, :], func=AF.Copy),
                         reads=[("ps", pb)], writes=[("GB", ub)])
                    P.op("act", lambda e, pb=pb, xb=xb: e.activation(out=XV[xb][:, :], in_=self.PS[pb + 2][:, :], func=AF.Copy),
                         reads=[("ps", pb + 2)], writes=[("XV", xb)])
                    P.op("dve", lambda e, pb=pb, xb=xb, ub=ub, tt=tt: e.tensor_tensor(
                        out=U[ub][:, 1 + tt * 512: 1 + (tt + 1) * 512], in0=self.PS[pb + 1][:, :], in1=XV[xb][:, :], op=ALU.mult),
                        reads=[("ps", pb + 1), ("XV", xb)], writes=[("U", ub)])
                sw = lambda tap, dc=dc: SCW[:, dc * 3 + tap: dc * 3 + tap + 1]
                P.op("act", lambda e, ub=ub, sw=sw: e.activation(out=ACC[:, :], in_=U[ub][:, 0:SEQ], func=AF.Copy, scale=sw(0)),
                     reads=[("U", ub), "SCW"], writes=["ACC"])
                for tap in (1, 2):
                    P.op("dve", lambda e, ub=ub, sw=sw, tap=tap: e.scalar_tensor_tensor(
                        out=ACC[:, :], in0=U[ub][:, tap:tap + SEQ], scalar=sw(tap), in1=ACC[:, :], op0=ALU.mult, op1=ALU.add),
                        reads=[("U", ub), "SCW", "ACC"], writes=["ACC"])
                P.op("dve", lambda e, ub=ub, dc=dc: e.tensor_tensor(out=F[:, dc * SEQ:(dc + 1) * SEQ], in0=ACC[:, :], in1=GB[ub][:, :], op=ALU.mult),
                     reads=["ACC", ("GB", ub)], writes=[("F", dc)])
            P.barrier()
            sc2.close()
            WOUT = [self.sb(sc, "WOUT%d" % i, [128, 1024], BF16) for i in range(2)]
            tl = self.ln_tiles_alloc(sc, 4)
            loadwout(0)
            it = 0
            for dc in range(8):
                if dc + 1 < 8:
                    loadwout(dc + 1)
                slot = dc % 2
                for tt in range(4):
                    b = 6 + it % 2
                    it += 1
                    py = self.PS[b]
                    fns = []
                    for fc in range(8):
                        fns.append(lambda e, slot=slot, fc=fc, py=py, tt=tt: e.matmul(
                            py[:, :], WOUT[slot][:, fc * 128:(fc + 1) * 128], F[:, fc * SEQ + tt * 512: fc * SEQ + (tt + 1) * 512],
                            start=(fc == 0), stop=(fc == 7)))
                    P.group("pe", fns, reads=[("WOUT", slot)] + [("F", fc) for fc in range(8)], writes=[("ps", b)])
                    hv = self.hview("h", dc, tt * 512, 512)
                    hk = self.hkeys("h", dc, tt * 512, 512)
                    P.op("dve", lambda e, py=py, hv=hv, dc=dc: e.scalar_tensor_tensor(
                        out=hv, in0=py[:, :], scalar=self.mv(l, 3 * k + 2, dc, w), in1=hv, op0=ALU.mult, op1=ALU.add),
                        reads=[("ps", b), "MV"] + hk, writes=hk)
                    self.stats_accum(tl, tt, "h", dc, tt * 512, 512)
            for tt in range(4):
                self.ln_tile(tl, "h", l, k, w, tt * 512, 512, ti=tt, pre=True)
            P.barrier()

    def l0mix_phase(self, s):
        P = self.P
        l, k, w = 0, 1, s
        ZT = CTX + SEQ
        with ExitStack() as sc:
            Z = self.sb(sc, "Z0", [128, 8 * ZT], BF16)
            F = self.sb(sc, "F0", [128, 4 * SEQ], BF16)
            NWS = 4
            WS = [self.sb(sc, "WS%d" % i, [128, 1024], BF16) for i in range(NWS)]
            sws = [self.gs("ws_%d" % i) for i in range(NWS)]
            WO = [self.sb(sc, "WO%d" % i, [128, 512], BF16) for i in range(2)]
            swo = [self.gs("wo_%d" % i) for i in range(2)]
            wst = {"n": 0}

            def loadw(cc):
                slot = wst["n"] % NWS
                wst["n"] += 1
                P.dma("sp", sws[slot], lambda e, slot=slot, cc=cc: e.dma_start(out=WS[slot][:], in_=self.win0s[cc * 128:(cc + 1) * 128, :]),
                      reads=self.scr_keys["win0"], writes=[("WS", slot)])
                return slot

            if self.dump_ctx:
                P.dma("sp", self.s_ioc, lambda e: e.dma_start(out=self.dbg3[:, :], in_=self.H[:]),
                      reads=[("h", dc, t) for dc in range(8) for t in range(4)], writes=["dbg3"])
                P.dma("sp", self.s_ioc, lambda e: e.dma_start(out=self.dbg4[:, :], in_=self.HC[:]),
                      reads=[("hc", dc) for dc in range(8)], writes=["dbg4"])
            for dc in range(8):
                for (stream, ww, off, n) in (("hc", 2, 0, CTX), ("h", s, CTX, SEQ)):
                    hv = self.hview(stream, dc, 0, n)
                    hk = self.hkeys(stream, dc, 0, n)
                    zv = Z[:, dc * ZT + off: dc * ZT + off + n]
                    if dc % 2 == 0:
                        P.op("act", lambda e, hv=hv, zv=zv, dc=dc, ww=ww: e.activation(
                            out=zv, in_=hv, func=AF.Identity, scale=self.mv(l, 3 * k + 1, dc, ww), bias=self.mv(l, 3 * k, dc, ww)),
                            reads=hk + ["MV"], writes=[("Z", dc)])
                    else:
                        P.op("dve", lambda e, hv=hv, zv=zv, dc=dc, ww=ww: e.tensor_scalar(
                            out=zv, in0=hv, scalar1=self.mv(l, 3 * k + 1, dc, ww), scalar2=self.mv(l, 3 * k, dc, ww),
                            op0=ALU.mult, op1=ALU.add), reads=hk + ["MV"], writes=[("Z", dc)])
            zkeys = [("Z", kk) for kk in range(8)]
            pst = {"n": 0}

            def proj(slot, c0, n, evac, bank=None):
                if bank is None:
                    bank = 2 + pst["n"] % 2
                    pst["n"] += 1
                pp = self.PS[bank]
                fns = []
                for kk in range(8):
                    fns.append(lambda e, kk=kk, pp=pp: e.matmul(
                        pp[:, 0:n], WS[slot][:, kk * 128:(kk + 1) * 128], Z[:, kk * ZT + c0: kk * ZT + c0 + n],
                        start=(kk == 0), stop=(kk == 7)))
                P.group("pe", fns, reads=[("WS", slot)] + zkeys, writes=[("ps", bank)])
                evac(pp, bank)

            def halfproj(half, tl=None):
                it = 0
                def loadwo(dc):
                    slot = dc % 2
                    P.dma("sp", swo[slot], lambda e, slot=slot, dc=dc: e.dma_start(
                        out=WO[slot][:], in_=self.wout0s[dc * 128:(dc + 1) * 128, half * 512:(half + 1) * 512]),
                        reads=self.scr_keys["wout0"], writes=[("WO", slot)])
                loadwo(0)
                for dc in range(8):
                    if dc + 1 < 8:
                        loadwo(dc + 1)
                    slot = dc % 2
                    for tt in range(4):
                        b = 2 + it % 2
                        it += 1
                        py = self.PS[b]
                        fns = []
                        for fc in range(4):
                            fns.append(lambda e, slot=slot, fc=fc, py=py, tt=tt: e.matmul(
                                py[:, :], WO[slot][:, fc * 128:(fc + 1) * 128], F[:, fc * SEQ + tt * 512: fc * SEQ + (tt + 1) * 512],
                                start=(fc == 0), stop=(fc == 3)))
                        P.group("pe", fns, reads=[("WO", slot)] + [("F", fc) for fc in range(4)], writes=[("ps", b)])
                        hv = self.hview("h", dc, tt * 512, 512)
                        hk = self.hkeys("h", dc, tt * 512, 512)
                        P.op("dve", lambda e, py=py, hv=hv, dc=dc: e.scalar_tensor_tensor(
                            out=hv, in0=py[:, :], scalar=self.mv(l, 3 * k + 2, dc, w), in1=hv, op0=ALU.mult, op1=ALU.add),
                            reads=[("ps", b), "MV"] + hk, writes=hk)
                        if tl is not None:
                            self.stats_accum(tl, tt, "h", dc, tt * 512, 512)

            with ExitStack() as sn:
                Tt = self.sb(sn, "Tt", [128, 3840], BF16)
                ID2 = self.sb(sn, "ID2", [128, 64], BF16)
                ONB = self.sb(sn, "ONB", [128, 64], BF16)
                with ExitStack() as stmp:
                    RPb = self.sb(stmp, "RPb", [128, 3840], BF16)
                    NMb = self.sb(stmp, "NMb", [128, 3840], BF16)
                    P.dma("pool", self.gs("constp"), lambda e: e.dma_start(out=RPb[:], in_=self.rpbg), writes=["RPb"])
                    P.dma("pool", self.gs("constp"), lambda e: e.dma_start(out=NMb[:], in_=self.nmask), writes=["NMb"])
                    P.dma("pool", self.gs("constp"), lambda e: e.dma_start(out=ID2[:], in_=self.id2), writes=["ID2"])
                    P.op("dve", lambda e: e.tensor_tensor(out=Tt[:], in0=RPb[:], in1=NMb[:], op=ALU.add), reads=["RPb", "NMb"], writes=["Tt"])
                    P.op("pool", lambda e: e.memset(ONB[:], 1.0), writes=["ONB"])
                    P.barrier()
                QT = self.sb(sn, "QT", [128, SEQ], BF16)
                KT = self.sb(sn, "KT", [128, ZT], BF16)
                V = self.sb(sn, "V", [128, 18 * 128], BF16)
                V2 = self.sb(sn, "V2", [128, 15 * 128], BF16)
                PC = [self.sb(sn, "PC%d" % i, [128, 2 * SEQ], BF16) for i in range(2)]
                PL = [self.sb(sn, "PL%d" % i, [128, 256], BF16) for i in range(4)]
                RD = [self.sb(sn, "RD%d" % i, [128, 512], F32) for i in range(2)]
                PLS = [self.sb(sn, "PLS%d" % i, [128, 512], F32) for i in range(2)]
                for hp in range(4):
                    sq_, sk_, sv_ = loadw(hp), loadw(4 + hp), loadw(8 + hp)
                    for tt in range(4):
                        proj(sq_, CTX + tt * 512, 512, lambda pp, bank, tt=tt: P.op(
                            "act", lambda e, pp=pp, tt=tt: e.activation(out=QT[:, tt * 512:(tt + 1) * 512], in_=pp[:, :], func=AF.Copy, scale=0.125),
                            reads=[("ps", bank)], writes=["QT"]))
                    proj(sk_, 0, CTX, lambda pp, bank: P.op(
                        "dve", lambda e, pp=pp: e.tensor_copy(out=KT[:, 0:CTX], in_=pp[:, 0:CTX]), reads=[("ps", bank)], writes=["KT"]))
                    for tt in range(4):
                        proj(sk_, CTX + tt * 512, 512, lambda pp, bank, tt=tt: P.op(
                            "dve", lambda e, pp=pp, tt=tt: e.tensor_copy(out=KT[:, CTX + tt * 512: CTX + (tt + 1) * 512], in_=pp[:, :]),
                            reads=[("ps", bank)], writes=["KT"]))
                    def vproj(dst, dkey, chunks):
                        for g0 in range(0, len(chunks), 4):
                            grp = chunks[g0:g0 + 4]
                            bank = 2 + pst["n"] % 2
                            pst["n"] += 1
                            pp = self.PS[bank]
                            fns = []
                            for gi, (ci, tok0) in enumerate(grp):
                                for kk in range(8):
                                    fns.append(lambda e, gi=gi, tok0=tok0, kk=kk, pp=pp, sv_=sv_: e.matmul(
                                        pp[:, gi * 128:(gi + 1) * 128], Z[:, kk * ZT + tok0: kk * ZT + tok0 + 128],
                                        WS[sv_][:, kk * 128:(kk + 1) * 128], start=(kk == 0), stop=(kk == 7)))
                            P.group("pe", fns, reads=[("WS", sv_)] + zkeys, writes=[("ps", bank)])
                            c0 = grp[0][0]
                            nn = len(grp) * 128
                            P.op("act", lambda e, pp=pp, c0=c0, nn=nn, dst=dst: e.activation(out=dst[:, c0 * 128: c0 * 128 + nn], in_=pp[:, 0:nn], func=AF.Copy),
                                 reads=[("ps", bank)], writes=[dkey])
                    vproj(V, "V", [(ci, ci * 128) for ci in range(18)])
                    vproj(V2, "V2", [(ci, CTX + 64 + ci * 128) for ci in range(15)])
                    for hh in range(2):
                        hs = slice(hh * 64, (hh + 1) * 64)
                        for cc in range(2):
                            for qt in range(4):
                                bank = 2 + pst["n"] % 2
                                pst["n"] += 1
                                pp = self.PS[bank]
                                P.group("pe", [lambda e, pp=pp, hs=hs, cc=cc, qt=qt: e.matmul(
                                    pp[:, :], KT[hs, cc * 128:(cc + 1) * 128], QT[hs, qt * 512:(qt + 1) * 512], start=True, stop=True)],
                                    reads=["KT", "QT"], writes=[("ps", bank)])
                                P.op("act", lambda e, pp=pp, hh=hh, cc=cc, qt=qt: e.activation(
                                    out=PC[hh][:, cc * SEQ + qt * 512: cc * SEQ + (qt + 1) * 512], in_=pp[:, :], func=AF.Exp),
                                    reads=[("ps", bank)], writes=[("PC", hh)])
                    units = [(rg, hh, rr) for rg in range(4) for hh in range(2) for rr in range(8)]

                    def qk(ui):
                        rg, hh, rr = units[ui]
                        r = rg * 8 + rr
                        r0 = min(max(r - 4, 0), 24)
                        hs = slice(hh * 64, (hh + 1) * 64)
                        bank = ui % 4
                        pp = self.PS[bank]
                        o = 0
                        fns = []
                        for c in range(4):
                            k0 = CTX + (r0 + 2 * c) * 64
                            ro0 = r0 + 2 * c - r + 7
                            t0 = hp * 960 + ro0 * 64
                            fns.append(lambda e, pp=pp, hs=hs, c=c, k0=k0, r=r, o=o: e.matmul(
                                pp[:, o + c * 64: o + (c + 1) * 64], KT[hs, k0:k0 + 128], QT[hs, r * 64:(r + 1) * 64], start=True, stop=False))
                            fns.append(lambda e, pp=pp, hs=hs, c=c, t0=t0, o=o: e.matmul(
                                pp[:, o + c * 64: o + (c + 1) * 64], Tt[hs, t0:t0 + 128], ID2[hs, 0:64], start=False, stop=True))
                        P.group("pe", fns, reads=["KT", "QT", "Tt", "ID2"], writes=[("ps", bank)])
                        P.op("act", lambda e, pp=pp, bank=bank, o=o: e.activation(out=PL[bank][:, :], in_=pp[:, o:o + 256], func=AF.Exp),
                             reads=[("ps", bank)], writes=[("PL", bank)])

                    def pv(ui):
                        rg, hh, rr = units[ui]
                        r = rg * 8 + rr
                        r0 = min(max(r - 4, 0), 24)
                        bank = ui % 4
                        nb, db = 4 + (rg % 2), 6 + (rg % 2)
                        hs = slice(hh * 64, (hh + 1) * 64)
                        sl_ = (rg * 2 + hh) % 2
                        if rr == 0:
                            fns = []
                            for cc in range(2):
                                fns.append(lambda e, cc=cc, nb=nb, hs=hs, hh=hh, rg=rg: e.matmul(
                                    self.PS[nb][hs, :], V[:, cc * 128 + hh * 64: cc * 128 + (hh + 1) * 64],
                                    PC[hh][:, cc * SEQ + rg * 512: cc * SEQ + (rg + 1) * 512], start=(cc == 0), stop=False))
                            for cc in range(2):
                                fns.append(lambda e, cc=cc, db=db, hs=hs, hh=hh, rg=rg: e.matmul(
                                    self.PS[db][hs, :], ONB[:, 0:64],
                                    PC[hh][:, cc * SEQ + rg * 512: cc * SEQ + (rg + 1) * 512], start=(cc == 0), stop=False))
                            P.group("pe", fns, reads=["V", ("PC", hh), "ONB"], writes=[("ps", nb), ("ps", db)])
                        P.op("dve", lambda e, bank=bank, sl_=sl_, rr=rr: e.tensor_reduce(
                            out=PLS[sl_][:, rr * 64:(rr + 1) * 64], in_=PL[bank][:, :].rearrange("p (c q) -> p q c", c=4),
                            axis=mybir.AxisListType.X, op=ALU.add), reads=[("PL", bank)], writes=[("PLS", sl_)])
                        fns = []
                        for c in range(4):
                            if r0 % 2 == 0:
                                vsrc, ci = V, 2 + r0 // 2 + c
                            else:
                                vsrc, ci = V2, (r0 - 1) // 2 + c
                            lv = vsrc[:, ci * 128 + hh * 64: ci * 128 + (hh + 1) * 64]
                            rv = PL[bank][:, c * 64:(c + 1) * 64]
                            fns.append(lambda e, lv=lv, rv=rv, c=c, nb=nb, hs=hs, rr=rr: e.matmul(
                                self.PS[nb][hs, rr * 64:(rr + 1) * 64], lv, rv, start=False, stop=(c == 3 and rr == 7)))
                        P.group("pe", fns, reads=["V", "V2", ("PL", bank)], writes=[("ps", nb)])
                        if rr == 7:
                            P.group("pe", [lambda e, db=db, hs=hs, sl_=sl_: e.matmul(
                                self.PS[db][hs, :], self.ONES[:, 0:64], PLS[sl_][:, :], start=False, stop=True)],
                                reads=[("PLS", sl_), "ONES"], writes=[("ps", db)])
                        if hh == 1 and rr == 7:
                            rb = rg % 2
                            P.op("dve", lambda e, rb=rb, db=db: e.reciprocal(out=RD[rb][:, :], in_=self.PS[db][:, :]),
                                 reads=[("ps", db)], writes=[("RD", rb)])
                            P.op("dve", lambda e, rb=rb, nb=nb, rg=rg, hp=hp: e.tensor_tensor(
                                out=F[:, hp * SEQ + rg * 512: hp * SEQ + (rg + 1) * 512], in0=self.PS[nb][:, :], in1=RD[rb][:, :], op=ALU.mult),
                                reads=[("ps", nb), ("RD", rb)], writes=[("F", hp)])

                    for ui in range(3):
                        qk(ui)
                    for ui in range(len(units)):
                        if ui + 3 < len(units):
                            qk(ui + 3)
                        pv(ui)
                P.barrier()
            if self.dump_ctx:
                P.dma("sp", self.s_ioc, lambda e: e.dma_start(out=self.dbg[0:128, :], in_=F[:]), reads=[("F", i) for i in range(4)], writes=["dbg0"])
                P.barrier()
            halfproj(0)
            P.barrier()

            with ExitStack() as sl:
                XRT = self.sb(sl, "XRT", [128, ZT + 8], F32)
                XC = self.sb(sl, "XC", [128, ZT], F32)
                XCB = self.sb(sl, "XCB", [128, ZT], BF16)
                GG = self.sb(sl, "GG", [128, SEQ], BF16)
                A = self.sb(sl, "A", [128, ZT], F32)
                B = self.sb(sl, "B", [128, ZT], F32)
                TR = [self.sb(sl, "TR%d" % i, [128, 512], F32) for i in range(2)]
                T2 = self.sb(sl, "T2", [128, ZT], F32)
                BD = self.sb(sl, "BD", [128, 16 * 128], BF16)
                LCW = self.sb(sl, "LCW", [128, 16], F32)
                LCB = self.sb(sl, "LCB", [128, 4], F32)
                LBA = self.sb(sl, "LBA", [128, 8], F32)
                LBX = self.sb(sl, "LBX", [128, 8], F32)
                CL = self.sb(sl, "CL", [128, 8], F32)
                CLH = self.sb(sl, "CLH", [128, 8], F32)
                QRT = self.sb(sl, "QRT", [128, 1], F32)
                P.dma("pool", self.gs("constp"), lambda e: e.dma_start(out=BD[:, 0:1024].rearrange("p (g c) -> p g c", g=8),
                                                                  in_=self.lwa.rearrange("(g p) c -> p g c", p=128)), writes=["BD"])
                P.dma("pool", self.gs("constp"), lambda e: e.dma_start(out=BD[:, 1024:2048].rearrange("p (g c) -> p g c", g=8),
                                                                  in_=self.lwx.rearrange("(g p) c -> p g c", p=128)), writes=["BD"])
                for (t, src, nm) in ((LCW, self.lcw, "LCW"), (LCB, self.lcb, "LCB"), (LBA, self.lba, "LBA"), (LBX, self.lbx, "LBX"), (CL, self.llam, "CL")):
                    P.dma("sp", self.s_const, lambda e, t=t, src=src: e.dma_start(out=t[:], in_=src), writes=[nm])
                self.flush_conv()
                P.op("pool", lambda e: e.memset(QRT[:], 0.25), writes=["QRT"])
                P.op("act", lambda e: e.activation(out=CL[:], in_=CL[:], func=AF.Exp, scale=-1.0), reads=["CL"], writes=["CL"])
                P.op("dve", lambda e: e.tensor_scalar_add(out=CL[:], in0=CL[:], scalar1=1.0), reads=["CL"], writes=["CL"])
                P.op("act", lambda e: e.activation(out=CL[:], in_=CL[:], func=AF.Ln), reads=["CL"], writes=["CL"])
                P.op("dve", lambda e: e.tensor_scalar_mul(out=CLH[:], in0=CL[:], scalar1=-4.0), reads=["CL"], writes=["CLH"])
                P.op("dve", lambda e: e.tensor_scalar_mul(out=CL[:], in0=CL[:], scalar1=-8.0), reads=["CL", "CLH"], writes=["CL"])
                P.op("dve", lambda e: e.tensor_scalar_mul(out=LBA[:], in0=LBA[:], scalar1=0.5), reads=["LBA"], writes=["LBA"])
                P.op("dve", lambda e: e.tensor_scalar_mul(out=LBX[:], in0=LBX[:], scalar1=0.5), reads=["LBX"], writes=["LBX"])
                segs = ((0, 0, CTX), (CTX + 4, CTX, SEQ))
                tiles5 = [(0, CTX)] + [(CTX + tt * 512, 512) for tt in range(4)]
                gst = {"n": 0}
                GK = 0.7978845608028654
                for j in range(4):
                    sx, sg_ = loadw(12 + j), loadw(16 + j)
                    for (xb, cb_, n) in segs:
                        P.op("pool", lambda e, xb=xb: e.memset(XRT[:, xb:xb + 2], 0.0), writes=["XRT"])
                        P.op("pool", lambda e, xb=xb, n=n: e.memset(XRT[:, xb + 2 + n: xb + 4 + n], 0.0), writes=["XRT"])
                    proj(sx, 0, CTX, lambda pp, bank: P.op(
                        "dve", lambda e, pp=pp: e.tensor_copy(out=XRT[:, 2:2 + CTX], in_=pp[:, 0:CTX]), reads=[("ps", bank)], writes=["XRT"]))
                    for tt in range(4):
                        proj(sx, CTX + tt * 512, 512, lambda pp, bank, tt=tt: P.op(
                            "dve", lambda e, pp=pp, tt=tt: e.tensor_copy(out=XRT[:, CTX + 6 + tt * 512: CTX + 6 + (tt + 1) * 512], in_=pp[:, :]),
                            reads=[("ps", bank)], writes=["XRT"]))
                    for tt in range(4):
                        def gevac(pp, bank, tt=tt):
                            b = gst["n"] % 2
                            gst["n"] += 1
                            P.op("act", lambda e, pp=pp, b=b: e.activation(out=TR[b][:, :], in_=pp[:, :], func=AF.Square), reads=[("ps", bank)], writes=[("TR", b)])
                            P.op("dve", lambda e, b=b: e.tensor_scalar(out=TR[b][:, :], in0=TR[b][:, :], scalar1=0.044715, scalar2=1.0, op0=ALU.mult, op1=ALU.add),
                                 reads=[("TR", b)], writes=[("TR", b)])
                            P.op("dve", lambda e, pp=pp, b=b: e.tensor_tensor(out=TR[b][:, :], in0=TR[b][:, :], in1=pp[:, :], op=ALU.mult),
                                 reads=[("TR", b), ("ps", bank)], writes=[("TR", b)])
                            P.op("act", lambda e, b=b: e.activation(out=TR[b][:, :], in_=TR[b][:, :], func=AF.Tanh, scale=GK), reads=[("TR", b)], writes=[("TR", b)])
                            P.op("dve", lambda e, pp=pp, b=b, tt=tt: e.scalar_tensor_tensor(out=GG[:, tt * 512:(tt + 1) * 512], in0=TR[b][:, :], scalar=1.0, in1=pp[:, :],
                                                                                           op0=ALU.add, op1=ALU.mult),
                                 reads=[("TR", b), ("ps", bank)], writes=["GG"])
                        proj(sg_, CTX + tt * 512, 512, gevac)
                    for (xb, cb_, n) in segs:
                        P.op("act", lambda e, xb=xb, cb_=cb_, n=n, j=j: e.activation(
                            out=XC[:, cb_:cb_ + n], in_=XRT[:, xb:xb + n], func=AF.Identity, scale=LCW[:, j * 4:j * 4 + 1], bias=LCB[:, j:j + 1]),
                            reads=["XRT", "LCW", "LCB"], writes=["XC"])
                        for tap in (1, 2, 3):
                            P.op("dve", lambda e, xb=xb, cb_=cb_, n=n, j=j, tap=tap: e.scalar_tensor_tensor(
                                out=XC[:, cb_:cb_ + n], in0=XRT[:, xb + tap: xb + tap + n], scalar=LCW[:, j * 4 + tap: j * 4 + tap + 1],
                                in1=XC[:, cb_:cb_ + n], op0=ALU.mult, op1=ALU.add), reads=["XRT", "LCW", "XC"], writes=["XC"])
                    P.op("act", lambda e: e.activation(out=XCB[:, :], in_=XC[:, :], func=AF.Copy), reads=["XC"], writes=["XCB"])
                    for d in range(2):
                        col = d * 4 + j
                        for (c0, n) in tiles5:
                            b = gst["n"] % 2
                            gst["n"] += 1
                            ba, bx = 2 + b, 4 + b
                            for (bank, kind) in ((ba, 0), (bx, 1)):
                                P.group("pe", [lambda e, bank=bank, kind=kind, col=col, c0=c0, n=n: e.matmul(
                                    self.PS[bank][:, 0:n], BD[:, (kind * 8 + col) * 128:(kind * 8 + col + 1) * 128], XCB[:, c0:c0 + n], start=True, stop=True)],
                                    reads=["BD", "XCB"], writes=[("ps", bank)])
                            o0 = c0 if d == 0 else (c0 - CTX if c0 >= CTX else SEQ)
                            P.op("act", lambda e, ba=ba, b=b, n=n, col=col: e.activation(out=TR[b][:, 0:n], in_=self.PS[ba][:, 0:n], func=AF.Tanh, scale=0.5, bias=LBA[:, col:col + 1]),
                                 reads=[("ps", ba), "LBA"], writes=[("TR", b)])
                            P.op("act", lambda e, bx=bx, n=n, col=col, o0=o0: e.activation(out=T2[:, o0:o0 + n], in_=self.PS[bx][:, 0:n], func=AF.Tanh, scale=0.5, bias=LBX[:, col:col + 1]),
                                 reads=[("ps", bx), "LBX"], writes=["T2"])
                            P.op("act", lambda e, b=b, n=n, col=col, o0=o0: e.activation(out=A[:, o0:o0 + n], in_=TR[b][:, 0:n], func=AF.Exp, scale=CLH[:, col:col + 1], bias=CLH[:, col:col + 1]),
                                 reads=[("TR", b), "CLH"], writes=["A"])
                            P.op("act", lambda e, b=b, n=n, col=col, o0=o0: e.activation(out=B[:, o0:o0 + n], in_=TR[b][:, 0:n], func=AF.Exp, scale=CL[:, col:col + 1], bias=CL[:, col:col + 1]),
                                 reads=[("TR", b), "CL"], writes=["B"])
                        P.op("act", lambda e: e.activation(out=B[:, :], in_=B[:, :], func=AF.Sqrt, scale=-0.25, bias=QRT[:, 0:1]), reads=["B", "QRT"], writes=["B"])
                        P.op("dve", lambda e: e.scalar_tensor_tensor(out=B[:, :], in0=T2[:, :], scalar=1.0, in1=B[:, :], op0=ALU.add, op1=ALU.mult),
                             reads=["T2", "B"], writes=["B"])
                        if d == 0:
                            P.op("dve", lambda e: e.tensor_tensor(out=B[:, :], in0=B[:, :], in1=XC[:, :], op=ALU.mult), reads=["B", "XC"], writes=["B"])
                            P.op("dve", lambda e: e.tensor_tensor_scan(out=XRT[:, 0:ZT], data0=A[:, 0:ZT], data1=B[:, 0:ZT], initial=0.0,
                                                                       op0=ALU.mult, op1=ALU.add), reads=["A", "B"], writes=["XRT"])
                        else:
                            P.op("dve", lambda e: e.tensor_tensor(out=B[:, 0:SEQ], in0=B[:, 0:SEQ], in1=XC[:, CTX:ZT], op=ALU.mult), reads=["B", "XC"], writes=["B"])
                            P.op("dve", lambda e: e.tensor_tensor(out=B[:, SEQ:ZT], in0=B[:, SEQ:ZT], in1=XC[:, 0:CTX], op=ALU.mult), reads=["B", "XC"], writes=["B"])
                            P.op("dve", lambda e: e.tensor_tensor_scan(out=XC[:, 0:ZT][:, ::-1], data0=A[:, 0:ZT][:, ::-1], data1=B[:, 0:ZT][:, ::-1],
                                                                       initial=0.0, op0=ALU.mult, op1=ALU.add), reads=["A", "B"], writes=["XC"])
                    P.op("dve", lambda e: e.tensor_tensor(out=A[:, 0:SEQ], in0=XRT[:, CTX:ZT], in1=XC[:, 0:SEQ], op=ALU.add),
                         reads=["XRT", "XC", "A"], writes=["A"], strict=True)
                    P.op("dve", lambda e, j=j: e.scalar_tensor_tensor(out=F[:, j * SEQ:(j + 1) * SEQ], in0=A[:, 0:SEQ], scalar=0.5, in1=GG[:, :],
                                                                      op0=ALU.mult, op1=ALU.mult),
                         reads=["A", "GG"], writes=[("F", j)])
                P.barrier()
            if self.dump_ctx:
                P.dma("sp", self.s_ioc, lambda e: e.dma_start(out=self.dbg[128:256, :], in_=F[:]), reads=[("F", i) for i in range(4)], writes=["dbg1"])
                P.barrier()
            with ExitStack() as sln:
                tl = self.ln_tiles_alloc(sln, 4)
                halfproj(1, tl)
                for tt in range(4):
                    self.ln_tile(tl, "h", l, k, w, tt * 512, 512, ti=tt, pre=True)
                P.barrier()


def prep_shared(inp):
    f = lambda a: np.ascontiguousarray(np.asarray(a, dtype=np.float32))
    sh = {}
    sh["modw"] = np.concatenate([lhsT_layout(f(inp["mod_w"][l])) for l in range(2)], axis=0)
    mb = np.concatenate([vec_pm(f(inp["mod_b"][l])) for l in range(2)], axis=1)
    sh["modb3"] = np.ascontiguousarray(np.repeat(mb, 3, axis=1))
    sh["lng"] = np.concatenate([vec_pm(f(inp["ln_g"][l, k])) for l in range(2) for k in range(3)], axis=1)
    sh["lnb"] = np.concatenate([vec_pm(f(inp["ln_b"][l, k])) for l in range(2) for k in range(3)], axis=1)
    sh["w1"] = np.concatenate([lhsT_layout(f(inp["ffn_w1"][l, j])) for l in range(2) for j in range(2)], axis=0)
    sh["w3"] = np.concatenate([lhsT_layout(f(inp["ffn_w3"][l, j])) for l in range(2) for j in range(2)], axis=0)
    sh["w2"] = np.concatenate([lhsT_layout(f(inp["ffn_w2"][l, j])) for l in range(2) for j in range(2)], axis=0)
    sh["win0"] = lhsT_layout(f(inp["mix0_w_in"][0]))
    sh["wout0"] = lhsT_layout(f(inp["mix0_w_out"][0]))
    sh["win1"] = lhsT_layout(f(inp["mix1_w_in"][0]))
    sh["wout1"] = lhsT_layout(f(inp["mix1_w_out"][0]))
    scw = f(inp["sconv_w"][0])
    sh["scw"] = np.ascontiguousarray(np.stack([vec_pm(scw[t]) for t in range(3)], axis=2).reshape(128, 24))
    rpb = f(inp["na_rpb"][0])
    j = np.arange(64)[:, None]
    kc = np.arange(64)[None, :]
    ci = np.clip(kc - j + 15, 0, 30)
    g = rpb[:, :, ci]
    g = g.transpose(0, 2, 1, 3)
    rp = np.zeros((128, 4 * 15 * 64), np.float32)
    for h in range(8):
        half, idx = h % 2, h // 2
        rp[half * 64:(half + 1) * 64, idx * 960:(idx + 1) * 960] = g[h].reshape(64, 960)
    sh["rpbg"] = rp
    start = np.clip(j - 8, 0, 48)
    valid = (kc >= start) & (kc < start + 16)
    nm = np.where(valid, 0.0, -30000.0).astype(np.float32)
    sh["nmask"] = np.ascontiguousarray(np.tile(np.concatenate([nm, nm], axis=0), (1, 60)))
    eye = np.eye(64, dtype=np.float32)
    sh["id2"] = np.ascontiguousarray(np.concatenate([eye, eye], axis=0))
    cw = f(inp["lru_conv_w"][0])
    sh["lcw"] = np.ascontiguousarray(np.stack([vec_pm(cw[t]) for t in range(4)], axis=2).reshape(128, 16))
    sh["lcb"] = vec_pm(f(inp["lru_conv_b"][0]))
    def bd(wm):
        o = np.zeros((2, 4, 128, 128), np.float32)
        for d in range(2):
            for n in range(8):
                c, hh = n // 2, n % 2
                o[d, c, hh * 64:(hh + 1) * 64, hh * 64:(hh + 1) * 64] = wm[d, n]
        return o.reshape(2 * 4 * 128, 128)
    sh["lwa"] = bd(f(inp["lru_w_a"][0]))
    sh["lwx"] = bd(f(inp["lru_w_x"][0]))
    sh["lba"] = np.concatenate([vec_pm(f(inp["lru_b_a"][0, d])) for d in range(2)], axis=1)
    sh["lbx"] = np.concatenate([vec_pm(f(inp["lru_b_x"][0, d])) for d in range(2)], axis=1)
    sh["llam"] = np.concatenate([vec_pm(f(inp["lru_lambda"][0, d])) for d in range(2)], axis=1)
    return sh


def prep_core(inp, bidx):
    f = lambda a: np.asarray(a, dtype=np.float32)
    m = {}
    m["xT"] = np.concatenate([to_pm(f(inp["x"][b])) for b in bidx], axis=0)
    m["cxT"] = np.concatenate([to_pm(f(inp["ctx"][b])) for b in bidx], axis=0)
    cols = [f(inp["c"][b]) for b in bidx]
    while len(cols) < 2:
        cols.append(cols[0])
    cols.append(f(inp["c_ctx"]))
    cm = np.stack([vec_pm(c) for c in cols], axis=2)
    m["cond"] = np.ascontiguousarray(cm.reshape(128, 24))
    return m


_CACHE = {}


def get_program(NS, phases=ALL_PHASES, dump_ctx=False):
    key = (NS, tuple(phases), dump_ctx)
    if key not in _CACHE:
        _CACHE[key] = Builder(NS, phases, dump_ctx).build()
    return _CACHE[key]


def kernel(**inputs):
    B = inputs["x"].shape[0]
    NS = B // NCORES
    nc = get_program(NS)
    sh = prep_shared(inputs)
    in_maps = []
    for c in range(NCORES):
        m = dict(sh)
        m.update(prep_core(inputs, list(range(c * NS, (c + 1) * NS))))
        in_maps.append(m)
    res = run_bass_kernel_spmd(nc, in_maps, core_ids=list(range(NCORES)))
    out = np.empty((B, SEQ, D), np.float32)
    for c in range(NCORES):
        o = res.results[c]["out"]
        for i in range(NS):
            out[c * NS + i] = from_pm(o[i * 128:(i + 1) * 128], SEQ)
    return out
```
